# Optimizing a Trainium2 kernel written in Bass

```python
import jax, jax.numpy as jnp
from jax import lax
import numpy as np

D_MODEL = 1024
BATCH = 8
SEQ = 2048
DEPTH = 1
DEC_BATCH = 128
DEC_SEQ = 8
PAST_LEN = 16384
PAGE_SIZE = 128

EPS = 1e-6
POOL_WINDOWS = (2, 4, 8, 16)
N_POOL_GROUPS = len(POOL_WINDOWS)
D_POOL = D_MODEL // 2
POOL_GROUP = D_POOL // N_POOL_GROUPS
POOL_BUF = max(POOL_WINDOWS) - 1
SSM_EXPAND = 2
D_INNER = SSM_EXPAND * D_MODEL
HEAD_DIM = 64
N_HEADS = D_INNER // HEAD_DIM
N_BC_GROUPS = 4
HEADS_PER_GROUP = N_HEADS // N_BC_GROUPS
D_STATE = 128
CONV_W = 4
D_XBC = D_INNER + 2 * N_BC_GROUPS * D_STATE
CHUNK = 128
N_BRANCH = 2
SPLITS = (D_POOL, 2 * D_POOL, 2 * D_POOL + D_INNER, 2 * D_POOL + D_INNER + D_XBC,
          2 * D_POOL + D_INNER + D_XBC + N_HEADS)
IN_COLS = 2 * D_POOL + D_INNER + D_XBC + N_HEADS + N_BRANCH * D_MODEL

kernel_name = "pool_ssd_gated_hybrid_step"


def rmsnorm(x, g):
    xf = x.astype(jnp.float32)
    y = xf * lax.rsqrt(jnp.mean(xf * xf, axis=-1, keepdims=True) + EPS)
    return (y * g.astype(jnp.float32)).astype(x.dtype)


def pool_mix(u, buf, pos0, pool_w, pool_scale):
    b, L, _ = u.shape
    xp = jnp.concatenate([buf.astype(u.dtype), u], axis=1)
    cs = jnp.pad(jnp.cumsum(xp.astype(jnp.float32), axis=1), ((0, 0), (1, 0), (0, 0)))
    pos = pos0 + jnp.arange(L)
    means = []
    for gi, w in enumerate(POOL_WINDOWS):
        sl = slice(gi * POOL_GROUP, (gi + 1) * POOL_GROUP)
        s = cs[:, POOL_BUF + 1:POOL_BUF + 1 + L, sl] - cs[:, POOL_BUF + 1 - w:POOL_BUF + 1 - w + L, sl]
        cnt = jnp.minimum(pos + 1, w).astype(jnp.float32)
        means.append(s / cnt[None, :, None])
    d = jnp.concatenate(means, axis=-1) - u.astype(jnp.float32)
    d = d.reshape(b, L, N_POOL_GROUPS, POOL_GROUP)
    y = jnp.einsum('blgc,gcd->blgd', d, pool_w.astype(jnp.float32)).reshape(b, L, D_POOL)
    y = y * pool_scale.astype(jnp.float32)
    return y.astype(u.dtype), xp[:, -POOL_BUF:]


def causal_conv(xbc, buf, w, bias):
    L = xbc.shape[1]
    xp = jnp.concatenate([buf.astype(xbc.dtype), xbc], axis=1)
    y = bias[None, None, :] + sum(xp[:, k:k + L] * w[k][None, None, :] for k in range(CONV_W))
    return y, xp[:, -(CONV_W - 1):]


def ssd(x, dt, A, B, C, state0):
    b, L = x.shape[:2]
    Q = CHUNK if L % CHUNK == 0 else L
    nc = L // Q

    def to_chunks(t):
        return jnp.moveaxis(t.reshape((b, nc, Q) + t.shape[2:]), 1, 0)

    xc = to_chunks(x.reshape(b, L, N_BC_GROUPS, HEADS_PER_GROUP, HEAD_DIM))
    dtc = to_chunks(dt.reshape(b, L, N_BC_GROUPS, HEADS_PER_GROUP))
    Bc, Cc = to_chunks(B), to_chunks(C)
    Ag = A.reshape(N_BC_GROUPS, HEADS_PER_GROUP)
    mask = jnp.tril(jnp.ones((Q, Q), dtype=bool))[None, :, :, None, None]

    def step(h, inp):
        xq, dq, Bq, Cq = inp
        acs = jnp.cumsum(dq * Ag, axis=1)
        seg = acs[:, :, None] - acs[:, None, :]
        decay = jnp.exp(jnp.where(mask, seg, -jnp.inf))
        cb = jnp.einsum('bign,bjgn->bijg', Cq, Bq)
        att = cb[..., None] * decay * dq[:, None]
        y = jnp.einsum('bijge,bjgep->bigep', att, xq)
        y = y + jnp.einsum('bign,bige,bgepn->bigep', Cq, jnp.exp(acs), h)
        last = acs[:, -1]
        w_in = jnp.exp(last[:, None] - acs) * dq
        h = h * jnp.exp(last)[..., None, None] + jnp.einsum('bjgn,bjge,bjgep->bgepn', Bq, w_in, xq)
        return h, y

    h0 = state0.astype(jnp.float32).reshape(b, N_BC_GROUPS, HEADS_PER_GROUP, HEAD_DIM, D_STATE)
    hT, ys = lax.scan(step, h0, (xc, dtc, Bc, Cc))
    y = jnp.moveaxis(ys, 0, 1).reshape(b, L, N_HEADS, HEAD_DIM)
    return y, hT.reshape(b, N_HEADS, HEAD_DIM, D_STATE)


def layer(x, c, pos0, pool_buf, conv_buf, ssm_state, w_ada, b_ada, norm_g, w_in, conv_w, conv_b,
          dt_bias, a_log, d_skip, ssm_norm_g, pool_w, pool_scale, w_pool_out, w_ssm_out, w_o):
    b, L, _ = x.shape
    mod = jax.nn.silu(c) @ w_ada + b_ada
    shift, scale, gate = jnp.split(mod, 3, axis=-1)
    h = rmsnorm(x, norm_g) * (1 + scale[:, None]) + shift[:, None]
    proj = h @ w_in
    u_pool, z_pool, z_ssm, xbc, dt_raw, g_raw = jnp.split(proj, SPLITS, axis=-1)
    p, pool_new = pool_mix(u_pool, pool_buf, pos0, pool_w, pool_scale)
    p = (p * jax.nn.silu(z_pool)) @ w_pool_out
    xbc_c, conv_new = causal_conv(xbc, conv_buf, conv_w, conv_b)
    xbc_c = jax.nn.silu(xbc_c).astype(jnp.float32)
    xs = xbc_c[..., :D_INNER].reshape(b, L, N_HEADS, HEAD_DIM)
    Bm = xbc_c[..., D_INNER:D_INNER + N_BC_GROUPS * D_STATE].reshape(b, L, N_BC_GROUPS, D_STATE)
    Cm = xbc_c[..., D_INNER + N_BC_GROUPS * D_STATE:].reshape(b, L, N_BC_GROUPS, D_STATE)
    dt = jax.nn.softplus(dt_raw.astype(jnp.float32) + dt_bias.astype(jnp.float32))
    A = -jnp.exp(a_log.astype(jnp.float32))
    y, ssm_new = ssd(xs, dt, A, Bm, Cm, ssm_state)
    y = (y + d_skip.astype(jnp.float32)[:, None] * xs).reshape(b, L, D_INNER)
    y = rmsnorm(y * jax.nn.silu(z_ssm.astype(jnp.float32)), ssm_norm_g)
    s = y.astype(x.dtype) @ w_ssm_out
    gates = jax.nn.sigmoid(g_raw)
    m = gates[..., :D_MODEL] * p + gates[..., D_MODEL:] * s
    x = x + gate[:, None] * (m @ w_o)
    return x, pool_new, conv_new, ssm_new


def setup_inputs(seed: int = 0) -> dict:
    key = jax.random.key(seed)
    ks = jax.random.split(key, 24)
    nrm = lambda k, shape, s: jax.random.normal(k, shape, jnp.float32) * s
    dt0 = jnp.exp(jax.random.uniform(ks[13], (DEPTH, N_HEADS), jnp.float32, np.log(1e-3), np.log(1e-1)))
    return {
        "x_prompt": nrm(ks[0], (BATCH, SEQ, D_MODEL), 1.0),
        "x_sample": nrm(ks[1], (DEC_BATCH, DEC_SEQ, D_MODEL), 1.0),
        "state_pool": nrm(ks[2], (DEPTH, DEC_BATCH, POOL_BUF, D_POOL), 1.0),
        "state_conv": nrm(ks[3], (DEPTH, DEC_BATCH, CONV_W - 1, D_XBC), 1.0),
        "state_ssm": nrm(ks[4], (DEPTH, DEC_BATCH, N_HEADS, HEAD_DIM, D_STATE), 0.5),
        "c_prompt": nrm(ks[5], (BATCH, D_MODEL), 1.0),
        "c_sample": nrm(ks[6], (DEC_BATCH, D_MODEL), 1.0),
        "w_ada": nrm(ks[7], (DEPTH, D_MODEL, 3 * D_MODEL), 0.5 * D_MODEL ** -0.5),
        "b_ada": nrm(ks[8], (DEPTH, 3 * D_MODEL), 0.02),
        "norm_g": 1.0 + nrm(ks[9], (DEPTH, D_MODEL), 0.05),
        "w_in": nrm(ks[10], (DEPTH, D_MODEL, IN_COLS), D_MODEL ** -0.5),
        "conv_w": nrm(ks[11], (DEPTH, CONV_W, D_XBC), CONV_W ** -0.5),
        "conv_b": nrm(ks[12], (DEPTH, D_XBC), 0.02),
        "dt_bias": dt0 + jnp.log(-jnp.expm1(-dt0)),
        "a_log": jnp.log(jax.random.uniform(ks[14], (DEPTH, N_HEADS), jnp.float32, 1.0, 16.0)),
        "d_skip": 1.0 + nrm(ks[15], (DEPTH, N_HEADS), 0.1),
        "ssm_norm_g": 1.0 + nrm(ks[16], (DEPTH, D_INNER), 0.05),
        "pool_w": nrm(ks[17], (DEPTH, N_POOL_GROUPS, POOL_GROUP, POOL_GROUP), POOL_GROUP ** -0.5),
        "pool_scale": 1.0 + nrm(ks[18], (DEPTH, D_POOL), 0.1),
        "w_pool_out": nrm(ks[19], (DEPTH, D_POOL, D_MODEL), D_POOL ** -0.5),
        "w_ssm_out": nrm(ks[20], (DEPTH, D_INNER, D_MODEL), D_INNER ** -0.5),
        "w_o": nrm(ks[21], (DEPTH, D_MODEL, D_MODEL), D_MODEL ** -0.5),
        "final_g": 1.0 + nrm(ks[22], (D_MODEL,), 0.05),
    }


def reference(x_prompt, x_sample, state_pool, state_conv, state_ssm, c_prompt, c_sample,
              w_ada, b_ada, norm_g, w_in, conv_w, conv_b, dt_bias, a_log, d_skip, ssm_norm_g,
              pool_w, pool_scale, w_pool_out, w_ssm_out, w_o, final_g):
    bp = x_prompt.shape[0]
    xp, xs = x_prompt, x_sample
    pp, cp, sp, ps, cs, ss = [], [], [], [], [], []
    for l in range(DEPTH):
        wl = (w_ada[l], b_ada[l], norm_g[l], w_in[l], conv_w[l], conv_b[l], dt_bias[l], a_log[l],
              d_skip[l], ssm_norm_g[l], pool_w[l], pool_scale[l], w_pool_out[l], w_ssm_out[l], w_o[l])
        xp, p_new, c_new, s_new = layer(
            xp, c_prompt, 0,
            jnp.zeros((bp, POOL_BUF, D_POOL), xp.dtype),
            jnp.zeros((bp, CONV_W - 1, D_XBC), xp.dtype),
            jnp.zeros((bp, N_HEADS, HEAD_DIM, D_STATE), jnp.float32), *wl)
        pp.append(p_new); cp.append(c_new); sp.append(s_new)
        xs, p_new, c_new, s_new = layer(xs, c_sample, PAST_LEN, state_pool[l], state_conv[l], state_ssm[l], *wl)
        ps.append(p_new); cs.append(c_new); ss.append(s_new)
    y_prompt = rmsnorm(xp, final_g)
    y_sample = rmsnorm(xs, final_g)
    return (y_prompt, y_sample, jnp.stack(pp), jnp.stack(cp), jnp.stack(sp),
            jnp.stack(ps), jnp.stack(cs), jnp.stack(ss))
```

```python
import numpy as np
import concourse.bass as bass
import concourse.mybir as mybir
from concourse.bass_utils import run_bass_kernel_spmd
from contextlib import ExitStack

F32 = mybir.dt.float32
BF16 = mybir.dt.bfloat16
AF = mybir.ActivationFunctionType
ALU = mybir.AluOpType

NCORES = 8
D = 1024
SEQ = 2048
NTOK = 2176
IN_U, IN_Z, IN_ZS, IN_XBC, IN_DT, IN_G = 0, 512, 1024, 3072, 6144, 6176
EPS = 1e-6
VF_BSHIFT, VF_BSCALE, VF_NORMG, VF_CONVW, VF_CONVB, VF_PSCALE, VF_SNG, NVF = 0, 8, 16, 24, 120, 144, 148, 164
VR_FING, VR_DTB, VR_ALOG, VR_DSKIP, NVR = 0, 1024, 1056, 1088, 1120
C_ID, C_SEQ, C_INVC, NCST = 0, 128, 144, 208

import os
DEBUG = bool(int(os.environ.get('K_DEBUG', '0')))
K_STOP = os.environ.get('K_STOP', '')


class _Stop(Exception):
    pass


class Buf:
    __slots__ = ("name", "lw", "rd", "excl")

    def __init__(self, name, excl=False):
        self.name = name
        self.lw = None
        self.rd = {}
        self.excl = excl


def bufs(name, *dims):
    if len(dims) == 1:
        return [Buf("%s%d" % (name, i)) for i in range(dims[0])]
    return [bufs("%s%d_" % (name, i), *dims[1:]) for i in range(dims[0])]


class Sched:
    ENG = ("pe", "act", "dve", "pool", "sp")

    def __init__(self, sems, dma_sems):
        self.sem = sems
        self.dma_sems = dma_sems
        self.dma_tgt = [0] * len(dma_sems)
        self.dma_rr = 0
        self.cnt = {e: 0 for e in self.ENG}
        self.ops = {e: [] for e in self.ENG}
        self.seen = {e: {} for e in self.ENG}
        self.final_tokens = []

    def _handle(self, key):
        return self.sem[key] if isinstance(key, str) else self.dma_sems[key]

    def _deps(self, eng, reads, writes):
        need = {}

        def add(tok):
            if tok is None:
                return
            k, v = tok
            if k == eng and eng == "pe":
                return
            if need.get(k, 0) < v:
                need[k] = v
        for b in reads:
            add(b.lw)
        for b in writes:
            add(b.lw)
            for k, v in b.rd.items():
                add((k, v))
        out = []
        for k, v in need.items():
            if self.seen[eng].get(k, 0) >= v:
                continue
            self.seen[eng][k] = v
            out.append((k, v))
        return out

    def op(self, eng, fn, reads=(), writes=()):
        ex = [b for b in reads if b.excl]
        if ex:
            reads = [b for b in reads if not b.excl]
            writes = list(writes) + ex
        waits = self._deps(eng, reads, writes)
        self.cnt[eng] += 1
        c = self.cnt[eng]
        self.ops[eng].append((waits, fn, (eng, 1)))
        for b in reads:
            if b.rd.get(eng, 0) < c:
                b.rd[eng] = c
        for b in writes:
            b.lw = (eng, c)
            b.rd = {}

    def dma(self, q, fn, reads=(), writes=(), final=False):
        waits = self._deps(q, reads, writes)
        s = self.dma_rr
        self.dma_rr = (self.dma_rr + 1) % len(self.dma_sems)
        prev = self.dma_tgt[s]
        if prev > 0 and self.seen[q].get(s, 0) < prev:
            self.seen[q][s] = prev
            waits.append((s, prev))
        tgt = prev + 16
        self.dma_tgt[s] = tgt
        self.ops[q].append((waits, fn, (s, 16)))
        for b in reads:
            if b.rd.get(s, 0) < tgt:
                b.rd[s] = tgt
        for b in writes:
            b.lw = (s, tgt)
            b.rd = {}
        if final:
            self.final_tokens.append((s, tgt))

    def emit(self, eng, e):
        for waits, fn, inc in self.ops[eng]:
            for k, v in waits:
                e.wait_ge(self._handle(k), v)
            fn(e).then_inc(self._handle(inc[0]), inc[1])
        if eng == "sp":
            done = {}
            for k, v in self.final_tokens:
                done[k] = max(done.get(k, 0), v)
            for k, v in done.items():
                e.wait_ge(self._handle(k), v)


def C(name, *args, **kw):
    def f(e):
        return getattr(e, name)(*args, **kw)
    return f


def bc(ap, shape):
    return ap.to_broadcast(list(shape))


def build_nc():
    nc = bass.Bass("TRN2", target_bir_lowering=False)

    def din(name, shape):
        return nc.dram_tensor(name, list(shape), F32, kind="ExternalInput").ap()

    def dout(name, shape):
        return nc.dram_tensor(name, list(shape), F32, kind="ExternalOutput").ap()

    x_d = din("x", [NTOK, D])
    cT_d = din("cT", [D, 17])
    spT_d = din("spT", [128, 4 * 16 * 15])
    scT_d = din("scT", [128, 24 * 16 * 3])
    ssT_d = din("ssT", [16, 128, 2048])
    wada_d = din("w_ada", [D, 3072])
    win_d = din("w_in", [D, 8224])
    wpo_d = din("w_pool_out", [512, D])
    wssm_d = din("w_ssm_out", [2048, D])
    wo_d = din("w_o", [D, D])
    poolw_d = din("pool_w", [512, 128])
    vfm_d = din("vecs_fm", [128, NVF])
    vrow_d = din("vecs_row", [1, NVR])
    bgate_d = din("b_gate", [1, 1024])
    cst_d = din("consts", [128, NCST])
    msk_d = din("masks", [128, 8 * 128])

    y_d = dout("y", [NTOK, D])
    npp_d = dout("npp", [128, 60])
    ncp_d = dout("ncp", [128, 72])
    nsp_d = dout("nsp", [128, 2048])
    nps_d = dout("nps", [128, 960])
    ncs_d = dout("ncs", [128, 1152])
    nss_d = dout("nss", [16, 128, 2048])
    dbg = {}
    if DEBUG:
        for nm, n in (("d_mT", 4096), ("d_ynT", 8192), ("d_pzT", 2048), ("d_th", 8192), ("d_yb", 2048), ("d_xc", 12288)):
            dbg[nm] = dout(nm, [128, n])

    with ExitStack() as es:
        def sb(name, shape, dt=F32):
            return es.enter_context(nc.sbuf_tensor(name, list(shape), dt))

        sems = {e: es.enter_context(nc.semaphore("s_" + e)) for e in Sched.ENG}
        dsem = [es.enter_context(nc.semaphore("d%d" % i)) for i in range(24)]
        S = Sched(sems, dsem)

        NSLOT = 4
        wslot = [sb("wslot%d" % i, [128, 4096], BF16) for i in range(NSLOT)]
        B_wslot = bufs("wslot", NSLOT)
        cst = sb("cst", [128, NCST]); B_cst = Buf("cst")
        cbf = sb("cbf", [128, 8 * 128], BF16); B_cbf = Buf("cbf")
        vfm = sb("vfm", [128, NVF]); B_vfm = Buf("vfm")
        vrow = sb("vrow", [128, NVR]); B_vrow = Buf("vrow")
        misc = sb("misc", [128, 256]); B_misc = Buf("misc")
        cTs = sb("cTs", [128, 8, 17]); B_cTs = Buf("cTs")
        siluc = sb("siluc", [128, 8, 17], BF16); B_siluc = Buf("siluc")
        Gt = sb("Gt", [128, 8, 17]); B_Gt = Buf("Gt")
        St = sb("St", [128, 8, 17]); B_St = Buf("St")
        gate_p = sb("gate_p", [128, 1024]); B_gate_p = Buf("gate_p")
        gate_s = sb("gate_s", [128, 1024]); B_gate_s = Buf("gate_s")
        poolw = sb("poolw", [128, 4, 128], BF16); B_poolw = Buf("poolw")
        xb = [sb("xb%d" % i, [128, 1024]) for i in range(2)]; B_xb = bufs("xb", 2)
        tmpA = sb("tmpA", [128, 1024]); B_tmpA = Buf("tmpA")
        stat = sb("stat", [128, 16]); B_stat = Buf("stat"); B_statA = bufs("statA", 2); B_statF = bufs("statF", 2)
        cm05 = sb("cm05", [128, 1]); B_cm05 = Buf("cm05")
        hT = sb("hT", [128, 8, 512], BF16); B_hT = bufs("hT", 4, 8)
        ynT = sb("ynT", [128, 16, 512], BF16); B_ynT = bufs("ynT", 4, 2)
        uX = sb("uX", [128, 4, 527]); B_uX = bufs("uX", 4)
        SaSb = sb("SaSb", [128, 1056]); B_Sa = Buf("Sa"); B_Sb = Buf("Sb")
        Sa = SaSb[:, 0:528]
        Sb_ = SaSb[:, 528:1056]
        xw = SaSb[:].bitcast(BF16)[:, 0:2048]; B_xw = [B_Sa, B_Sb]
        siluz = sb("siluz", [128, 4, 512], BF16); B_siluz = bufs("siluz", 4)
        dTt = sb("dT", [128, 4, 512], BF16); B_dT = bufs("dT", 4)
        pzT = siluz; B_pzT = B_siluz
        raw = [sb("raw%d" % i, [128, 516], BF16) for i in range(2)]; B_raw = bufs("raw", 2)
        chist = sb("chist", [128, 24, 3], BF16); B_chist = bufs("chist", 24)
        dg = [sb("dg%d" % i, [128, 4, 128], BF16) for i in range(2)]; B_dg = bufs("dg", 2)
        xcT = sb("xcT", [128, 24, 512], BF16); B_xcT = bufs("xcT", 24)
        sz = sb("sz", [128, 4, 2048], BF16); B_sz = bufs("sz", 4)
        vdt = sb("vdt", [128, 4, 32]); B_vdt = bufs("vdt", 4)
        dtt = sb("dtt", [128, 32]); B_dtt = Buf("dtt")
        aa = sb("aa", [128, 32]); B_aa = Buf("aa")
        aab = sb("aab", [128, 32], BF16); B_aab = Buf("aab")
        ex = sb("ex", [128, 96]); B_ex = Buf("ex")
        xdt = sb("xdt", [128, 2048], BF16); B_xdt = Buf("xdt")
        xD = sb("xD", [128, 2048], BF16); B_xD = Buf("xD")
        Btm = sb("Btm", [128, 512], BF16); B_Btm = Buf("Btm")
        Rg = [sb("Rg%d" % i, [128, 1024], BF16) for i in range(2)]; B_Rg = bufs("Rg", 2)
        dec = sb("dec", [128, 2176], BF16); B_dec = bufs("dec", 2)
        att = sb("att", [128, 2048], BF16); B_att = bufs("att", 2)
        cbTm = sb("cbTm", [128, 4, 128], BF16); B_cbTm = Buf("cbTm")
        ynb = att; B_ynb = B_att
        yb = sb("yb", [128, 2048]); B_yb = bufs("yb", 4)
        hst = sb("hst", [128, 2048]); B_hst = bufs("hst", 4)
        hstb = sb("hstb", [128, 2048], BF16); B_hstb = bufs("hstb", 4)
        tmpg = [sb("tmpg%d" % i, [128, 512]) for i in range(2)]; B_tmpg = bufs("tmpg", 2)
        hsf = [hstb[:].bitcast(F32)[:, i * 512:(i + 1) * 512] for i in range(2)]; B_hsf = [[B_hstb[0], B_hstb[1]], [B_hstb[2], B_hstb[3]]]
        hsb = [sb("hsb%d" % i, [128, 512], BF16) for i in range(2)]; B_hsb = bufs("hsb", 2)
        am = sb("am", [128, 16, 32], BF16); B_am = Buf("am")
        exl = sb("exl", [128, 512]); B_exl = Buf("exl")
        ncf = hst[:, 0:1152].rearrange("p (b s r) -> p b s r", b=24, s=16); B_ncf = [B_hst[(b * 48) // 512] for b in range(24)]
        ncpt = sb("ncpt", [128, 24, 3]); B_ncpt = Buf("ncpt")
        outb = [yb[:, 0:1024], yb[:, 1024:2048]]; B_outb = [[B_yb[0], B_yb[1]], [B_yb[2], B_yb[3]]]

        ps = es.enter_context(nc.psum_tensor("ps", [128, 4096], F32))
        B_ps = [Buf("ps%d" % i, excl=True) for i in range(8)]
        psrr = [0]

        def ps_one():
            b = psrr[0]
            psrr[0] = (b + 1) % 8
            return b

        def ps_pair():
            if psrr[0] % 2:
                psrr[0] = (psrr[0] + 1) % 8
            b = psrr[0]
            psrr[0] = (b + 2) % 8
            return b

        def psf(b, n=512, off=0):
            return ps[:, b * 512 + off: b * 512 + off + n]

        def psb(b, n, off=0):
            nb = (off + n + 1023) // 1024
            v = ps[:, b * 512: (b + nb) * 512].bitcast(BF16)
            return v[:, off: off + n]

        ident = cst[:, C_ID:C_ID + 128]
        identb = cbf[:, 0:128]
        Ub, Lmb, oneb, Usb, Lmsb, onesb, zerob = (cbf[:, i * 128:(i + 1) * 128] for i in range(1, 8))
        seqm = cst[:, C_SEQ:C_SEQ + 16]
        A_b = misc[:, 0:32]
        D_b = misc[:, 32:64]
        hbg = tmpA

        wq = []
        wstate = {"issued": 0, "used": 0}

        def w_enqueue(src, kk, n):
            wq.append((src, kk, n))

        wfree = list(range(NSLOT))
        wslot_of = {}

        def w_prefetch():
            while wfree and wstate["issued"] < len(wq):
                i = wstate["issued"]
                src, kk, n = wq[i]
                sl = wfree.pop(0)
                wslot_of[i] = sl
                dst = wslot[sl][:, 0:kk * n].rearrange("p (k n) -> p k n", k=kk)
                S.dma("pool", C("dma_start", out=dst, in_=src), writes=[B_wslot[sl]])
                wstate["issued"] += 1

        w_cur = [None]
        w_prev = [None]

        def w_next(kk, n, hold=False, lag=True):
            if w_prev[0] is not None:
                w_release(w_prev[0])
            w_prev[0] = w_cur[0]
            w_cur[0] = None
            if not lag and w_prev[0] is not None:
                w_release(w_prev[0])
                w_prev[0] = None
            j = wstate["used"]
            wstate["used"] += 1
            w_prefetch()
            assert j in wslot_of, ("weight stream stalled: no free slot", j)
            sl = wslot_of[j]
            assert wq[j][1] == kk and wq[j][2] == n, (j, wq[j][1:], kk, n)
            if not hold:
                w_cur[0] = j
            return wslot[sl][:, 0:kk * n].rearrange("p (k n) -> p k n", k=kk), B_wslot[sl], j

        def w_release(j):
            wfree.append(wslot_of[j])
            w_prefetch()

        def src_rows(d_ap, c0, n):
            return d_ap[:, c0:c0 + n].rearrange("(k p) n -> p k n", p=128)

        for gi in range(6):
            w_enqueue(src_rows(wada_d, gi * 512, 512), 8, 512)
        ST_LIST = [(0, 4, False), (512, 4, False), (1024, 4, False), (1536, 4, False), (2048, 1, True)]
        for _ in ST_LIST:
            w_enqueue(src_rows(win_d, IN_U, 512), 8, 512)
            w_enqueue(src_rows(win_d, IN_Z, 512), 8, 512)
            for i in range(6):
                w_enqueue(src_rows(win_d, IN_XBC + i * 512, 512), 8, 512)
            for i in range(4):
                w_enqueue(src_rows(win_d, IN_ZS + i * 512, 512), 8, 512)
            w_enqueue(src_rows(win_d, IN_DT, 32), 8, 32)
            for i in range(4):
                w_enqueue(src_rows(win_d, IN_G + i * 512, 512), 8, 512)
            w_enqueue(src_rows(wpo_d, 0, 1024), 4, 1024)
            for i in range(4):
                w_enqueue(src_rows(wssm_d, i * 256, 256), 16, 256)
            for i in range(2):
                w_enqueue(src_rows(wo_d, i * 512, 512), 8, 512)

        S.dma("sp", C("dma_start", out=cst[:], in_=cst_d[:, :]), writes=[B_cst])
        S.dma("sp", C("dma_start", out=vfm[:], in_=vfm_d[:, :]), writes=[B_vfm])
        S.dma("sp", C("dma_start", out=vrow[:], in_=vrow_d[0:1, :].partition_broadcast(128)), writes=[B_vrow])
        S.dma("sp", C("dma_start", out=cTs[:], in_=cT_d[:, :].rearrange("(k p) b -> p k b", p=128)), writes=[B_cTs])
        S.dma("pool", C("dma_start", out=poolw[:], in_=poolw_d[:, :].rearrange("(g c) d -> c g d", c=128)), writes=[B_poolw])
        w_prefetch()
        S.dma("pool", C("dma_start", out=cbf[:], in_=msk_d[:, :]), writes=[B_cbf])
        S.op("pool", C("memset", cm05[:], -0.5), writes=[B_cm05])
        S.op("pool", C("memset", hst[:], 0.0), writes=B_hst)
        S.op("pool", C("memset", hstb[:], 0.0), writes=B_hstb)
        S.op("pool", C("memset", chist[:], 0.0), writes=B_chist)
        S.op("pool", C("memset", uX[:], 0.0), writes=B_uX)
        S.op("act", C("activation", out=A_b, in_=vrow[:, VR_ALOG:VR_ALOG + 32], func=AF.Exp), reads=[B_vrow], writes=[B_misc])
        S.op("dve", C("tensor_scalar", out=A_b, in0=A_b, scalar1=-1.0, scalar2=None, op0=ALU.mult), reads=[B_misc], writes=[B_misc])
        S.op("dve", C("tensor_copy", out=D_b, in_=vrow[:, VR_DSKIP:VR_DSKIP + 32]), reads=[B_vrow], writes=[B_misc])
        S.op("act", C("activation", out=siluc[:], in_=cTs[:], func=AF.Silu), reads=[B_cTs], writes=[B_siluc])
        cexp_p = xD[:, 0:1024].rearrange("p (k t) -> p k t", k=8)
        cexp_s = xD[:, 1024:2048].rearrange("p (k t) -> p k t", k=8)
        S.op("dve", C("tensor_copy", out=cexp_p, in_=bc(siluc[:, :, 0:1], [128, 8, 128])), reads=[B_siluc], writes=[B_xD])
        for k in range(8):
            S.op("dve", C("tensor_copy",
                out=cexp_s[:, k, :].rearrange("p (t s) -> p t s", s=16),
                in_=bc(siluc[:, k, 1:17].unsqueeze(1), [128, 8, 16])), reads=[B_siluc], writes=[B_xD])
        bmod = ps_one()
        for gi in range(4):
            wv, wb, wh = w_next(8, 512)
            for jb in range(4):
                j = gi * 4 + jb
                for k in range(8):
                    S.op("pe", C("matmul",
                        psf(bmod, 17, j * 17), lhsT=wv[:, k, jb * 128:(jb + 1) * 128], rhs=siluc[:, k, :],
                        start=(k == 0), stop=(k == 7)), reads=[wb, B_siluc], writes=[B_ps[bmod]])
        modv = psf(bmod, 272).rearrange("p (j b) -> p j b", j=16)
        S.op("dve", C("tensor_tensor", out=St[:], in0=modv[:, 0:8, :], in1=bc(vfm[:, VF_BSHIFT:VF_BSHIFT + 8].unsqueeze(2), [128, 8, 17]), op=ALU.add),
             reads=[B_ps[bmod], B_vfm], writes=[B_St])
        S.op("dve", C("tensor_tensor", out=Gt[:], in0=modv[:, 8:16, :], in1=bc(vfm[:, VF_BSCALE:VF_BSCALE + 8].unsqueeze(2), [128, 8, 17]), op=ALU.add),
             reads=[B_ps[bmod], B_vfm], writes=[B_Gt])
        S.op("dve", C("scalar_tensor_tensor", out=Gt[:], in0=Gt[:], scalar=1.0, in1=bc(vfm[:, VF_NORMG:VF_NORMG + 8].unsqueeze(2), [128, 8, 17]), op0=ALU.add, op1=ALU.mult),
             reads=[B_Gt, B_vfm], writes=[B_Gt])
        S.dma("sp", C("dma_start", out=hbg[:], in_=bgate_d[0:1, :].partition_broadcast(128)), writes=[B_tmpA])
        S.op("dve", C("tensor_scalar", out=hbg[:], in0=hbg[:], scalar1=0.5, scalar2=None, op0=ALU.mult), reads=[B_tmpA], writes=[B_tmpA])
        for hf in range(2):
            wv, wb, wh = w_next(8, 512)
            for (cexp, gt, Bg) in ((cexp_p, gate_p, B_gate_p), (cexp_s, gate_s, B_gate_s)):
                b = ps_one()
                for k in range(8):
                    S.op("pe", C("matmul", psf(b), lhsT=cexp[:, k, :], rhs=wv[:, k, :], start=(k == 0), stop=(k == 7)),
                         reads=[B_xD, wb], writes=[B_ps[b]])
                S.op("dve", C("scalar_tensor_tensor", out=gt[:, hf * 512:(hf + 1) * 512], in0=psf(b), scalar=0.5, in1=hbg[:, hf * 512:(hf + 1) * 512], op0=ALU.mult, op1=ALU.add),
                     reads=[B_ps[b], B_tmpA], writes=[Bg])

        def rstd_from_ss(col, n, Bs=None):
            Bs = Bs or B_stat
            S.op("dve", C("tensor_scalar", out=stat[:, col + 1:col + 2], in0=stat[:, col:col + 1], scalar1=1.0 / n, scalar2=EPS, op0=ALU.mult, op1=ALU.add),
                 reads=[Bs], writes=[Bs])
            S.op("pool", C("tensor_tensor", out=stat[:, col + 1:col + 2], in0=stat[:, col + 1:col + 2], in1=cm05[:], op=ALU.pow),
                 reads=[Bs, B_cm05], writes=[Bs])

        xbrr = [0]

        def x_load(r0):
            i = xbrr[0]
            xbrr[0] ^= 1
            S.dma("sp", C("dma_start", out=xb[i][:], in_=x_d[r0:r0 + 128, :]), writes=[B_xb[i]])
            return i

        def ckpt(tag):
            if K_STOP == tag:
                raise _Stop()

        def main_loop():
          ckpt('P')
          def run_il(*gens):
              gens = list(gens)
              while gens:
                  for gen in list(gens):
                      try:
                          next(gen)
                      except StopIteration:
                          gens.remove(gen)

          def phaseA(tok0, NCH, SAMPLE):
              junk = dec[:, 0:1024]
              st = {}

              def sa(c):
                  xi = x_load(tok0 + c * 128)
                  st[c] = xi
                  S.op("act", C("activation", out=junk, in_=xb[xi][:], func=AF.Square, accum_out=stat[:, 2 * (c % 2):2 * (c % 2) + 1]),
                       reads=[B_xb[xi]], writes=[B_dec[0], B_statA[c % 2]])
                  rstd_from_ss(2 * (c % 2), 1024, B_statA[c % 2])

              def sb_(c):
                  xi = st[c]
                  S.op("act", C("activation", out=tmpA[:], in_=xb[xi][:], func=AF.Copy, scale=stat[:, 2 * (c % 2) + 1:2 * (c % 2) + 2]),
                       reads=[B_xb[xi], B_statA[c % 2]], writes=[B_tmpA])
                  bp = ps_pair()
                  st[("bp", c)] = bp
                  for k in range(8):
                      S.op("pe", C("transpose", out=ps[:, bp * 512 + k * 128: bp * 512 + (k + 1) * 128], in_=tmpA[:, k * 128:(k + 1) * 128], identity=ident),
                           reads=[B_tmpA, B_cst], writes=[B_ps[bp + k // 4]])

              def sd(c):
                  bp = st[("bp", c)]
                  if not SAMPLE:
                      for k in range(8):
                          src = ps[:, bp * 512 + k * 128: bp * 512 + (k + 1) * 128]
                          dst = hT[:, k, c * 128:(c + 1) * 128]
                          if k < 4:
                              S.op("dve", C("tensor_scalar", out=dst, in0=src, scalar1=Gt[:, k, 0:1], scalar2=St[:, k, 0:1], op0=ALU.mult, op1=ALU.add),
                                   reads=[B_ps[bp + k // 4], B_Gt, B_St], writes=[B_hT[c][k]])
                          else:
                              S.op("act", C("activation", out=dst, in_=src, func=AF.Identity, scale=Gt[:, k, 0:1], bias=St[:, k, 0:1]),
                                   reads=[B_ps[bp + k // 4], B_Gt, B_St], writes=[B_hT[c][k]])
                  else:
                      for k in range(8):
                          src = ps[:, bp * 512 + k * 128: bp * 512 + (k + 1) * 128].rearrange("p (t s) -> p t s", s=16)
                          dst = hT[:, k, 0:128].rearrange("p (t s) -> p t s", s=16)
                          tv = tmpg[0][:, 0:128].rearrange("p (t s) -> p t s", s=16)
                          S.op("dve", C("tensor_tensor", out=tv, in0=src, in1=bc(Gt[:, k, 1:17].unsqueeze(1), [128, 8, 16]), op=ALU.mult),
                               reads=[B_ps[bp + k // 4], B_Gt], writes=[B_tmpg[0]])
                          S.op("dve", C("tensor_tensor", out=dst, in0=tv, in1=bc(St[:, k, 1:17].unsqueeze(1), [128, 8, 16]), op=ALU.add),
                               reads=[B_tmpg[0], B_St], writes=[B_hT[0][k]])

              sa(0)
              yield
              for c in range(NCH):
                  sb_(c)
                  yield
                  if c + 1 < NCH:
                      sa(c + 1)
                      yield
                  sd(c)
                  yield

          for sti, (tok0, NCH, SAMPLE) in enumerate(ST_LIST):
              T = NCH * 128
              LAST_PROMPT = (tok0 == 1536)
              NS = 16 if SAMPLE else 1
              L = 8 if SAMPLE else T
              gate_t, Bgate = (gate_s, B_gate_s) if SAMPLE else (gate_p, B_gate_p)
              mU, mLm, mOne = (Usb, Lmsb, onesb) if SAMPLE else (Ub, Lmb, oneb)

              if sti == 0:
                  run_il(phaseA(tok0, NCH, SAMPLE))

              def hT_reads(k):
                  return [B_hT[c][k] for c in range(NCH)]

              def inproj_block(wv, wb, jb, b):
                  for k in range(8):
                      S.op("pe", C("matmul", psf(b, T), lhsT=wv[:, k, jb * 128:(jb + 1) * 128], rhs=hT[:, k, 0:T], start=(k == 0), stop=(k == 7)),
                           reads=[wb] + hT_reads(k), writes=[B_ps[b]])

              ckpt('%d:A' % sti)
              TS = 16 if SAMPLE else 1
              if SAMPLE:
                  for g in range(4):
                      S.dma("sp", C("dma_start", out=uX[:, g, 0:240], in_=spT_d[:, g * 240:(g + 1) * 240]), writes=[B_uX[g]])
              wv, wb, wh = w_next(8, 512)
              for g in range(4):
                  b = ps_one()
                  inproj_block(wv, wb, g, b)
                  S.op("act", C("activation", out=uX[:, g, 15 * TS:15 * TS + T], in_=psf(b, T), func=AF.Copy),
                       reads=[B_ps[b]], writes=[B_uX[g]])
              wv, wb, wh = w_next(8, 512)
              for g in range(4):
                  b = ps_one()
                  inproj_block(wv, wb, g, b)
                  S.op("act", C("activation", out=siluz[:, g, 0:T], in_=psf(b, T), func=AF.Silu),
                       reads=[B_ps[b]], writes=[B_siluz[g]])
              for g in range(4):
                  w = 2 << g
                  M = 15 + L
                  los = []
                  lo = 15
                  for lev in range(g, -1, -1):
                      los.append(lo)
                      lo -= (1 << lev)
                  los = los[::-1]
                  cur, Bcur = uX[:, g, :], B_uX[g]
                  for lev in range(g + 1):
                      sh = 1 << lev
                      lo_l = los[lev]
                      dstv, Bd = (Sa, B_Sa) if lev % 2 == 0 else (Sb_, B_Sb)
                      S.op("pool", C("tensor_tensor", out=dstv[:, lo_l * TS:M * TS], in0=cur[:, lo_l * TS:M * TS], in1=cur[:, (lo_l - sh) * TS:(M - sh) * TS], op=ALU.add),
                           reads=[Bcur], writes=[Bd])
                      cur, Bcur = dstv, Bd
                  S.op("dve", C("scalar_tensor_tensor", out=dTt[:, g, 0:T], in0=cur[:, 15 * TS:15 * TS + T], scalar=1.0 / w, in1=uX[:, g, 15 * TS:15 * TS + T], op0=ALU.mult, op1=ALU.subtract),
                       reads=[Bcur, B_uX[g]], writes=[B_dT[g]])
                  if tok0 == 0 and not SAMPLE:
                      S.op("dve", C("tensor_tensor", out=tmpg[0][:, 0:16], in0=cur[:, 15:31], in1=cst[:, C_INVC + g * 16:C_INVC + (g + 1) * 16], op=ALU.mult),
                           reads=[Bcur, B_cst], writes=[B_tmpg[0]])
                      S.op("dve", C("tensor_tensor", out=dTt[:, g, 0:16], in0=tmpg[0][:, 0:16], in1=uX[:, g, 15:31], op=ALU.subtract),
                           reads=[B_tmpg[0], B_uX[g]], writes=[B_dT[g]])
                  if LAST_PROMPT:
                      S.dma("sp", C("dma_start", out=npp_d[:, g * 15:(g + 1) * 15], in_=uX[:, g, T:T + 15]), reads=[B_uX[g]], final=True)
                  elif SAMPLE:
                      S.dma("sp", C("dma_start", out=nps_d[:, g * 240:(g + 1) * 240], in_=uX[:, g, 8 * 16:23 * 16]), reads=[B_uX[g]], final=True)
                  if not SAMPLE and not LAST_PROMPT:
                      S.op("pool", C("tensor_copy", out=uX[:, g, 0:15], in_=uX[:, g, T:T + 15]), reads=[B_uX[g]], writes=[B_uX[g]])

              ckpt('%d:B1' % sti)
              def diag_build(blk):
                  di = blk % 2
                  S.op("pool", C("tensor_tensor", out=dg[di][:], in0=bc(identb.unsqueeze(1), [128, 4, 128]), in1=bc(vfm[:, VF_CONVW + blk * 4:VF_CONVW + blk * 4 + 4].unsqueeze(2), [128, 4, 128]), op=ALU.mult),
                       reads=[B_cbf, B_vfm], writes=[B_dg[di]])

              def conv_block(blk, ri, braw):
                  di = blk % 2
                  b2 = ps_one()
                  for k in range(4):
                      S.op("pe", C("matmul", psf(b2, T), lhsT=dg[di][:, k, :], rhs=raw[ri][:, k * TS:k * TS + T], start=(k == 0), stop=(k == 3)),
                           reads=[B_dg[di], B_raw[ri]], writes=[B_ps[b2]])
                  S.op("act", C("activation", out=xcT[:, blk, 0:T], in_=psf(b2, T), func=AF.Silu, bias=vfm[:, VF_CONVB + blk:VF_CONVB + blk + 1]),
                       reads=[B_ps[b2], B_vfm], writes=[B_xcT[blk]])

              ncf2 = hst[:, 0:1152].rearrange("p (b m) -> p b m", b=24)
              if SAMPLE:
                  S.dma("sp", C("dma_start", out=hst[:, 0:1152], in_=scT_d[:, :]), writes=B_hst)
              pend = None
              for gi in range(6):
                  wv, wb, wh = w_next(8, 512)
                  for jb in range(4):
                      blk = gi * 4 + jb
                      ri = blk % 2
                      diag_build(blk)
                      if SAMPLE:
                          S.op("dve", C("tensor_copy", out=raw[ri][:, 0:48], in_=ncf2[:, blk, :]), reads=[B_ncf[blk]], writes=[B_raw[ri]])
                      else:
                          S.op("dve", C("tensor_copy", out=raw[ri][:, 0:3], in_=chist[:, blk, :]), reads=[B_chist[blk]], writes=[B_raw[ri]])
                      b = ps_one()
                      inproj_block(wv, wb, jb, b)
                      S.op("dve", C("tensor_copy", out=raw[ri][:, 3 * TS:3 * TS + T], in_=psf(b, T)), reads=[B_ps[b]], writes=[B_raw[ri]])
                      if LAST_PROMPT:
                          S.op("dve", C("tensor_copy", out=ncpt[:, blk, :], in_=psf(b, 3, T - 3)), reads=[B_ps[b]], writes=[B_ncpt])
                      elif SAMPLE:
                          S.op("dve", C("tensor_copy", out=ncf2[:, blk, :], in_=psf(b, 48, 80)), reads=[B_ps[b]], writes=[B_ncf[blk]])
                      else:
                          S.op("dve", C("tensor_copy", out=chist[:, blk, :], in_=psf(b, 3, T - 3)), reads=[B_ps[b]], writes=[B_chist[blk]])
                      if pend is not None:
                          conv_block(*pend)
                      pend = (blk, ri, b)
              conv_block(*pend)
              if SAMPLE:
                  S.dma("sp", C("dma_start", out=ncs_d[:, :], in_=hst[:, 0:1152]), reads=B_hst, final=True)
              if LAST_PROMPT:
                  S.dma("sp", C("dma_start", out=ncp_d[:, :].rearrange("p (b r) -> p b r", b=24), in_=ncpt[:]), reads=[B_ncpt], final=True)

              for g in range(4):
                  b = ps_one()
                  S.op("pe", C("matmul", psf(b, T), lhsT=poolw[:, g, :], rhs=dTt[:, g, 0:T], start=True, stop=True),
                       reads=[B_poolw, B_dT[g]], writes=[B_ps[b]])
                  S.op("dve", C("scalar_tensor_tensor", out=pzT[:, g, 0:T], in0=psf(b, T), scalar=vfm[:, VF_PSCALE + g:VF_PSCALE + g + 1], in1=siluz[:, g, 0:T], op0=ALU.mult, op1=ALU.mult),
                       reads=[B_ps[b], B_vfm, B_siluz[g]], writes=[B_siluz[g]])
              ckpt('%d:C' % sti)
              for zg in range(4):
                  wv, wb, wh = w_next(8, 512)
                  for c in range(NCH):
                      b = ps_one()
                      for k in range(8):
                          S.op("pe", C("matmul", psf(b), lhsT=hT[:, k, c * 128:(c + 1) * 128], rhs=wv[:, k, :], start=(k == 0), stop=(k == 7)),
                               reads=[wb, B_hT[c][k]], writes=[B_ps[b]])
                      S.op("act", C("activation", out=sz[:, c, zg * 512:(zg + 1) * 512], in_=psf(b), func=AF.Silu),
                           reads=[B_ps[b]], writes=[B_sz[c]])
              wv, wb, wh = w_next(8, 32)
              for c in range(NCH):
                  b = ps_one()
                  for k in range(8):
                      S.op("pe", C("matmul", psf(b, 32), lhsT=hT[:, k, c * 128:(c + 1) * 128], rhs=wv[:, k, :], start=(k == 0), stop=(k == 7)),
                           reads=[wb, B_hT[c][k]], writes=[B_ps[b]])
                  S.op("dve", C("tensor_tensor", out=vdt[:, c, :], in0=psf(b, 32), in1=vrow[:, VR_DTB:VR_DTB + 32], op=ALU.add),
                       reads=[B_ps[b], B_vrow], writes=[B_vdt[c]])

              ckpt('%d:B3' % sti)
              ybv = [yb[:].rearrange("p (g q) -> p g q", g=4), uX[:, :, 15:527]]
              B_ybv = [B_yb, B_uX]
              expacs, exprem, explast = ex[:, 0:32], ex[:, 32:64], ex[:, 64:96]
              xdt3 = xdt[:].rearrange("p (h q) -> p h q", h=32)
              xw3 = xw.rearrange("p (h q) -> p h q", h=32)

              def ystate_evac(b, g, yv, Byv):
                  S.op("dve", C("tensor_tensor", out=yv[:, g, :].rearrange("p (h q) -> p h q", h=8), in0=psf(b).rearrange("p (h q) -> p h q", h=8),
                                 in1=bc(expacs[:, g * 8:(g + 1) * 8].unsqueeze(2), [128, 8, 64]), op=ALU.mult),
                       reads=[B_ps[b], B_ex], writes=[Byv[g]])

              def r_build(g):
                  ri = g % 2
                  S.op("pool", C("tensor_tensor", out=Rg[ri][:].rearrange("p (h i) -> p h i", h=8), in0=bc(aa[:, g * 8:(g + 1) * 8].unsqueeze(2), [128, 8, 128]),
                                                                    in1=bc(mU.unsqueeze(1), [128, 8, 128]), op=ALU.mult),
                       reads=[B_aa, B_cbf], writes=[B_Rg[ri]])

              def stage1(c):
                  cs = slice(c * 128, (c + 1) * 128)
                  yv, Byv = ybv[c % 2], B_ybv[c % 2]
                  S.op("act", C("activation", out=dtt[:], in_=vdt[:, c, :], func=AF.Exp), reads=[B_vdt[c]], writes=[B_dtt])
                  S.op("act", C("activation", out=dtt[:], in_=dtt[:], func=AF.Ln, bias=1.0), reads=[B_dtt], writes=[B_dtt])
                  S.op("dve", C("tensor_tensor", out=aa[:], in0=dtt[:], in1=A_b, op=ALU.mult), reads=[B_dtt, B_misc], writes=[B_aa])
                  S.op("dve", C("tensor_copy", out=aab[:], in_=aa[:]), reads=[B_aa], writes=[B_aab])
                  bsm = ps_one()
                  for i, m in enumerate((mU, mLm, mOne)):
                      S.op("pe", C("matmul", psf(bsm, 32, i * 32), lhsT=m, rhs=aab[:], start=True, stop=True),
                           reads=[B_cbf, B_aab], writes=[B_ps[bsm]])
                  S.op("act", C("activation", out=ex[:], in_=psf(bsm, 96), func=AF.Exp), reads=[B_ps[bsm]], writes=[B_ex])
                  for g in range(2):
                      r_build(g)
                  yield
                  bx = ps_pair()
                  for blk in range(16):
                      S.op("pe", C("transpose", out=psb(bx, 128, blk * 128), in_=xcT[:, blk, cs], identity=identb),
                           reads=[B_xcT[blk], B_cbf], writes=[B_ps[bx + blk // 8]])
                  bB = ps_one()
                  for g in range(4):
                      S.op("pe", C("transpose", out=psb(bB, 128, g * 128), in_=xcT[:, 16 + g, cs], identity=identb),
                           reads=[B_xcT[16 + g], B_cbf], writes=[B_ps[bB]])
                  yield
                  xT3 = psb(bx, 2048).rearrange("p (h q) -> p h q", h=32)
                  S.op("dve", C("tensor_tensor", out=xdt3, in0=xT3, in1=bc(dtt[:].unsqueeze(2), [128, 32, 64]), op=ALU.mult),
                       reads=[B_ps[bx], B_ps[bx + 1], B_dtt], writes=[B_xdt])
                  S.op("act", C("activation", out=Btm[:], in_=psb(bB, 512), func=AF.Copy), reads=[B_ps[bB]], writes=[B_Btm])
                  yield
                  S.op("pool", C("tensor_tensor", out=xw3, in0=xdt3, in1=bc(exprem.unsqueeze(2), [128, 32, 64]), op=ALU.mult),
                       reads=[B_xdt, B_ex], writes=B_xw)
                  yield
                  bcb = ps_one()
                  for g in range(4):
                      S.op("pe", C("matmul", psf(bcb, 128, g * 128), lhsT=xcT[:, 16 + g, cs], rhs=xcT[:, 20 + g, cs], start=True, stop=True),
                           reads=[B_xcT[16 + g], B_xcT[20 + g]], writes=[B_ps[bcb]])
                  S.op("dve", C("tensor_tensor", out=cbTm[:], in0=psf(bcb).rearrange("p (g i) -> p g i", g=4), in1=bc(mU.unsqueeze(1), [128, 4, 128]), op=ALU.mult),
                       reads=[B_ps[bcb], B_cbf], writes=[B_cbTm])
                  S.op("dve", C("tensor_tensor", out=xD[:].rearrange("p (h q) -> p h q", h=32), in0=xT3, in1=bc(D_b.unsqueeze(2), [128, 32, 64]), op=ALU.mult),
                       reads=[B_ps[bx], B_ps[bx + 1], B_misc], writes=[B_xD])
                  yield
                  if not SAMPLE:
                      for g in range(4):
                          b = ps_one()
                          S.op("pe", C("matmul", psf(b), lhsT=xcT[:, 20 + g, cs], rhs=hstb[:, g * 512:(g + 1) * 512], start=True, stop=True),
                               reads=[B_xcT[20 + g], B_hstb[g]], writes=[B_ps[b]])
                          ystate_evac(b, g, yv, Byv)
                      yield
                      S.op("pool", C("tensor_tensor", out=hst[:].rearrange("p (h q) -> p h q", h=32), in0=hst[:].rearrange("p (h q) -> p h q", h=32),
                                     in1=bc(explast.unsqueeze(2), [128, 32, 64]), op=ALU.mult), reads=B_hst + [B_ex], writes=B_hst)
                      yield
                      for g in range(4):
                          b = ps_one()
                          S.op("pe", C("matmul", psf(b), lhsT=Btm[:, g * 128:(g + 1) * 128], rhs=xw[:, g * 512:(g + 1) * 512], start=True, stop=True),
                               reads=[B_Btm] + B_xw, writes=[B_ps[b]])
                          S.op("dve", C("tensor_tensor", out=hst[:, g * 512:(g + 1) * 512], in0=hst[:, g * 512:(g + 1) * 512], in1=psf(b), op=ALU.add),
                               reads=[B_hst[g], B_ps[b]], writes=[B_hst[g]])
                          S.op("act", C("activation", out=hstb[:, g * 512:(g + 1) * 512], in_=hst[:, g * 512:(g + 1) * 512], func=AF.Copy),
                               reads=[B_hst[g]], writes=[B_hstb[g]])
                      if LAST_PROMPT and c == NCH - 1:
                          S.dma("sp", C("dma_start", out=nsp_d[:, :], in_=hst[:]), reads=B_hst, final=True)
                  else:
                      S.op("pool", C("tensor_tensor", out=am[:], in0=bc(aa[:].unsqueeze(1), [128, 16, 32]), in1=bc(seqm.unsqueeze(2), [128, 16, 32]), op=ALU.mult),
                           reads=[B_aa, B_cst], writes=[B_am])
                      bl = ps_one()
                      S.op("pe", C("matmul", psf(bl), lhsT=oneb, rhs=am[:].rearrange("p s h -> p (s h)"), start=True, stop=True), reads=[B_cbf, B_am], writes=[B_ps[bl]])
                      S.op("act", C("activation", out=exl[:], in_=psf(bl), func=AF.Exp), reads=[B_ps[bl]], writes=[B_exl])
                      exl3 = exl[:].rearrange("p (s h) -> p s h", s=16)
                      NFB = 6
                      fbuf = [hsf[0], hsf[1]] + [uX[:, j, 15:527] for j in range(4)]
                      B_fbuf = [B_hsf[0], B_hsf[1]] + [[B_uX[j]] for j in range(4)]
                      bbuf = [hsb[0][:], hsb[1][:]] + [dTt[:, j, :] for j in range(4)]
                      B_bbuf = [[B_hsb[0]], [B_hsb[1]]] + [[B_dT[j]] for j in range(4)]
                      items = [(g, s) for g in range(4) for s in range(16)]

                      def ld(it):
                          g, s = items[it]
                          S.dma("sp", C("dma_start", out=fbuf[it % NFB], in_=ssT_d[s, :, g * 512:(g + 1) * 512]), writes=B_fbuf[it % NFB])
                          S.dma("pool", C("dma_start", out=bbuf[it % NFB], in_=ssT_d[s, :, g * 512:(g + 1) * 512]), writes=B_bbuf[it % NFB])

                      for it in range(NFB - 1):
                          ld(it)
                      bS = None
                      for it, (g, s) in enumerate(items):
                          if s == 0:
                              S.op("pool", C("memset", dec[:], 0.0), writes=B_dec)
                              S.op("pool", C("tensor_copy", out=dec[:, 0:16 * 129].rearrange("p (s m) -> p s m", m=129)[:, :, 0:113:16], in_=xcT[:, 20 + g, 0:128].rearrange("p (t s) -> p s t", s=16)),
                                   reads=[B_xcT[20 + g]], writes=B_dec)
                              S.op("pool", C("tensor_tensor", out=att[:].rearrange("p (s n) -> p s n", s=16), in0=bc(Btm[:, g * 128:(g + 1) * 128].unsqueeze(1), [128, 16, 128]),
                                                                         in1=bc(seqm.unsqueeze(2), [128, 16, 128]), op=ALU.mult),
                                   reads=[B_Btm, B_cst], writes=B_att)
                              bS = ps_one()
                          if it + NFB - 1 < len(items):
                              ld(it + NFB - 1)
                          fb_, Bf = fbuf[it % NFB], B_fbuf[it % NFB]
                          bb_, Bb = bbuf[it % NFB], B_bbuf[it % NFB]
                          S.op("pe", C("matmul", psf(bS), lhsT=dec[:, s * 128:(s + 1) * 128], rhs=bb_, start=(s == 0), stop=(s == 15)),
                               reads=B_dec + Bb, writes=[B_ps[bS]])
                          bH = ps_one()
                          if bH == bS:
                              bH = ps_one()
                          S.op("pe", C("matmul", psf(bH), lhsT=att[:, s * 128:(s + 1) * 128], rhs=xw[:, g * 512:(g + 1) * 512], start=True, stop=True),
                               reads=B_att + B_xw, writes=[B_ps[bH]])
                          S.op("pool" if it % 2 == 0 else "dve", C("tensor_tensor", out=fb_.rearrange("p (h q) -> p h q", h=8), in0=fb_.rearrange("p (h q) -> p h q", h=8),
                                                                               in1=bc(exl3[:, s, g * 8:(g + 1) * 8].unsqueeze(2), [128, 8, 64]), op=ALU.mult),
                               reads=Bf + [B_exl], writes=Bf)
                          S.op("dve", C("tensor_tensor", out=fb_, in0=fb_, in1=psf(bH), op=ALU.add),
                               reads=Bf + [B_ps[bH]], writes=Bf)
                          S.dma("act", C("dma_start", out=nss_d[s, :, g * 512:(g + 1) * 512], in_=fb_), reads=Bf, final=True)
                          if s == 15:
                              ystate_evac(bS, g, yv, Byv)

              def stage2(c):
                  yv, Byv = ybv[c % 2], B_ybv[c % 2]
                  def pre(g):
                      ri = g % 2
                      if g >= 2:
                          r_build(g)
                      bs = ps_pair()
                      for q in range(2):
                          S.op("pe", C("matmul", psf(bs + q), lhsT=mLm, rhs=Rg[ri][:, q * 512:(q + 1) * 512], start=True, stop=True),
                               reads=[B_cbf, B_Rg[ri]], writes=[B_ps[bs + q]])
                      dv = dec[:, ri * 1024:(ri + 1) * 1024]
                      S.op("act", C("activation", out=dv, in_=ps[:, bs * 512:(bs + 2) * 512], func=AF.Exp),
                           reads=[B_ps[bs], B_ps[bs + 1]], writes=[B_dec[ri]])

                  def post(g):
                      ri = g % 2
                      dv = dec[:, ri * 1024:(ri + 1) * 1024]
                      av = att[:, ri * 1024:(ri + 1) * 1024].rearrange("p (h i) -> p h i", h=8)
                      S.op("dve", C("tensor_tensor", out=av, in0=dv.rearrange("p (h i) -> p h i", h=8), in1=bc(cbTm[:, g, :].unsqueeze(1), [128, 8, 128]), op=ALU.mult),
                           reads=[B_dec[ri], B_cbTm], writes=[B_att[ri]])
                      by = ps_one()
                      S.op("pe", C("matmul", psf(by), lhsT=zerob, rhs=xdt[:, g * 512:(g + 1) * 512], start=True, stop=False),
                           reads=[B_cbf, B_xdt], writes=[B_ps[by]])
                      for hh in range(8):
                          S.op("pe", C("matmul", psf(by, 64, hh * 64), lhsT=av[:, hh, :], rhs=xdt[:, (g * 8 + hh) * 64:(g * 8 + hh + 1) * 64], start=False, stop=False),
                               reads=[B_att[ri], B_xdt], writes=[B_ps[by]])
                      S.op("pe", C("matmul", psf(by), lhsT=identb, rhs=xD[:, g * 512:(g + 1) * 512], start=False, stop=True),
                           reads=[B_cbf, B_xD], writes=[B_ps[by]])
                      S.op("dve", C("tensor_tensor", out=yv[:, g, :], in0=yv[:, g, :], in1=psf(by), op=ALU.add),
                           reads=[Byv[g], B_ps[by]], writes=[Byv[g]])

                  pre(0)
                  yield
                  pre(1)
                  yield
                  post(0)
                  yield
                  pre(2)
                  yield
                  post(1)
                  yield
                  pre(3)
                  yield
                  post(2)
                  yield
                  post(3)

              def stage3(c):
                  cs = slice(c * 128, (c + 1) * 128)
                  yv, Byv = ybv[c % 2], B_ybv[c % 2]
                  if DEBUG and sti == 0 and c == 0:
                      S.dma("sp", C("dma_start", out=dbg["d_yb"][:, :].rearrange("p (g q) -> p g q", g=4), in_=yv), reads=Byv, final=True)
                  S.op("dve", C("tensor_tensor", out=yv, in0=yv, in1=sz[:, c, :].rearrange("p (g q) -> p g q", g=4), op=ALU.mult), reads=Byv + [B_sz[c]], writes=Byv)
                  yield
                  S.op("act", C("activation", out=tmpA[:].rearrange("p (g q) -> p g q", g=2), in_=yv[:, 0:2, :], func=AF.Square, accum_out=stat[:, 4:5]), reads=Byv, writes=[B_tmpA, B_stat])
                  S.op("act", C("activation", out=tmpA[:].rearrange("p (g q) -> p g q", g=2), in_=yv[:, 2:4, :], func=AF.Square, accum_out=stat[:, 5:6]), reads=Byv, writes=[B_tmpA, B_stat])
                  yield
                  S.op("dve", C("tensor_tensor", out=stat[:, 6:7], in0=stat[:, 4:5], in1=stat[:, 5:6], op=ALU.add), reads=[B_stat], writes=[B_stat])
                  rstd_from_ss(6, 2048)
                  yield
                  S.op("act", C("activation", out=ynb[:].rearrange("p (g q) -> p g q", g=4), in_=yv, func=AF.Copy, scale=stat[:, 7:8]), reads=Byv + [B_stat], writes=B_ynb)
                  yield
                  by2 = ps_pair()
                  for kk in range(16):
                      S.op("pe", C("transpose", out=psb(by2, 128, kk * 128), in_=ynb[:, kk * 128:(kk + 1) * 128], identity=identb),
                           reads=B_ynb + [B_cbf], writes=[B_ps[by2 + kk // 8]])
                  yield
                  for hb in range(2):
                      S.op("dve", C("tensor_tensor", out=ynT[:, hb * 8:(hb + 1) * 8, cs], in0=psb(by2 + hb, 1024).rearrange("p (k t) -> p k t", k=8),
                                     in1=bc(vfm[:, VF_SNG + hb * 8:VF_SNG + (hb + 1) * 8].unsqueeze(2), [128, 8, 128]), op=ALU.mult),
                           reads=[B_ps[by2 + hb], B_vfm], writes=[B_ynT[c][hb]])

              run_il(stage1(0))
              run_il(stage2(0))
              for c in range(1, NCH):
                  run_il(stage1(c), stage3(c - 1))
                  run_il(stage2(c))
              run_il(stage3(NCH - 1))

              if DEBUG and sti == 0:
                  S.dma("pool", C("dma_start", out=dbg["d_ynT"][:, :], in_=ynT[:].rearrange("p k t -> p (k t)")), reads=[b for r in B_ynT for b in r], final=True)
                  S.dma("pool", C("dma_start", out=dbg["d_pzT"][:, :], in_=siluz[:].rearrange("p k t -> p (k t)")), reads=B_siluz, final=True)
                  S.dma("pool", C("dma_start", out=dbg["d_xc"][:, :], in_=xcT[:].rearrange("p k t -> p (k t)")), reads=B_xcT, final=True)
              ckpt('%d:D' % sti)
              th12 = sz[:].rearrange("p c n -> p (c n)")[:, 0:16 * T].rearrange("p (j t) -> p j t", j=16)
              for gg in range(4):
                  wv, wb, wh = w_next(8, 512)
                  for jb in range(4):
                      j = gg * 4 + jb
                      b = ps_one()
                      inproj_block(wv, wb, jb, b)
                      S.op("act", C("activation", out=th12[:, j, :], in_=psf(b, T), func=AF.Tanh, scale=0.5), reads=[B_ps[b]], writes=B_sz)
              def phaseEF():
                  for c in range(min(2, NCH)):
                      r0 = tok0 + c * 128
                      S.dma("sp", C("dma_start", out=outb[c % 2], in_=x_d[r0:r0 + 128, :]), writes=B_outb[c % 2])
                  wpv, wpb, wph = w_next(4, 1024, hold=True, lag=False)
                  for fb in range(8):
                      if fb % 2 == 0:
                          wsv, wsb, wsh = w_next(16, 256, lag=False)
                      bP = ps_one()
                      for kc in range(4):
                          S.op("pe", C("matmul", psf(bP, T), lhsT=wpv[:, kc, fb * 128:(fb + 1) * 128], rhs=pzT[:, kc, 0:T], start=(kc == 0), stop=(kc == 3)),
                               reads=[wpb, B_pzT[kc]], writes=[B_ps[bP]])
                      bS2 = ps_one()
                      for kc in range(16):
                          S.op("pe", C("matmul", psf(bS2, T), lhsT=wsv[:, kc, (fb % 2) * 128:(fb % 2 + 1) * 128], rhs=ynT[:, kc, 0:T], start=(kc == 0), stop=(kc == 15)),
                               reads=[wsb] + [B_ynT[cc][pp] for cc in range(NCH) for pp in range(2)], writes=[B_ps[bS2]])
                      S.op("dve", C("scalar_tensor_tensor", out=tmpg[0][:, 0:T], in0=th12[:, fb, :], scalar=1.0, in1=psf(bP, T), op0=ALU.add, op1=ALU.mult),
                           reads=B_sz + [B_ps[bP]], writes=[B_tmpg[0]])
                      S.op("dve", C("scalar_tensor_tensor", out=tmpg[1][:, 0:T], in0=th12[:, 8 + fb, :], scalar=1.0, in1=psf(bS2, T), op0=ALU.add, op1=ALU.mult),
                           reads=B_sz + [B_ps[bS2]], writes=[B_tmpg[1]])
                      S.op("pool", C("tensor_tensor", out=xcT[:, fb, 0:T], in0=tmpg[0][:, 0:T], in1=tmpg[1][:, 0:T], op=ALU.add),
                           reads=[B_tmpg[0], B_tmpg[1]], writes=[B_xcT[fb]])
                      yield
                  if DEBUG and sti == 0:
                      S.dma("pool", C("dma_start", out=dbg["d_mT"][:, :], in_=xcT[:, 0:8, :].rearrange("p k t -> p (k t)")), reads=B_xcT[0:8], final=True)
                      S.dma("pool", C("dma_start", out=dbg["d_th"][:, :], in_=sz[:].rearrange("p c n -> p (c n)")), reads=B_sz, final=True)
                  w_release(wph)
                  wov = []
                  for hf in range(2):
                      wov.append(w_next(8, 512, hold=True, lag=False))
                  fst = {}

                  def F1(c):
                      oi = c % 2
                      r0 = tok0 + c * 128
                      if c >= 2:
                          S.dma("sp", C("dma_start", out=outb[oi], in_=x_d[r0:r0 + 128, :]), writes=B_outb[oi])
                      for hf in range(2):
                          b = ps_one()
                          for k in range(8):
                              S.op("pe", C("matmul", psf(b), lhsT=xcT[:, k, c * 128:(c + 1) * 128], rhs=wov[hf][0][:, k, :], start=(k == 0), stop=(k == 7)),
                                   reads=[wov[hf][1], B_xcT[k]], writes=[B_ps[b]])
                          S.op("dve", C("tensor_tensor", out=tmpg[hf][:], in0=psf(b), in1=gate_t[:, hf * 512:(hf + 1) * 512], op=ALU.mult),
                               reads=[B_ps[b], Bgate], writes=[B_tmpg[hf]])
                          S.op("pool", C("tensor_tensor", out=outb[oi][:, hf * 512:(hf + 1) * 512], in0=outb[oi][:, hf * 512:(hf + 1) * 512], in1=tmpg[hf][:], op=ALU.add),
                               reads=B_outb[oi] + [B_tmpg[hf]], writes=B_outb[oi])

                  def F2(c):
                      oi = c % 2
                      S.op("act", C("activation", out=dec[:, 0:1024], in_=outb[oi], func=AF.Square, accum_out=stat[:, 8 + 2 * oi:9 + 2 * oi]), reads=B_outb[oi], writes=[B_dec[0], B_statF[oi]])
                      rstd_from_ss(8 + 2 * oi, 1024, B_statF[oi])

                  def F3(c):
                      oi = c % 2
                      S.op("act", C("activation", out=outb[oi], in_=outb[oi], func=AF.Copy, scale=stat[:, 9 + 2 * oi:10 + 2 * oi]), reads=B_outb[oi] + [B_statF[oi]], writes=B_outb[oi])
                      S.op("dve", C("tensor_tensor", out=outb[oi], in0=outb[oi], in1=vrow[:, VR_FING:VR_FING + 1024], op=ALU.mult), reads=B_outb[oi] + [B_vrow], writes=B_outb[oi])
                      r0 = tok0 + c * 128
                      S.dma("sp", C("dma_start", out=y_d[r0:r0 + 128, :], in_=outb[oi]), reads=B_outb[oi], final=True)

                  for c0 in range(0, NCH, 2):
                      cc = [c for c in (c0, c0 + 1) if c < NCH]
                      for fn in (F1, F2, F3):
                          for c in cc:
                              fn(c)
                          yield
                  for (_, _, whh) in wov:
                      w_release(whh)

              ckpt('%d:E' % sti)
              if sti + 1 < len(ST_LIST):
                  run_il(phaseEF(), phaseA(*ST_LIST[sti + 1]))
              else:
                  run_il(phaseEF())

        try:
            main_loop()
            assert wstate["used"] == len(wq), (wstate, len(wq))
        except _Stop:
            pass

        with nc.Block() as block:
            @block.sync
            def _(e):
                S.emit("sp", e)

            @block.tensor
            def _(e):
                S.emit("pe", e)

            @block.scalar
            def _(e):
                S.emit("act", e)

            @block.vector
            def _(e):
                S.emit("dve", e)

            @block.gpsimd
            def _(e):
                S.emit("pool", e)
    return nc


def _consts():
    k = np.arange(128)
    c = np.zeros((128, NCST), np.float32)
    c[:, C_ID:C_ID + 128] = np.eye(128)
    c[:, C_SEQ:C_SEQ + 16] = (k[:, None] % 16 == np.arange(16)[None, :])
    pos = np.arange(16)
    for g in range(4):
        w = 2 << g
        c[:, C_INVC + g * 16:C_INVC + (g + 1) * 16] = (1.0 / np.minimum(pos + 1, w))[None, :]
    U = (k[:, None] <= k[None, :])
    Lm = (k[:, None] > k[None, :])
    same = (k[:, None] % 16 == k[None, :] % 16)
    m = np.concatenate([np.eye(128), U, Lm, np.ones((128, 128)), U & same, Lm & same, same, np.zeros((128, 128))], axis=1).astype(np.float32)
    return c, m


def make_in_maps(inp, cores):
    f = lambda a: np.ascontiguousarray(np.asarray(a, dtype=np.float32))
    w_ada, w_in = f(inp["w_ada"][0]), f(inp["w_in"][0])
    wpo, wssm, wo = f(inp["w_pool_out"][0]), f(inp["w_ssm_out"][0]), f(inp["w_o"][0])
    poolw = f(np.asarray(inp["pool_w"][0]).reshape(512, 128))
    b_ada = np.asarray(inp["b_ada"][0], np.float32)
    fm = lambda v: np.asarray(v, np.float32).reshape(-1, 128).T
    vfm = np.zeros((128, NVF), np.float32)
    vfm[:, VF_BSHIFT:VF_BSHIFT + 8] = fm(b_ada[0:1024])
    vfm[:, VF_BSCALE:VF_BSCALE + 8] = fm(b_ada[1024:2048])
    vfm[:, VF_NORMG:VF_NORMG + 8] = fm(inp["norm_g"][0])
    cw = np.asarray(inp["conv_w"][0], np.float32)
    vfm[:, VF_CONVW:VF_CONVW + 96] = cw.reshape(4, 24, 128).transpose(2, 1, 0).reshape(128, 96)
    vfm[:, VF_CONVB:VF_CONVB + 24] = fm(inp["conv_b"][0])
    vfm[:, VF_PSCALE:VF_PSCALE + 4] = fm(inp["pool_scale"][0])
    vfm[:, VF_SNG:VF_SNG + 16] = fm(inp["ssm_norm_g"][0])
    vrow = np.zeros((1, NVR), np.float32)
    vrow[0, VR_FING:VR_FING + 1024] = np.asarray(inp["final_g"], np.float32)
    vrow[0, VR_DTB:VR_DTB + 32] = np.asarray(inp["dt_bias"][0], np.float32)
    vrow[0, VR_ALOG:VR_ALOG + 32] = np.asarray(inp["a_log"][0], np.float32)
    vrow[0, VR_DSKIP:VR_DSKIP + 32] = np.asarray(inp["d_skip"][0], np.float32)
    cst, msk = _consts()
    xp, xs = np.asarray(inp["x_prompt"], np.float32), np.asarray(inp["x_sample"], np.float32)
    cp, csm = np.asarray(inp["c_prompt"], np.float32), np.asarray(inp["c_sample"], np.float32)
    sp, sc, ss = np.asarray(inp["state_pool"][0]), np.asarray(inp["state_conv"][0]), np.asarray(inp["state_ssm"][0])
    maps = []
    for c in cores:
        sl = slice(16 * c, 16 * c + 16)
        x = np.concatenate([xp[c], xs[sl].transpose(1, 0, 2).reshape(128, D)], axis=0)
        cT = np.concatenate([cp[c:c + 1], csm[sl]], axis=0).T
        spT = sp[sl].reshape(16, 15, 4, 128).transpose(3, 2, 1, 0).reshape(128, 960)
        scT = sc[sl].reshape(16, 3, 24, 128).transpose(3, 2, 1, 0).reshape(128, 1152)
        ssT = ss[sl].reshape(16, 2048, 128).transpose(0, 2, 1)
        maps.append({
            "x": f(x), "cT": f(cT), "spT": f(spT), "scT": f(scT), "ssT": f(ssT),
            "w_ada": w_ada, "w_in": w_in, "w_pool_out": wpo, "w_ssm_out": wssm, "w_o": wo,
            "pool_w": poolw, "vecs_fm": vfm, "vecs_row": vrow, "consts": cst, "masks": msk, "b_gate": f(b_ada[None, 2048:3072]),
        })
    return maps


def assemble(results, cores):
    n = len(cores)
    y_prompt = np.zeros((n, SEQ, D), np.float32)
    y_sample = np.zeros((16 * n, 8, D), np.float32)
    npp = np.zeros((1, n, 15, 512), np.float32)
    ncp = np.zeros((1, n, 3, 3072), np.float32)
    nsp = np.zeros((1, n, 32, 64, 128), np.float32)
    nps = np.zeros((1, 16 * n, 15, 512), np.float32)
    ncs = np.zeros((1, 16 * n, 3, 3072), np.float32)
    nss = np.zeros((1, 16 * n, 32, 64, 128), np.float32)
    for i, r in enumerate(results):
        sl = slice(16 * i, 16 * i + 16)
        y = np.asarray(r["y"])
        y_prompt[i] = y[0:SEQ]
        y_sample[sl] = y[SEQ:].reshape(8, 16, D).transpose(1, 0, 2)
        npp[0, i] = np.asarray(r["npp"]).reshape(128, 4, 15).transpose(2, 1, 0).reshape(15, 512)
        ncp[0, i] = np.asarray(r["ncp"]).reshape(128, 24, 3).transpose(2, 1, 0).reshape(3, 3072)
        nsp[0, i] = np.asarray(r["nsp"]).T.reshape(32, 64, 128)
        nps[0, sl] = np.asarray(r["nps"]).reshape(128, 4, 15, 16).transpose(3, 2, 1, 0).reshape(16, 15, 512)
        ncs[0, sl] = np.asarray(r["ncs"]).reshape(128, 24, 3, 16).transpose(3, 2, 1, 0).reshape(16, 3, 3072)
        nss[0, sl] = np.asarray(r["nss"]).transpose(0, 2, 1).reshape(16, 32, 64, 128)
    return (y_prompt, y_sample, npp, ncp, nsp, nps, ncs, nss)


def kernel(**inputs):
    cores = list(range(NCORES))
    nc = build_nc()
    in_maps = make_in_maps(inputs, cores)
    res = run_bass_kernel_spmd(nc, in_maps, core_ids=cores)
    return assemble(res.results, cores)
```

```python
import numpy as np
import concourse.bass as bass
import concourse.mybir as mybir
from concourse.bass_utils import run_bass_kernel_spmd
from contextlib import ExitStack

F32 = mybir.dt.float32
BF16 = mybir.dt.bfloat16
AF = mybir.ActivationFunctionType
ALU = mybir.AluOpType

NCORES = 8
D = 1024
SEQ = 2048
NTOK = 2176
IN_U, IN_Z, IN_ZS, IN_XBC, IN_DT, IN_G = 0, 512, 1024, 3072, 6144, 6176
EPS = 1e-6
VF_BSHIFT, VF_BSCALE, VF_NORMG, VF_CONVW, VF_CONVB, VF_PSCALE, VF_SNG, NVF = 0, 8, 16, 24, 120, 144, 148, 164
VR_FING, VR_DTB, VR_ALOG, VR_DSKIP, NVR = 0, 1024, 1056, 1088, 1120
C_ID, C_SEQ, C_INVC, NCST = 0, 128, 144, 208

import os
DEBUG = bool(int(os.environ.get('K_DEBUG', '0')))
K_STOP = os.environ.get('K_STOP', '')


class _Stop(Exception):
    pass


class Buf:
    __slots__ = ("name", "lw", "rd", "excl")

    def __init__(self, name, excl=False):
        self.name = name
        self.lw = None
        self.rd = {}
        self.excl = excl


def bufs(name, *dims):
    if len(dims) == 1:
        return [Buf("%s%d" % (name, i)) for i in range(dims[0])]
    return [bufs("%s%d_" % (name, i), *dims[1:]) for i in range(dims[0])]


class Sched:
    ENG = ("pe", "act", "dve", "pool", "sp")

    def __init__(self, sems, dma_sems):
        self.sem = sems
        self.dma_sems = dma_sems
        self.dma_tgt = [0] * len(dma_sems)
        self.dma_rr = 0
        self.cnt = {e: 0 for e in self.ENG}
        self.ops = {e: [] for e in self.ENG}
        self.seen = {e: {} for e in self.ENG}
        self.final_tokens = []

    def _handle(self, key):
        return self.sem[key] if isinstance(key, str) else self.dma_sems[key]

    def _deps(self, eng, reads, writes):
        need = {}

        def add(tok):
            if tok is None:
                return
            k, v = tok
            if k == eng and eng == "pe":
                return
            if need.get(k, 0) < v:
                need[k] = v
        for b in reads:
            add(b.lw)
        for b in writes:
            add(b.lw)
            for k, v in b.rd.items():
                add((k, v))
        out = []
        for k, v in need.items():
            if self.seen[eng].get(k, 0) >= v:
                continue
            self.seen[eng][k] = v
            out.append((k, v))
        return out

    def op(self, eng, fn, reads=(), writes=()):
        ex = [b for b in reads if b.excl]
        if ex:
            reads = [b for b in reads if not b.excl]
            writes = list(writes) + ex
        waits = self._deps(eng, reads, writes)
        self.cnt[eng] += 1
        c = self.cnt[eng]
        self.ops[eng].append((waits, fn, (eng, 1)))
        for b in reads:
            if b.rd.get(eng, 0) < c:
                b.rd[eng] = c
        for b in writes:
            b.lw = (eng, c)
            b.rd = {}

    def dma(self, q, fn, reads=(), writes=(), final=False):
        waits = self._deps(q, reads, writes)
        s = self.dma_rr
        self.dma_rr = (self.dma_rr + 1) % len(self.dma_sems)
        prev = self.dma_tgt[s]
        if prev > 0 and self.seen[q].get(s, 0) < prev:
            self.seen[q][s] = prev
            waits.append((s, prev))
        tgt = prev + 16
        self.dma_tgt[s] = tgt
        self.ops[q].append((waits, fn, (s, 16)))
        for b in reads:
            if b.rd.get(s, 0) < tgt:
                b.rd[s] = tgt
        for b in writes:
            b.lw = (s, tgt)
            b.rd = {}
        if final:
            self.final_tokens.append((s, tgt))

    def emit(self, eng, e):
        for waits, fn, inc in self.ops[eng]:
            for k, v in waits:
                e.wait_ge(self._handle(k), v)
            fn(e).then_inc(self._handle(inc[0]), inc[1])
        if eng == "sp":
            done = {}
            for k, v in self.final_tokens:
                done[k] = max(done.get(k, 0), v)
            for k, v in done.items():
                e.wait_ge(self._handle(k), v)


def C(name, *args, **kw):
    def f(e):
        return getattr(e, name)(*args, **kw)
    return f


def bc(ap, shape):
    return ap.to_broadcast(list(shape))


def build_nc():
    nc = bass.Bass("TRN2", target_bir_lowering=False)

    def din(name, shape):
        return nc.dram_tensor(name, list(shape), F32, kind="ExternalInput").ap()

    def dout(name, shape):
        return nc.dram_tensor(name, list(shape), F32, kind="ExternalOutput").ap()

    x_d = din("x", [NTOK, D])
    cT_d = din("cT", [D, 17])
    spT_d = din("spT", [128, 4 * 16 * 15])
    scT_d = din("scT", [128, 24 * 16 * 3])
    ssT_d = din("ssT", [16, 128, 2048])
    wada_d = din("w_ada", [D, 3072])
    win_d = din("w_in", [D, 8224])
    wpo_d = din("w_pool_out", [512, D])
    wssm_d = din("w_ssm_out", [2048, D])
    wo_d = din("w_o", [D, D])
    poolw_d = din("pool_w", [512, 128])
    vfm_d = din("vecs_fm", [128, NVF])
    vrow_d = din("vecs_row", [1, NVR])
    bgate_d = din("b_gate", [1, 1024])
    cst_d = din("consts", [128, NCST])
    msk_d = din("masks", [128, 8 * 128])

    y_d = dout("y", [NTOK, D])
    npp_d = dout("npp", [128, 60])
    ncp_d = dout("ncp", [128, 72])
    nsp_d = dout("nsp", [128, 2048])
    nps_d = dout("nps", [128, 960])
    ncs_d = dout("ncs", [128, 1152])
    nss_d = dout("nss", [16, 128, 2048])
    dbg = {}
    if DEBUG:
        for nm, n in (("d_mT", 4096), ("d_ynT", 8192), ("d_pzT", 2048), ("d_th", 8192), ("d_yb", 2048), ("d_xc", 12288)):
            dbg[nm] = dout(nm, [128, n])

    with ExitStack() as es:
        def sb(name, shape, dt=F32):
            return es.enter_context(nc.sbuf_tensor(name, list(shape), dt))

        sems = {e: es.enter_context(nc.semaphore("s_" + e)) for e in Sched.ENG}
        dsem = [es.enter_context(nc.semaphore("d%d" % i)) for i in range(24)]
        S = Sched(sems, dsem)

        NSLOT = 4
        wslot = [sb("wslot%d" % i, [128, 4096], BF16) for i in range(NSLOT)]
        B_wslot = bufs("wslot", NSLOT)
        cst = sb("cst", [128, NCST]); B_cst = Buf("cst")
        cbf = sb("cbf", [128, 8 * 128], BF16); B_cbf = Buf("cbf")
        vfm = sb("vfm", [128, NVF]); B_vfm = Buf("vfm")
        vrow = sb("vrow", [128, NVR]); B_vrow = Buf("vrow")
        misc = sb("misc", [128, 256]); B_misc = Buf("misc")
        cTs = sb("cTs", [128, 8, 17]); B_cTs = Buf("cTs")
        siluc = sb("siluc", [128, 8, 17], BF16); B_siluc = Buf("siluc")
        Gt = sb("Gt", [128, 8, 17]); B_Gt = Buf("Gt")
        St = sb("St", [128, 8, 17]); B_St = Buf("St")
        gate_p = sb("gate_p", [128, 1024]); B_gate_p = Buf("gate_p")
        gate_s = sb("gate_s", [128, 1024]); B_gate_s = Buf("gate_s")
        poolw = sb("poolw", [128, 4, 128], BF16); B_poolw = Buf("poolw")
        xb = [sb("xb%d" % i, [128, 1024]) for i in range(2)]; B_xb = bufs("xb", 2)
        tmpA = sb("tmpA", [128, 1024]); B_tmpA = Buf("tmpA")
        stat = sb("stat", [128, 16]); B_stat = Buf("stat"); B_statA = bufs("statA", 2); B_statF = bufs("statF", 2)
        cm05 = sb("cm05", [128, 1]); B_cm05 = Buf("cm05")
        hT = sb("hT", [128, 8, 512], BF16); B_hT = bufs("hT", 4, 8)
        ynT = sb("ynT", [128, 16, 512], BF16); B_ynT = bufs("ynT", 4, 2)
        uX = sb("uX", [128, 4, 527]); B_uX = bufs("uX", 4)
        SaSb = sb("SaSb", [128, 1056]); B_Sa = Buf("Sa"); B_Sb = Buf("Sb")
        Sa = SaSb[:, 0:528]
        Sb_ = SaSb[:, 528:1056]
        xw = SaSb[:].bitcast(BF16)[:, 0:2048]; B_xw = [B_Sa, B_Sb]
        siluz = sb("siluz", [128, 4, 512], BF16); B_siluz = bufs("siluz", 4)
        dTt = sb("dT", [128, 4, 512], BF16); B_dT = bufs("dT", 4)
        pzT = siluz; B_pzT = B_siluz
        raw = [sb("raw%d" % i, [128, 516], BF16) for i in range(2)]; B_raw = bufs("raw", 2)
        chist = sb("chist", [128, 24, 3], BF16); B_chist = bufs("chist", 24)
        dg = [sb("dg%d" % i, [128, 4, 128], BF16) for i in range(2)]; B_dg = bufs("dg", 2)
        xcT = sb("xcT", [128, 24, 512], BF16); B_xcT = bufs("xcT", 24)
        sz = sb("sz", [128, 4, 2048], BF16); B_sz = bufs("sz", 4)
        vdt = sb("vdt", [128, 4, 32]); B_vdt = bufs("vdt", 4)
        dtt = sb("dtt", [128, 32]); B_dtt = Buf("dtt")
        aa = sb("aa", [128, 32]); B_aa = Buf("aa")
        aab = sb("aab", [128, 32], BF16); B_aab = Buf("aab")
        ex = sb("ex", [128, 96]); B_ex = Buf("ex")
        xdt = sb("xdt", [128, 2048], BF16); B_xdt = Buf("xdt")
        xD = sb("xD", [128, 2048], BF16); B_xD = Buf("xD")
        Btm = sb("Btm", [128, 512], BF16); B_Btm = Buf("Btm")
        Rg = [sb("Rg%d" % i, [128, 1024], BF16) for i in range(2)]; B_Rg = bufs("Rg", 2)
        dec = sb("dec", [128, 2176], BF16); B_dec = bufs("dec", 2)
        att = sb("att", [128, 2048], BF16); B_att = bufs("att", 2)
        cbTm = sb("cbTm", [128, 4, 128], BF16); B_cbTm = Buf("cbTm")
        ynb = att; B_ynb = B_att
        yb = sb("yb", [128, 2048]); B_yb = bufs("yb", 4)
        hst = sb("hst", [128, 2048]); B_hst = bufs("hst", 4)
        hstb = sb("hstb", [128, 2048], BF16); B_hstb = bufs("hstb", 4)
        tmpg = [sb("tmpg%d" % i, [128, 512]) for i in range(2)]; B_tmpg = bufs("tmpg", 2)
        hsf = [hstb[:].bitcast(F32)[:, i * 512:(i + 1) * 512] for i in range(2)]; B_hsf = [[B_hstb[0], B_hstb[1]], [B_hstb[2], B_hstb[3]]]
        hsb = [sb("hsb%d" % i, [128, 512], BF16) for i in range(2)]; B_hsb = bufs("hsb", 2)
        am = sb("am", [128, 16, 32], BF16); B_am = Buf("am")
        exl = sb("exl", [128, 512]); B_exl = Buf("exl")
        ncf = hst[:, 0:1152].rearrange("p (b s r) -> p b s r", b=24, s=16); B_ncf = [B_hst[(b * 48) // 512] for b in range(24)]
        ncpt = sb("ncpt", [128, 24, 3]); B_ncpt = Buf("ncpt")
        outb = [yb[:, 0:1024], yb[:, 1024:2048]]; B_outb = [[B_yb[0], B_yb[1]], [B_yb[2], B_yb[3]]]

        ps = es.enter_context(nc.psum_tensor("ps", [128, 4096], F32))
        B_ps = [Buf("ps%d" % i, excl=True) for i in range(8)]
        psrr = [0]

        def ps_one():
            b = psrr[0]
            psrr[0] = (b + 1) % 8
            return b

        def ps_pair():
            if psrr[0] % 2:
                psrr[0] = (psrr[0] + 1) % 8
            b = psrr[0]
            psrr[0] = (b + 2) % 8
            return b

        def psf(b, n=512, off=0):
            return ps[:, b * 512 + off: b * 512 + off + n]

        def psb(b, n, off=0):
            nb = (off + n + 1023) // 1024
            v = ps[:, b * 512: (b + nb) * 512].bitcast(BF16)
            return v[:, off: off + n]

        ident = cst[:, C_ID:C_ID + 128]
        identb = cbf[:, 0:128]
        Ub, Lmb, oneb, Usb, Lmsb, onesb, zerob = (cbf[:, i * 128:(i + 1) * 128] for i in range(1, 8))
        seqm = cst[:, C_SEQ:C_SEQ + 16]
        A_b = misc[:, 0:32]
        D_b = misc[:, 32:64]
        hbg = tmpA

        wq = []
        wstate = {"issued": 0, "used": 0}

        def w_enqueue(src, kk, n):
            wq.append((src, kk, n))

        wfree = list(range(NSLOT))
        wslot_of = {}

        def w_prefetch():
            while wfree and wstate["issued"] < len(wq):
                i = wstate["issued"]
                src, kk, n = wq[i]
                sl = wfree.pop(0)
                wslot_of[i] = sl
                dst = wslot[sl][:, 0:kk * n].rearrange("p (k n) -> p k n", k=kk)
                S.dma("pool", C("dma_start", out=dst, in_=src), writes=[B_wslot[sl]])
                wstate["issued"] += 1

        w_cur = [None]
        w_prev = [None]

        def w_next(kk, n, hold=False, lag=True):
            if w_prev[0] is not None:
                w_release(w_prev[0])
            w_prev[0] = w_cur[0]
            w_cur[0] = None
            if not lag and w_prev[0] is not None:
                w_release(w_prev[0])
                w_prev[0] = None
            j = wstate["used"]
            wstate["used"] += 1
            w_prefetch()
            assert j in wslot_of, ("weight stream stalled: no free slot", j)
            sl = wslot_of[j]
            assert wq[j][1] == kk and wq[j][2] == n, (j, wq[j][1:], kk, n)
            if not hold:
                w_cur[0] = j
            return wslot[sl][:, 0:kk * n].rearrange("p (k n) -> p k n", k=kk), B_wslot[sl], j

        def w_release(j):
            wfree.append(wslot_of[j])
            w_prefetch()

        def src_rows(d_ap, c0, n):
            return d_ap[:, c0:c0 + n].rearrange("(k p) n -> p k n", p=128)

        for gi in range(6):
            w_enqueue(src_rows(wada_d, gi * 512, 512), 8, 512)
        ST_LIST = [(0, 4, False), (512, 4, False), (1024, 4, False), (1536, 4, False), (2048, 1, True)]
        for _ in ST_LIST:
            w_enqueue(src_rows(win_d, IN_U, 512), 8, 512)
            w_enqueue(src_rows(win_d, IN_Z, 512), 8, 512)
            for i in range(6):
                w_enqueue(src_rows(win_d, IN_XBC + i * 512, 512), 8, 512)
            for i in range(4):
                w_enqueue(src_rows(win_d, IN_ZS + i * 512, 512), 8, 512)
            w_enqueue(src_rows(win_d, IN_DT, 32), 8, 32)
            for i in range(4):
                w_enqueue(src_rows(win_d, IN_G + i * 512, 512), 8, 512)
            w_enqueue(src_rows(wpo_d, 0, 1024), 4, 1024)
            for i in range(4):
                w_enqueue(src_rows(wssm_d, i * 256, 256), 16, 256)
            for i in range(2):
                w_enqueue(src_rows(wo_d, i * 512, 512), 8, 512)

        S.dma("sp", C("dma_start", out=cst[:], in_=cst_d[:, :]), writes=[B_cst])
        S.dma("sp", C("dma_start", out=vfm[:], in_=vfm_d[:, :]), writes=[B_vfm])
        S.dma("sp", C("dma_start", out=vrow[:], in_=vrow_d[0:1, :].partition_broadcast(128)), writes=[B_vrow])
        S.dma("sp", C("dma_start", out=cTs[:], in_=cT_d[:, :].rearrange("(k p) b -> p k b", p=128)), writes=[B_cTs])
        S.dma("pool", C("dma_start", out=poolw[:], in_=poolw_d[:, :].rearrange("(g c) d -> c g d", c=128)), writes=[B_poolw])
        w_prefetch()
        S.dma("pool", C("dma_start", out=cbf[:], in_=msk_d[:, :]), writes=[B_cbf])
        S.op("pool", C("memset", cm05[:], -0.5), writes=[B_cm05])
        S.op("pool", C("memset", hst[:], 0.0), writes=B_hst)
        S.op("pool", C("memset", hstb[:], 0.0), writes=B_hstb)
        S.op("pool", C("memset", chist[:], 0.0), writes=B_chist)
        S.op("pool", C("memset", uX[:], 0.0), writes=B_uX)
        S.op("act", C("activation", out=A_b, in_=vrow[:, VR_ALOG:VR_ALOG + 32], func=AF.Exp), reads=[B_vrow], writes=[B_misc])
        S.op("dve", C("tensor_scalar", out=A_b, in0=A_b, scalar1=-1.0, scalar2=None, op0=ALU.mult), reads=[B_misc], writes=[B_misc])
        S.op("dve", C("tensor_copy", out=D_b, in_=vrow[:, VR_DSKIP:VR_DSKIP + 32]), reads=[B_vrow], writes=[B_misc])
        S.op("act", C("activation", out=siluc[:], in_=cTs[:], func=AF.Silu), reads=[B_cTs], writes=[B_siluc])
        cexp_p = xD[:, 0:1024].rearrange("p (k t) -> p k t", k=8)
        cexp_s = xD[:, 1024:2048].rearrange("p (k t) -> p k t", k=8)
        S.op("dve", C("tensor_copy", out=cexp_p, in_=bc(siluc[:, :, 0:1], [128, 8, 128])), reads=[B_siluc], writes=[B_xD])
        for k in range(8):
            S.op("dve", C("tensor_copy",
                out=cexp_s[:, k, :].rearrange("p (t s) -> p t s", s=16),
                in_=bc(siluc[:, k, 1:17].unsqueeze(1), [128, 8, 16])), reads=[B_siluc], writes=[B_xD])
        bmod = ps_one()
        for gi in range(4):
            wv, wb, wh = w_next(8, 512)
            for jb in range(4):
                j = gi * 4 + jb
                for k in range(8):
                    S.op("pe", C("matmul",
                        psf(bmod, 17, j * 17), lhsT=wv[:, k, jb * 128:(jb + 1) * 128], rhs=siluc[:, k, :],
                        start=(k == 0), stop=(k == 7)), reads=[wb, B_siluc], writes=[B_ps[bmod]])
        modv = psf(bmod, 272).rearrange("p (j b) -> p j b", j=16)
        S.op("dve", C("tensor_tensor", out=St[:], in0=modv[:, 0:8, :], in1=bc(vfm[:, VF_BSHIFT:VF_BSHIFT + 8].unsqueeze(2), [128, 8, 17]), op=ALU.add),
             reads=[B_ps[bmod], B_vfm], writes=[B_St])
        S.op("dve", C("tensor_tensor", out=Gt[:], in0=modv[:, 8:16, :], in1=bc(vfm[:, VF_BSCALE:VF_BSCALE + 8].unsqueeze(2), [128, 8, 17]), op=ALU.add),
             reads=[B_ps[bmod], B_vfm], writes=[B_Gt])
        S.op("dve", C("scalar_tensor_tensor", out=Gt[:], in0=Gt[:], scalar=1.0, in1=bc(vfm[:, VF_NORMG:VF_NORMG + 8].unsqueeze(2), [128, 8, 17]), op0=ALU.add, op1=ALU.mult),
             reads=[B_Gt, B_vfm], writes=[B_Gt])
        S.dma("sp", C("dma_start", out=hbg[:], in_=bgate_d[0:1, :].partition_broadcast(128)), writes=[B_tmpA])
        S.op("dve", C("tensor_scalar", out=hbg[:], in0=hbg[:], scalar1=0.5, scalar2=None, op0=ALU.mult), reads=[B_tmpA], writes=[B_tmpA])
        for hf in range(2):
            wv, wb, wh = w_next(8, 512)
            for (cexp, gt, Bg) in ((cexp_p, gate_p, B_gate_p), (cexp_s, gate_s, B_gate_s)):
                b = ps_one()
                for k in range(8):
                    S.op("pe", C("matmul", psf(b), lhsT=cexp[:, k, :], rhs=wv[:, k, :], start=(k == 0), stop=(k == 7)),
                         reads=[B_xD, wb], writes=[B_ps[b]])
                S.op("dve", C("scalar_tensor_tensor", out=gt[:, hf * 512:(hf + 1) * 512], in0=psf(b), scalar=0.5, in1=hbg[:, hf * 512:(hf + 1) * 512], op0=ALU.mult, op1=ALU.add),
                     reads=[B_ps[b], B_tmpA], writes=[Bg])

        def rstd_from_ss(col, n, Bs=None):
            Bs = Bs or B_stat
            S.op("dve", C("tensor_scalar", out=stat[:, col + 1:col + 2], in0=stat[:, col:col + 1], scalar1=1.0 / n, scalar2=EPS, op0=ALU.mult, op1=ALU.add),
                 reads=[Bs], writes=[Bs])
            S.op("pool", C("tensor_tensor", out=stat[:, col + 1:col + 2], in0=stat[:, col + 1:col + 2], in1=cm05[:], op=ALU.pow),
                 reads=[Bs, B_cm05], writes=[Bs])

        xbrr = [0]

        def x_load(r0):
            i = xbrr[0]
            xbrr[0] ^= 1
            S.dma("sp", C("dma_start", out=xb[i][:], in_=x_d[r0:r0 + 128, :]), writes=[B_xb[i]])
            return i

        def ckpt(tag):
            if K_STOP == tag:
                raise _Stop()

        def main_loop():
          ckpt('P')
          def run_il(*gens):
              gens = list(gens)
              while gens:
                  for gen in list(gens):
                      try:
                          next(gen)
                      except StopIteration:
                          gens.remove(gen)

          def phaseA(tok0, NCH, SAMPLE):
              junk = dec[:, 0:1024]
              st = {}

              def sa(c):
                  xi = x_load(tok0 + c * 128)
                  st[c] = xi
                  S.op("act", C("activation", out=junk, in_=xb[xi][:], func=AF.Square, accum_out=stat[:, 2 * (c % 2):2 * (c % 2) + 1]),
                       reads=[B_xb[xi]], writes=[B_dec[0], B_statA[c % 2]])
                  rstd_from_ss(2 * (c % 2), 1024, B_statA[c % 2])

              def sb_(c):
                  xi = st[c]
                  S.op("act", C("activation", out=tmpA[:], in_=xb[xi][:], func=AF.Copy, scale=stat[:, 2 * (c % 2) + 1:2 * (c % 2) + 2]),
                       reads=[B_xb[xi], B_statA[c % 2]], writes=[B_tmpA])
                  bp = ps_pair()
                  st[("bp", c)] = bp
                  for k in range(8):
                      S.op("pe", C("transpose", out=ps[:, bp * 512 + k * 128: bp * 512 + (k + 1) * 128], in_=tmpA[:, k * 128:(k + 1) * 128], identity=ident),
                           reads=[B_tmpA, B_cst], writes=[B_ps[bp + k // 4]])

              def sd(c):
                  bp = st[("bp", c)]
                  if not SAMPLE:
                      for k in range(8):
                          src = ps[:, bp * 512 + k * 128: bp * 512 + (k + 1) * 128]
                          dst = hT[:, k, c * 128:(c + 1) * 128]
                          if k < 4:
                              S.op("dve", C("tensor_scalar", out=dst, in0=src, scalar1=Gt[:, k, 0:1], scalar2=St[:, k, 0:1], op0=ALU.mult, op1=ALU.add),
                                   reads=[B_ps[bp + k // 4], B_Gt, B_St], writes=[B_hT[c][k]])
                          else:
                              S.op("act", C("activation", out=dst, in_=src, func=AF.Identity, scale=Gt[:, k, 0:1], bias=St[:, k, 0:1]),
                                   reads=[B_ps[bp + k // 4], B_Gt, B_St], writes=[B_hT[c][k]])
                  else:
                      for k in range(8):
                          src = ps[:, bp * 512 + k * 128: bp * 512 + (k + 1) * 128].rearrange("p (t s) -> p t s", s=16)
                          dst = hT[:, k, 0:128].rearrange("p (t s) -> p t s", s=16)
                          tv = tmpg[0][:, 0:128].rearrange("p (t s) -> p t s", s=16)
                          S.op("dve", C("tensor_tensor", out=tv, in0=src, in1=bc(Gt[:, k, 1:17].unsqueeze(1), [128, 8, 16]), op=ALU.mult),
                               reads=[B_ps[bp + k // 4], B_Gt], writes=[B_tmpg[0]])
                          S.op("dve", C("tensor_tensor", out=dst, in0=tv, in1=bc(St[:, k, 1:17].unsqueeze(1), [128, 8, 16]), op=ALU.add),
                               reads=[B_tmpg[0], B_St], writes=[B_hT[0][k]])

              sa(0)
              yield
              for c in range(NCH):
                  sb_(c)
                  yield
                  if c + 1 < NCH:
                      sa(c + 1)
                      yield
                  sd(c)
                  yield

          for sti, (tok0, NCH, SAMPLE) in enumerate(ST_LIST):
              T = NCH * 128
              LAST_PROMPT = (tok0 == 1536)
              NS = 16 if SAMPLE else 1
              L = 8 if SAMPLE else T
              gate_t, Bgate = (gate_s, B_gate_s) if SAMPLE else (gate_p, B_gate_p)
              mU, mLm, mOne = (Usb, Lmsb, onesb) if SAMPLE else (Ub, Lmb, oneb)

              if sti == 0:
                  run_il(phaseA(tok0, NCH, SAMPLE))

              def hT_reads(k):
                  return [B_hT[c][k] for c in range(NCH)]

              def inproj_block(wv, wb, jb, b):
                  for k in range(8):
                      S.op("pe", C("matmul", psf(b, T), lhsT=wv[:, k, jb * 128:(jb + 1) * 128], rhs=hT[:, k, 0:T], start=(k == 0), stop=(k == 7)),
                           reads=[wb] + hT_reads(k), writes=[B_ps[b]])

              ckpt('%d:A' % sti)
              TS = 16 if SAMPLE else 1
              if SAMPLE:
                  for g in range(4):
                      S.dma("sp", C("dma_start", out=uX[:, g, 0:240], in_=spT_d[:, g * 240:(g + 1) * 240]), writes=[B_uX[g]])
              wv, wb, wh = w_next(8, 512)
              for g in range(4):
                  b = ps_one()
                  inproj_block(wv, wb, g, b)
                  S.op("act", C("activation", out=uX[:, g, 15 * TS:15 * TS + T], in_=psf(b, T), func=AF.Copy),
                       reads=[B_ps[b]], writes=[B_uX[g]])
              wv, wb, wh = w_next(8, 512)
              for g in range(4):
                  b = ps_one()
                  inproj_block(wv, wb, g, b)
                  S.op("act", C("activation", out=siluz[:, g, 0:T], in_=psf(b, T), func=AF.Silu),
                       reads=[B_ps[b]], writes=[B_siluz[g]])
              for g in range(4):
                  w = 2 << g
                  M = 15 + L
                  los = []
                  lo = 15
                  for lev in range(g, -1, -1):
                      los.append(lo)
                      lo -= (1 << lev)
                  los = los[::-1]
                  cur, Bcur = uX[:, g, :], B_uX[g]
                  for lev in range(g + 1):
                      sh = 1 << lev
                      lo_l = los[lev]
                      dstv, Bd = (Sa, B_Sa) if lev % 2 == 0 else (Sb_, B_Sb)
                      S.op("pool", C("tensor_tensor", out=dstv[:, lo_l * TS:M * TS], in0=cur[:, lo_l * TS:M * TS], in1=cur[:, (lo_l - sh) * TS:(M - sh) * TS], op=ALU.add),
                           reads=[Bcur], writes=[Bd])
                      cur, Bcur = dstv, Bd
                  S.op("dve", C("scalar_tensor_tensor", out=dTt[:, g, 0:T], in0=cur[:, 15 * TS:15 * TS + T], scalar=1.0 / w, in1=uX[:, g, 15 * TS:15 * TS + T], op0=ALU.mult, op1=ALU.subtract),
                       reads=[Bcur, B_uX[g]], writes=[B_dT[g]])
                  if tok0 == 0 and not SAMPLE:
                      S.op("dve", C("tensor_tensor", out=tmpg[0][:, 0:16], in0=cur[:, 15:31], in1=cst[:, C_INVC + g * 16:C_INVC + (g + 1) * 16], op=ALU.mult),
                           reads=[Bcur, B_cst], writes=[B_tmpg[0]])
                      S.op("dve", C("tensor_tensor", out=dTt[:, g, 0:16], in0=tmpg[0][:, 0:16], in1=uX[:, g, 15:31], op=ALU.subtract),
                           reads=[B_tmpg[0], B_uX[g]], writes=[B_dT[g]])
                  if LAST_PROMPT:
                      S.dma("sp", C("dma_start", out=npp_d[:, g * 15:(g + 1) * 15], in_=uX[:, g, T:T + 15]), reads=[B_uX[g]], final=True)
                  elif SAMPLE:
                      S.dma("sp", C("dma_start", out=nps_d[:, g * 240:(g + 1) * 240], in_=uX[:, g, 8 * 16:23 * 16]), reads=[B_uX[g]], final=True)
                  if not SAMPLE and not LAST_PROMPT:
                      S.op("pool", C("tensor_copy", out=uX[:, g, 0:15], in_=uX[:, g, T:T + 15]), reads=[B_uX[g]], writes=[B_uX[g]])

              ckpt('%d:B1' % sti)
              def diag_build(blk):
                  di = blk % 2
                  S.op("pool", C("tensor_tensor", out=dg[di][:], in0=bc(identb.unsqueeze(1), [128, 4, 128]), in1=bc(vfm[:, VF_CONVW + blk * 4:VF_CONVW + blk * 4 + 4].unsqueeze(2), [128, 4, 128]), op=ALU.mult),
                       reads=[B_cbf, B_vfm], writes=[B_dg[di]])

              def conv_block(blk, ri, braw):
                  di = blk % 2
                  b2 = ps_one()
                  for k in range(4):
                      S.op("pe", C("matmul", psf(b2, T), lhsT=dg[di][:, k, :], rhs=raw[ri][:, k * TS:k * TS + T], start=(k == 0), stop=(k == 3)),
                           reads=[B_dg[di], B_raw[ri]], writes=[B_ps[b2]])
                  S.op("act", C("activation", out=xcT[:, blk, 0:T], in_=psf(b2, T), func=AF.Silu, bias=vfm[:, VF_CONVB + blk:VF_CONVB + blk + 1]),
                       reads=[B_ps[b2], B_vfm], writes=[B_xcT[blk]])

              ncf2 = hst[:, 0:1152].rearrange("p (b m) -> p b m", b=24)
              if SAMPLE:
                  S.dma("sp", C("dma_start", out=hst[:, 0:1152], in_=scT_d[:, :]), writes=B_hst)
              pend = None
              for gi in range(6):
                  wv, wb, wh = w_next(8, 512)
                  for jb in range(4):
                      blk = gi * 4 + jb
                      ri = blk % 2
                      diag_build(blk)
                      if SAMPLE:
                          S.op("dve", C("tensor_copy", out=raw[ri][:, 0:48], in_=ncf2[:, blk, :]), reads=[B_ncf[blk]], writes=[B_raw[ri]])
                      else:
                          S.op("dve", C("tensor_copy", out=raw[ri][:, 0:3], in_=chist[:, blk, :]), reads=[B_chist[blk]], writes=[B_raw[ri]])
                      b = ps_one()
                      inproj_block(wv, wb, jb, b)
                      S.op("dve", C("tensor_copy", out=raw[ri][:, 3 * TS:3 * TS + T], in_=psf(b, T)), reads=[B_ps[b]], writes=[B_raw[ri]])
                      if LAST_PROMPT:
                          S.op("dve", C("tensor_copy", out=ncpt[:, blk, :], in_=psf(b, 3, T - 3)), reads=[B_ps[b]], writes=[B_ncpt])
                      elif SAMPLE:
                          S.op("dve", C("tensor_copy", out=ncf2[:, blk, :], in_=psf(b, 48, 80)), reads=[B_ps[b]], writes=[B_ncf[blk]])
                      else:
                          S.op("dve", C("tensor_copy", out=chist[:, blk, :], in_=psf(b, 3, T - 3)), reads=[B_ps[b]], writes=[B_chist[blk]])
                      if pend is not None:
                          conv_block(*pend)
                      pend = (blk, ri, b)
              conv_block(*pend)
              if SAMPLE:
                  S.dma("sp", C("dma_start", out=ncs_d[:, :], in_=hst[:, 0:1152]), reads=B_hst, final=True)
              if LAST_PROMPT:
                  S.dma("sp", C("dma_start", out=ncp_d[:, :].rearrange("p (b r) -> p b r", b=24), in_=ncpt[:]), reads=[B_ncpt], final=True)

              for g in range(4):
                  b = ps_one()
                  S.op("pe", C("matmul", psf(b, T), lhsT=poolw[:, g, :], rhs=dTt[:, g, 0:T], start=True, stop=True),
                       reads=[B_poolw, B_dT[g]], writes=[B_ps[b]])
                  S.op("dve", C("scalar_tensor_tensor", out=pzT[:, g, 0:T], in0=psf(b, T), scalar=vfm[:, VF_PSCALE + g:VF_PSCALE + g + 1], in1=siluz[:, g, 0:T], op0=ALU.mult, op1=ALU.mult),
                       reads=[B_ps[b], B_vfm, B_siluz[g]], writes=[B_siluz[g]])
              ckpt('%d:C' % sti)
              for zg in range(4):
                  wv, wb, wh = w_next(8, 512)
                  for c in range(NCH):
                      b = ps_one()
                      for k in range(8):
                          S.op("pe", C("matmul", psf(b), lhsT=hT[:, k, c * 128:(c + 1) * 128], rhs=wv[:, k, :], start=(k == 0), stop=(k == 7)),
                               reads=[wb, B_hT[c][k]], writes=[B_ps[b]])
                      S.op("act", C("activation", out=sz[:, c, zg * 512:(zg + 1) * 512], in_=psf(b), func=AF.Silu),
                           reads=[B_ps[b]], writes=[B_sz[c]])
              wv, wb, wh = w_next(8, 32)
              for c in range(NCH):
                  b = ps_one()
                  for k in range(8):
                      S.op("pe", C("matmul", psf(b, 32), lhsT=hT[:, k, c * 128:(c + 1) * 128], rhs=wv[:, k, :], start=(k == 0), stop=(k == 7)),
                           reads=[wb, B_hT[c][k]], writes=[B_ps[b]])
                  S.op("dve", C("tensor_tensor", out=vdt[:, c, :], in0=psf(b, 32), in1=vrow[:, VR_DTB:VR_DTB + 32], op=ALU.add),
                       reads=[B_ps[b], B_vrow], writes=[B_vdt[c]])

              ckpt('%d:B3' % sti)
              ybv = [yb[:].rearrange("p (g q) -> p g q", g=4), uX[:, :, 15:527]]
              B_ybv = [B_yb, B_uX]
              expacs, exprem, explast = ex[:, 0:32], ex[:, 32:64], ex[:, 64:96]
              xdt3 = xdt[:].rearrange("p (h q) -> p h q", h=32)
              xw3 = xw.rearrange("p (h q) -> p h q", h=32)

              def ystate_evac(b, g, yv, Byv):
                  S.op("dve", C("tensor_tensor", out=yv[:, g, :].rearrange("p (h q) -> p h q", h=8), in0=psf(b).rearrange("p (h q) -> p h q", h=8),
                                 in1=bc(expacs[:, g * 8:(g + 1) * 8].unsqueeze(2), [128, 8, 64]), op=ALU.mult),
                       reads=[B_ps[b], B_ex], writes=[Byv[g]])

              def r_build(g):
                  ri = g % 2
                  S.op("pool", C("tensor_tensor", out=Rg[ri][:].rearrange("p (h i) -> p h i", h=8), in0=bc(aa[:, g * 8:(g + 1) * 8].unsqueeze(2), [128, 8, 128]),
                                                                    in1=bc(mU.unsqueeze(1), [128, 8, 128]), op=ALU.mult),
                       reads=[B_aa, B_cbf], writes=[B_Rg[ri]])

              def stage1(c):
                  cs = slice(c * 128, (c + 1) * 128)
                  yv, Byv = ybv[c % 2], B_ybv[c % 2]
                  S.op("act", C("activation", out=dtt[:], in_=vdt[:, c, :], func=AF.Exp), reads=[B_vdt[c]], writes=[B_dtt])
                  S.op("act", C("activation", out=dtt[:], in_=dtt[:], func=AF.Ln, bias=1.0), reads=[B_dtt], writes=[B_dtt])
                  S.op("dve", C("tensor_tensor", out=aa[:], in0=dtt[:], in1=A_b, op=ALU.mult), reads=[B_dtt, B_misc], writes=[B_aa])
                  S.op("dve", C("tensor_copy", out=aab[:], in_=aa[:]), reads=[B_aa], writes=[B_aab])
                  bsm = ps_one()
                  for i, m in enumerate((mU, mLm, mOne)):
                      S.op("pe", C("matmul", psf(bsm, 32, i * 32), lhsT=m, rhs=aab[:], start=True, stop=True),
                           reads=[B_cbf, B_aab], writes=[B_ps[bsm]])
                  S.op("act", C("activation", out=ex[:], in_=psf(bsm, 96), func=AF.Exp), reads=[B_ps[bsm]], writes=[B_ex])
                  for g in range(2):
                      r_build(g)
                  yield
                  bx = ps_pair()
                  for blk in range(16):
                      S.op("pe", C("transpose", out=psb(bx, 128, blk * 128), in_=xcT[:, blk, cs], identity=identb),
                           reads=[B_xcT[blk], B_cbf], writes=[B_ps[bx + blk // 8]])
                  bB = ps_one()
                  for g in range(4):
                      S.op("pe", C("transpose", out=psb(bB, 128, g * 128), in_=xcT[:, 16 + g, cs], identity=identb),
                           reads=[B_xcT[16 + g], B_cbf], writes=[B_ps[bB]])
                  yield
                  xT3 = psb(bx, 2048).rearrange("p (h q) -> p h q", h=32)
                  S.op("dve", C("tensor_tensor", out=xdt3, in0=xT3, in1=bc(dtt[:].unsqueeze(2), [128, 32, 64]), op=ALU.mult),
                       reads=[B_ps[bx], B_ps[bx + 1], B_dtt], writes=[B_xdt])
                  S.op("act", C("activation", out=Btm[:], in_=psb(bB, 512), func=AF.Copy), reads=[B_ps[bB]], writes=[B_Btm])
                  yield
                  S.op("pool", C("tensor_tensor", out=xw3, in0=xdt3, in1=bc(exprem.unsqueeze(2), [128, 32, 64]), op=ALU.mult),
                       reads=[B_xdt, B_ex], writes=B_xw)
                  S.op("dve", C("tensor_tensor", out=xD[:].rearrange("p (h q) -> p h q", h=32), in0=xT3, in1=bc(D_b.unsqueeze(2), [128, 32, 64]), op=ALU.mult),
                       reads=[B_ps[bx], B_ps[bx + 1], B_misc], writes=[B_xD])
                  yield
                  bcb = ps_one()
                  for g in range(4):
                      S.op("pe", C("matmul", psf(bcb, 128, g * 128), lhsT=xcT[:, 16 + g, cs], rhs=xcT[:, 20 + g, cs], start=True, stop=True),
                           reads=[B_xcT[16 + g], B_xcT[20 + g]], writes=[B_ps[bcb]])
                  S.op("dve", C("tensor_tensor", out=cbTm[:], in0=psf(bcb).rearrange("p (g i) -> p g i", g=4), in1=bc(mU.unsqueeze(1), [128, 4, 128]), op=ALU.mult),
                       reads=[B_ps[bcb], B_cbf], writes=[B_cbTm])
                  yield
                  if not SAMPLE:
                      for g in range(4):
                          b = ps_one()
                          S.op("pe", C("matmul", psf(b), lhsT=xcT[:, 20 + g, cs], rhs=hstb[:, g * 512:(g + 1) * 512], start=True, stop=True),
                               reads=[B_xcT[20 + g], B_hstb[g]], writes=[B_ps[b]])
                          ystate_evac(b, g, yv, Byv)
                      yield
                      S.op("pool", C("tensor_tensor", out=hst[:].rearrange("p (h q) -> p h q", h=32), in0=hst[:].rearrange("p (h q) -> p h q", h=32),
                                     in1=bc(explast.unsqueeze(2), [128, 32, 64]), op=ALU.mult), reads=B_hst + [B_ex], writes=B_hst)
                      yield
                      for g in range(4):
                          b = ps_one()
                          S.op("pe", C("matmul", psf(b), lhsT=Btm[:, g * 128:(g + 1) * 128], rhs=xw[:, g * 512:(g + 1) * 512], start=True, stop=True),
                               reads=[B_Btm] + B_xw, writes=[B_ps[b]])
                          S.op("dve", C("tensor_tensor", out=hst[:, g * 512:(g + 1) * 512], in0=hst[:, g * 512:(g + 1) * 512], in1=psf(b), op=ALU.add),
                               reads=[B_hst[g], B_ps[b]], writes=[B_hst[g]])
                          S.op("act", C("activation", out=hstb[:, g * 512:(g + 1) * 512], in_=hst[:, g * 512:(g + 1) * 512], func=AF.Copy),
                               reads=[B_hst[g]], writes=[B_hstb[g]])
                      if LAST_PROMPT and c == NCH - 1:
                          S.dma("sp", C("dma_start", out=nsp_d[:, :], in_=hst[:]), reads=B_hst, final=True)
                  else:
                      S.op("pool", C("tensor_tensor", out=am[:], in0=bc(aa[:].unsqueeze(1), [128, 16, 32]), in1=bc(seqm.unsqueeze(2), [128, 16, 32]), op=ALU.mult),
                           reads=[B_aa, B_cst], writes=[B_am])
                      bl = ps_one()
                      S.op("pe", C("matmul", psf(bl), lhsT=oneb, rhs=am[:].rearrange("p s h -> p (s h)"), start=True, stop=True), reads=[B_cbf, B_am], writes=[B_ps[bl]])
                      S.op("act", C("activation", out=exl[:], in_=psf(bl), func=AF.Exp), reads=[B_ps[bl]], writes=[B_exl])
                      exl3 = exl[:].rearrange("p (s h) -> p s h", s=16)
                      NFB = 6
                      fbuf = [hsf[0], hsf[1]] + [uX[:, j, 15:527] for j in range(4)]
                      B_fbuf = [B_hsf[0], B_hsf[1]] + [[B_uX[j]] for j in range(4)]
                      bbuf = [hsb[0][:], hsb[1][:]] + [dTt[:, j, :] for j in range(4)]
                      B_bbuf = [[B_hsb[0]], [B_hsb[1]]] + [[B_dT[j]] for j in range(4)]
                      items = [(g, s) for g in range(4) for s in range(16)]

                      def ld(it):
                          g, s = items[it]
                          S.dma("sp", C("dma_start", out=fbuf[it % NFB], in_=ssT_d[s, :, g * 512:(g + 1) * 512]), writes=B_fbuf[it % NFB])
                          S.dma("pool", C("dma_start", out=bbuf[it % NFB], in_=ssT_d[s, :, g * 512:(g + 1) * 512]), writes=B_bbuf[it % NFB])

                      for it in range(NFB - 1):
                          ld(it)
                      bS = None
                      for it, (g, s) in enumerate(items):
                          if s == 0:
                              S.op("pool", C("memset", dec[:], 0.0), writes=B_dec)
                              S.op("pool", C("tensor_copy", out=dec[:, 0:16 * 129].rearrange("p (s m) -> p s m", m=129)[:, :, 0:113:16], in_=xcT[:, 20 + g, 0:128].rearrange("p (t s) -> p s t", s=16)),
                                   reads=[B_xcT[20 + g]], writes=B_dec)
                              S.op("pool", C("tensor_tensor", out=att[:].rearrange("p (s n) -> p s n", s=16), in0=bc(Btm[:, g * 128:(g + 1) * 128].unsqueeze(1), [128, 16, 128]),
                                                                         in1=bc(seqm.unsqueeze(2), [128, 16, 128]), op=ALU.mult),
                                   reads=[B_Btm, B_cst], writes=B_att)
                              bS = ps_one()
                          if it + NFB - 1 < len(items):
                              ld(it + NFB - 1)
                          fb_, Bf = fbuf[it % NFB], B_fbuf[it % NFB]
                          bb_, Bb = bbuf[it % NFB], B_bbuf[it % NFB]
                          S.op("pe", C("matmul", psf(bS), lhsT=dec[:, s * 128:(s + 1) * 128], rhs=bb_, start=(s == 0), stop=(s == 15)),
                               reads=B_dec + Bb, writes=[B_ps[bS]])
                          bH = ps_one()
                          if bH == bS:
                              bH = ps_one()
                          S.op("pe", C("matmul", psf(bH), lhsT=att[:, s * 128:(s + 1) * 128], rhs=xw[:, g * 512:(g + 1) * 512], start=True, stop=True),
                               reads=B_att + B_xw, writes=[B_ps[bH]])
                          S.op("pool" if it % 2 == 0 else "dve", C("tensor_tensor", out=fb_.rearrange("p (h q) -> p h q", h=8), in0=fb_.rearrange("p (h q) -> p h q", h=8),
                                                                               in1=bc(exl3[:, s, g * 8:(g + 1) * 8].unsqueeze(2), [128, 8, 64]), op=ALU.mult),
                               reads=Bf + [B_exl], writes=Bf)
                          S.op("dve", C("tensor_tensor", out=fb_, in0=fb_, in1=psf(bH), op=ALU.add),
                               reads=Bf + [B_ps[bH]], writes=Bf)
                          S.dma("act", C("dma_start", out=nss_d[s, :, g * 512:(g + 1) * 512], in_=fb_), reads=Bf, final=True)
                          if s == 15:
                              ystate_evac(bS, g, yv, Byv)

              def stage2(c):
                  yv, Byv = ybv[c % 2], B_ybv[c % 2]
                  def pre(g):
                      ri = g % 2
                      if g >= 2:
                          r_build(g)
                      bs = ps_pair()
                      for q in range(2):
                          S.op("pe", C("matmul", psf(bs + q), lhsT=mLm, rhs=Rg[ri][:, q * 512:(q + 1) * 512], start=True, stop=True),
                               reads=[B_cbf, B_Rg[ri]], writes=[B_ps[bs + q]])
                      dv = dec[:, ri * 1024:(ri + 1) * 1024]
                      S.op("act", C("activation", out=dv, in_=ps[:, bs * 512:(bs + 2) * 512], func=AF.Exp),
                           reads=[B_ps[bs], B_ps[bs + 1]], writes=[B_dec[ri]])

                  def post(g):
                      ri = g % 2
                      dv = dec[:, ri * 1024:(ri + 1) * 1024]
                      av = att[:, ri * 1024:(ri + 1) * 1024].rearrange("p (h i) -> p h i", h=8)
                      S.op("dve", C("tensor_tensor", out=av, in0=dv.rearrange("p (h i) -> p h i", h=8), in1=bc(cbTm[:, g, :].unsqueeze(1), [128, 8, 128]), op=ALU.mult),
                           reads=[B_dec[ri], B_cbTm], writes=[B_att[ri]])
                      by = ps_one()
                      S.op("pe", C("matmul", psf(by), lhsT=zerob, rhs=xdt[:, g * 512:(g + 1) * 512], start=True, stop=False),
                           reads=[B_cbf, B_xdt], writes=[B_ps[by]])
                      for hh in range(8):
                          S.op("pe", C("matmul", psf(by, 64, hh * 64), lhsT=av[:, hh, :], rhs=xdt[:, (g * 8 + hh) * 64:(g * 8 + hh + 1) * 64], start=False, stop=False),
                               reads=[B_att[ri], B_xdt], writes=[B_ps[by]])
                      S.op("pe", C("matmul", psf(by), lhsT=identb, rhs=xD[:, g * 512:(g + 1) * 512], start=False, stop=True),
                           reads=[B_cbf, B_xD], writes=[B_ps[by]])
                      S.op("dve", C("tensor_tensor", out=yv[:, g, :], in0=yv[:, g, :], in1=psf(by), op=ALU.add),
                           reads=[Byv[g], B_ps[by]], writes=[Byv[g]])

                  pre(0)
                  yield
                  pre(1)
                  yield
                  post(0)
                  yield
                  pre(2)
                  yield
                  post(1)
                  yield
                  pre(3)
                  yield
                  post(2)
                  yield
                  post(3)

              def stage3(c):
                  cs = slice(c * 128, (c + 1) * 128)
                  yv, Byv = ybv[c % 2], B_ybv[c % 2]
                  if DEBUG and sti == 0 and c == 0:
                      S.dma("sp", C("dma_start", out=dbg["d_yb"][:, :].rearrange("p (g q) -> p g q", g=4), in_=yv), reads=Byv, final=True)
                  S.op("dve", C("tensor_tensor", out=yv, in0=yv, in1=sz[:, c, :].rearrange("p (g q) -> p g q", g=4), op=ALU.mult), reads=Byv + [B_sz[c]], writes=Byv)
                  yield
                  S.op("act", C("activation", out=tmpA[:].rearrange("p (g q) -> p g q", g=2), in_=yv[:, 0:2, :], func=AF.Square, accum_out=stat[:, 4:5]), reads=Byv, writes=[B_tmpA, B_stat])
                  S.op("act", C("activation", out=tmpA[:].rearrange("p (g q) -> p g q", g=2), in_=yv[:, 2:4, :], func=AF.Square, accum_out=stat[:, 5:6]), reads=Byv, writes=[B_tmpA, B_stat])
                  yield
                  S.op("dve", C("tensor_tensor", out=stat[:, 6:7], in0=stat[:, 4:5], in1=stat[:, 5:6], op=ALU.add), reads=[B_stat], writes=[B_stat])
                  S.op("dve", C("tensor_scalar", out=stat[:, 7:8], in0=stat[:, 6:7], scalar1=1.0 / 2048, scalar2=EPS, op0=ALU.mult, op1=ALU.add), reads=[B_stat], writes=[B_stat])
                  S.op("act", C("activation", out=stat[:, 7:8], in_=stat[:, 7:8], func=AF.Ln), reads=[B_stat], writes=[B_stat])
                  S.op("act", C("activation", out=stat[:, 7:8], in_=stat[:, 7:8], func=AF.Exp, scale=-0.5), reads=[B_stat], writes=[B_stat])
                  yield
                  S.op("act", C("activation", out=ynb[:].rearrange("p (g q) -> p g q", g=4), in_=yv, func=AF.Copy, scale=stat[:, 7:8]), reads=Byv + [B_stat], writes=B_ynb)
                  yield
                  by2 = ps_pair()
                  for kk in range(16):
                      S.op("pe", C("transpose", out=psb(by2, 128, kk * 128), in_=ynb[:, kk * 128:(kk + 1) * 128], identity=identb),
                           reads=B_ynb + [B_cbf], writes=[B_ps[by2 + kk // 8]])
                  yield
                  for hb in range(2):
                      S.op("dve", C("tensor_tensor", out=ynT[:, hb * 8:(hb + 1) * 8, cs], in0=psb(by2 + hb, 1024).rearrange("p (k t) -> p k t", k=8),
                                     in1=bc(vfm[:, VF_SNG + hb * 8:VF_SNG + (hb + 1) * 8].unsqueeze(2), [128, 8, 128]), op=ALU.mult),
                           reads=[B_ps[by2 + hb], B_vfm], writes=[B_ynT[c][hb]])

              run_il(stage1(0))
              run_il(stage2(0))
              for c in range(1, NCH):
                  run_il(stage1(c), stage3(c - 1))
                  run_il(stage2(c))
              run_il(stage3(NCH - 1))

              if DEBUG and sti == 0:
                  S.dma("pool", C("dma_start", out=dbg["d_ynT"][:, :], in_=ynT[:].rearrange("p k t -> p (k t)")), reads=[b for r in B_ynT for b in r], final=True)
                  S.dma("pool", C("dma_start", out=dbg["d_pzT"][:, :], in_=siluz[:].rearrange("p k t -> p (k t)")), reads=B_siluz, final=True)
                  S.dma("pool", C("dma_start", out=dbg["d_xc"][:, :], in_=xcT[:].rearrange("p k t -> p (k t)")), reads=B_xcT, final=True)
              ckpt('%d:D' % sti)
              th12 = sz[:].rearrange("p c n -> p (c n)")[:, 0:16 * T].rearrange("p (j t) -> p j t", j=16)
              for gg in range(4):
                  wv, wb, wh = w_next(8, 512)
                  for jb in range(4):
                      j = gg * 4 + jb
                      b = ps_one()
                      inproj_block(wv, wb, jb, b)
                      S.op("act", C("activation", out=th12[:, j, :], in_=psf(b, T), func=AF.Tanh, scale=0.5), reads=[B_ps[b]], writes=B_sz)
              def phaseEF():
                  for c in range(min(2, NCH)):
                      r0 = tok0 + c * 128
                      S.dma("sp", C("dma_start", out=outb[c % 2], in_=x_d[r0:r0 + 128, :]), writes=B_outb[c % 2])
                  wpv, wpb, wph = w_next(4, 1024, hold=True, lag=False)
                  for fb in range(8):
                      if fb % 2 == 0:
                          wsv, wsb, wsh = w_next(16, 256, lag=False)
                      bP = ps_one()
                      for kc in range(4):
                          S.op("pe", C("matmul", psf(bP, T), lhsT=wpv[:, kc, fb * 128:(fb + 1) * 128], rhs=pzT[:, kc, 0:T], start=(kc == 0), stop=(kc == 3)),
                               reads=[wpb, B_pzT[kc]], writes=[B_ps[bP]])
                      bS2 = ps_one()
                      for kc in range(16):
                          S.op("pe", C("matmul", psf(bS2, T), lhsT=wsv[:, kc, (fb % 2) * 128:(fb % 2 + 1) * 128], rhs=ynT[:, kc, 0:T], start=(kc == 0), stop=(kc == 15)),
                               reads=[wsb] + [B_ynT[cc][pp] for cc in range(NCH) for pp in range(2)], writes=[B_ps[bS2]])
                      S.op("dve", C("scalar_tensor_tensor", out=tmpg[0][:, 0:T], in0=th12[:, fb, :], scalar=1.0, in1=psf(bP, T), op0=ALU.add, op1=ALU.mult),
                           reads=B_sz + [B_ps[bP]], writes=[B_tmpg[0]])
                      S.op("dve", C("scalar_tensor_tensor", out=tmpg[1][:, 0:T], in0=th12[:, 8 + fb, :], scalar=1.0, in1=psf(bS2, T), op0=ALU.add, op1=ALU.mult),
                           reads=B_sz + [B_ps[bS2]], writes=[B_tmpg[1]])
                      S.op("pool", C("tensor_tensor", out=xcT[:, fb, 0:T], in0=tmpg[0][:, 0:T], in1=tmpg[1][:, 0:T], op=ALU.add),
                           reads=[B_tmpg[0], B_tmpg[1]], writes=[B_xcT[fb]])
                      yield
                  if DEBUG and sti == 0:
                      S.dma("pool", C("dma_start", out=dbg["d_mT"][:, :], in_=xcT[:, 0:8, :].rearrange("p k t -> p (k t)")), reads=B_xcT[0:8], final=True)
                      S.dma("pool", C("dma_start", out=dbg["d_th"][:, :], in_=sz[:].rearrange("p c n -> p (c n)")), reads=B_sz, final=True)
                  w_release(wph)
                  wov = []
                  for hf in range(2):
                      wov.append(w_next(8, 512, hold=True, lag=False))
                  fst = {}

                  def F1(c):
                      oi = c % 2
                      r0 = tok0 + c * 128
                      if c >= 2:
                          S.dma("sp", C("dma_start", out=outb[oi], in_=x_d[r0:r0 + 128, :]), writes=B_outb[oi])
                      for hf in range(2):
                          b = ps_one()
                          for k in range(8):
                              S.op("pe", C("matmul", psf(b), lhsT=xcT[:, k, c * 128:(c + 1) * 128], rhs=wov[hf][0][:, k, :], start=(k == 0), stop=(k == 7)),
                                   reads=[wov[hf][1], B_xcT[k]], writes=[B_ps[b]])
                          S.op("dve", C("tensor_tensor", out=tmpg[hf][:], in0=psf(b), in1=gate_t[:, hf * 512:(hf + 1) * 512], op=ALU.mult),
                               reads=[B_ps[b], Bgate], writes=[B_tmpg[hf]])
                          S.op("pool", C("tensor_tensor", out=outb[oi][:, hf * 512:(hf + 1) * 512], in0=outb[oi][:, hf * 512:(hf + 1) * 512], in1=tmpg[hf][:], op=ALU.add),
                               reads=B_outb[oi] + [B_tmpg[hf]], writes=B_outb[oi])

                  def F2(c):
                      oi = c % 2
                      S.op("act", C("activation", out=dec[:, 0:1024], in_=outb[oi], func=AF.Square, accum_out=stat[:, 8 + 2 * oi:9 + 2 * oi]), reads=B_outb[oi], writes=[B_dec[0], B_statF[oi]])
                      rstd_from_ss(8 + 2 * oi, 1024, B_statF[oi])

                  def F3(c):
                      oi = c % 2
                      S.op("act", C("activation", out=outb[oi], in_=outb[oi], func=AF.Copy, scale=stat[:, 9 + 2 * oi:10 + 2 * oi]), reads=B_outb[oi] + [B_statF[oi]], writes=B_outb[oi])
                      S.op("dve", C("tensor_tensor", out=outb[oi], in0=outb[oi], in1=vrow[:, VR_FING:VR_FING + 1024], op=ALU.mult), reads=B_outb[oi] + [B_vrow], writes=B_outb[oi])
                      r0 = tok0 + c * 128
                      S.dma("sp", C("dma_start", out=y_d[r0:r0 + 128, :], in_=outb[oi]), reads=B_outb[oi], final=True)

                  for c0 in range(0, NCH, 2):
                      cc = [c for c in (c0, c0 + 1) if c < NCH]
                      for fn in (F1, F2, F3):
                          for c in cc:
                              fn(c)
                          yield
                  for (_, _, whh) in wov:
                      w_release(whh)

              ckpt('%d:E' % sti)
              if sti + 1 < len(ST_LIST):
                  run_il(phaseEF(), phaseA(*ST_LIST[sti + 1]))
              else:
                  run_il(phaseEF())

        try:
            main_loop()
            assert wstate["used"] == len(wq), (wstate, len(wq))
        except _Stop:
            pass

        with nc.Block() as block:
            @block.sync
            def _(e):
                S.emit("sp", e)

            @block.tensor
            def _(e):
                S.emit("pe", e)

            @block.scalar
            def _(e):
                S.emit("act", e)

            @block.vector
            def _(e):
                S.emit("dve", e)

            @block.gpsimd
            def _(e):
                S.emit("pool", e)
    return nc


def _consts():
    k = np.arange(128)
    c = np.zeros((128, NCST), np.float32)
    c[:, C_ID:C_ID + 128] = np.eye(128)
    c[:, C_SEQ:C_SEQ + 16] = (k[:, None] % 16 == np.arange(16)[None, :])
    pos = np.arange(16)
    for g in range(4):
        w = 2 << g
        c[:, C_INVC + g * 16:C_INVC + (g + 1) * 16] = (1.0 / np.minimum(pos + 1, w))[None, :]
    U = (k[:, None] <= k[None, :])
    Lm = (k[:, None] > k[None, :])
    same = (k[:, None] % 16 == k[None, :] % 16)
    m = np.concatenate([np.eye(128), U, Lm, np.ones((128, 128)), U & same, Lm & same, same, np.zeros((128, 128))], axis=1).astype(np.float32)
    return c, m


def make_in_maps(inp, cores):
    f = lambda a: np.ascontiguousarray(np.asarray(a, dtype=np.float32))
    w_ada, w_in = f(inp["w_ada"][0]), f(inp["w_in"][0])
    wpo, wssm, wo = f(inp["w_pool_out"][0]), f(inp["w_ssm_out"][0]), f(inp["w_o"][0])
    poolw = f(np.asarray(inp["pool_w"][0]).reshape(512, 128))
    b_ada = np.asarray(inp["b_ada"][0], np.float32)
    fm = lambda v: np.asarray(v, np.float32).reshape(-1, 128).T
    vfm = np.zeros((128, NVF), np.float32)
    vfm[:, VF_BSHIFT:VF_BSHIFT + 8] = fm(b_ada[0:1024])
    vfm[:, VF_BSCALE:VF_BSCALE + 8] = fm(b_ada[1024:2048])
    vfm[:, VF_NORMG:VF_NORMG + 8] = fm(inp["norm_g"][0])
    cw = np.asarray(inp["conv_w"][0], np.float32)
    vfm[:, VF_CONVW:VF_CONVW + 96] = cw.reshape(4, 24, 128).transpose(2, 1, 0).reshape(128, 96)
    vfm[:, VF_CONVB:VF_CONVB + 24] = fm(inp["conv_b"][0])
    vfm[:, VF_PSCALE:VF_PSCALE + 4] = fm(inp["pool_scale"][0])
    vfm[:, VF_SNG:VF_SNG + 16] = fm(inp["ssm_norm_g"][0])
    vrow = np.zeros((1, NVR), np.float32)
    vrow[0, VR_FING:VR_FING + 1024] = np.asarray(inp["final_g"], np.float32)
    vrow[0, VR_DTB:VR_DTB + 32] = np.asarray(inp["dt_bias"][0], np.float32)
    vrow[0, VR_ALOG:VR_ALOG + 32] = np.asarray(inp["a_log"][0], np.float32)
    vrow[0, VR_DSKIP:VR_DSKIP + 32] = np.asarray(inp["d_skip"][0], np.float32)
    cst, msk = _consts()
    xp, xs = np.asarray(inp["x_prompt"], np.float32), np.asarray(inp["x_sample"], np.float32)
    cp, csm = np.asarray(inp["c_prompt"], np.float32), np.asarray(inp["c_sample"], np.float32)
    sp, sc, ss = np.asarray(inp["state_pool"][0]), np.asarray(inp["state_conv"][0]), np.asarray(inp["state_ssm"][0])
    maps = []
    for c in cores:
        sl = slice(16 * c, 16 * c + 16)
        x = np.concatenate([xp[c], xs[sl].transpose(1, 0, 2).reshape(128, D)], axis=0)
        cT = np.concatenate([cp[c:c + 1], csm[sl]], axis=0).T
        spT = sp[sl].reshape(16, 15, 4, 128).transpose(3, 2, 1, 0).reshape(128, 960)
        scT = sc[sl].reshape(16, 3, 24, 128).transpose(3, 2, 1, 0).reshape(128, 1152)
        ssT = ss[sl].reshape(16, 2048, 128).transpose(0, 2, 1)
        maps.append({
            "x": f(x), "cT": f(cT), "spT": f(spT), "scT": f(scT), "ssT": f(ssT),
            "w_ada": w_ada, "w_in": w_in, "w_pool_out": wpo, "w_ssm_out": wssm, "w_o": wo,
            "pool_w": poolw, "vecs_fm": vfm, "vecs_row": vrow, "consts": cst, "masks": msk, "b_gate": f(b_ada[None, 2048:3072]),
        })
    return maps


def assemble(results, cores):
    n = len(cores)
    y_prompt = np.zeros((n, SEQ, D), np.float32)
    y_sample = np.zeros((16 * n, 8, D), np.float32)
    npp = np.zeros((1, n, 15, 512), np.float32)
    ncp = np.zeros((1, n, 3, 3072), np.float32)
    nsp = np.zeros((1, n, 32, 64, 128), np.float32)
    nps = np.zeros((1, 16 * n, 15, 512), np.float32)
    ncs = np.zeros((1, 16 * n, 3, 3072), np.float32)
    nss = np.zeros((1, 16 * n, 32, 64, 128), np.float32)
    for i, r in enumerate(results):
        sl = slice(16 * i, 16 * i + 16)
        y = np.asarray(r["y"])
        y_prompt[i] = y[0:SEQ]
        y_sample[sl] = y[SEQ:].reshape(8, 16, D).transpose(1, 0, 2)
        npp[0, i] = np.asarray(r["npp"]).reshape(128, 4, 15).transpose(2, 1, 0).reshape(15, 512)
        ncp[0, i] = np.asarray(r["ncp"]).reshape(128, 24, 3).transpose(2, 1, 0).reshape(3, 3072)
        nsp[0, i] = np.asarray(r["nsp"]).T.reshape(32, 64, 128)
        nps[0, sl] = np.asarray(r["nps"]).reshape(128, 4, 15, 16).transpose(3, 2, 1, 0).reshape(16, 15, 512)
        ncs[0, sl] = np.asarray(r["ncs"]).reshape(128, 24, 3, 16).transpose(3, 2, 1, 0).reshape(16, 3, 3072)
        nss[0, sl] = np.asarray(r["nss"]).transpose(0, 2, 1).reshape(16, 32, 64, 128)
    return (y_prompt, y_sample, npp, ncp, nsp, nps, ncs, nss)


def kernel(**inputs):
    cores = list(range(NCORES))
    nc = build_nc()
    in_maps = make_in_maps(inputs, cores)
    res = run_bass_kernel_spmd(nc, in_maps, core_ids=cores)
    return assemble(res.results, cores)
```

```python
import numpy as np
import concourse.bass as bass
import concourse.mybir as mybir
from concourse.bass_utils import run_bass_kernel_spmd
from contextlib import ExitStack

F32 = mybir.dt.float32
BF16 = mybir.dt.bfloat16
AF = mybir.ActivationFunctionType
ALU = mybir.AluOpType

NCORES = 8
D = 1024
SEQ = 2048
NTOK = 2176
IN_U, IN_Z, IN_ZS, IN_XBC, IN_DT, IN_G = 0, 512, 1024, 3072, 6144, 6176
EPS = 1e-6
VF_BSHIFT, VF_BSCALE, VF_NORMG, VF_CONVW, VF_CONVB, VF_PSCALE, VF_SNG, NVF = 0, 8, 16, 24, 120, 144, 148, 164
VR_FING, VR_DTB, VR_ALOG, VR_DSKIP, NVR = 0, 1024, 1056, 1088, 1120
C_ID, C_SEQ, C_INVC, NCST = 0, 128, 144, 208

import os
DEBUG = bool(int(os.environ.get('K_DEBUG', '0')))
K_STOP = os.environ.get('K_STOP', '')


class _Stop(Exception):
    pass


class Buf:
    __slots__ = ("name", "lw", "rd", "excl")

    def __init__(self, name, excl=False):
        self.name = name
        self.lw = None
        self.rd = {}
        self.excl = excl


def bufs(name, *dims):
    if len(dims) == 1:
        return [Buf("%s%d" % (name, i)) for i in range(dims[0])]
    return [bufs("%s%d_" % (name, i), *dims[1:]) for i in range(dims[0])]


class Sched:
    ENG = ("pe", "act", "dve", "pool", "sp")

    def __init__(self, sems, dma_sems):
        self.sem = sems
        self.dma_sems = dma_sems
        self.dma_tgt = [0] * len(dma_sems)
        self.dma_rr = 0
        self.cnt = {e: 0 for e in self.ENG}
        self.ops = {e: [] for e in self.ENG}
        self.seen = {e: {} for e in self.ENG}
        self.final_tokens = []

    def _handle(self, key):
        return self.sem[key] if isinstance(key, str) else self.dma_sems[key]

    def _deps(self, eng, reads, writes):
        need = {}

        def add(tok):
            if tok is None:
                return
            k, v = tok
            if k == eng and eng == "pe":
                return
            if need.get(k, 0) < v:
                need[k] = v
        for b in reads:
            add(b.lw)
        for b in writes:
            add(b.lw)
            for k, v in b.rd.items():
                add((k, v))
        out = []
        for k, v in need.items():
            if self.seen[eng].get(k, 0) >= v:
                continue
            self.seen[eng][k] = v
            out.append((k, v))
        return out

    def op(self, eng, fn, reads=(), writes=()):
        ex = [b for b in reads if b.excl]
        if ex:
            reads = [b for b in reads if not b.excl]
            writes = list(writes) + ex
        waits = self._deps(eng, reads, writes)
        self.cnt[eng] += 1
        c = self.cnt[eng]
        self.ops[eng].append((waits, fn, (eng, 1)))
        for b in reads:
            if b.rd.get(eng, 0) < c:
                b.rd[eng] = c
        for b in writes:
            b.lw = (eng, c)
            b.rd = {}

    def dma(self, q, fn, reads=(), writes=(), final=False):
        waits = self._deps(q, reads, writes)
        s = self.dma_rr
        self.dma_rr = (self.dma_rr + 1) % len(self.dma_sems)
        prev = self.dma_tgt[s]
        if prev > 0 and self.seen[q].get(s, 0) < prev:
            self.seen[q][s] = prev
            waits.append((s, prev))
        tgt = prev + 16
        self.dma_tgt[s] = tgt
        self.ops[q].append((waits, fn, (s, 16)))
        for b in reads:
            if b.rd.get(s, 0) < tgt:
                b.rd[s] = tgt
        for b in writes:
            b.lw = (s, tgt)
            b.rd = {}
        if final:
            self.final_tokens.append((s, tgt))

    def emit(self, eng, e):
        for waits, fn, inc in self.ops[eng]:
            for k, v in waits:
                e.wait_ge(self._handle(k), v)
            fn(e).then_inc(self._handle(inc[0]), inc[1])
        if eng == "sp":
            done = {}
            for k, v in self.final_tokens:
                done[k] = max(done.get(k, 0), v)
            for k, v in done.items():
                e.wait_ge(self._handle(k), v)


def C(name, *args, **kw):
    def f(e):
        return getattr(e, name)(*args, **kw)
    return f


def bc(ap, shape):
    return ap.to_broadcast(list(shape))


def build_nc():
    nc = bass.Bass("TRN2", target_bir_lowering=False)

    def din(name, shape):
        return nc.dram_tensor(name, list(shape), F32, kind="ExternalInput").ap()

    def dout(name, shape):
        return nc.dram_tensor(name, list(shape), F32, kind="ExternalOutput").ap()

    x_d = din("x", [NTOK, D])
    cT_d = din("cT", [D, 17])
    spT_d = din("spT", [128, 4 * 16 * 15])
    scT_d = din("scT", [128, 24 * 16 * 3])
    ssT_d = din("ssT", [16, 128, 2048])
    wada_d = din("w_ada", [D, 3072])
    win_d = din("w_in", [D, 8224])
    wpo_d = din("w_pool_out", [512, D])
    wssm_d = din("w_ssm_out", [2048, D])
    wo_d = din("w_o", [D, D])
    poolw_d = din("pool_w", [512, 128])
    vfm_d = din("vecs_fm", [128, NVF])
    vrow_d = din("vecs_row", [1, NVR])
    bgate_d = din("b_gate", [1, 1024])
    cst_d = din("consts", [128, NCST])
    msk_d = din("masks", [128, 8 * 128])

    y_d = dout("y", [NTOK, D])
    npp_d = dout("npp", [128, 60])
    ncp_d = dout("ncp", [128, 72])
    nsp_d = dout("nsp", [128, 2048])
    nps_d = dout("nps", [128, 960])
    ncs_d = dout("ncs", [128, 1152])
    nss_d = dout("nss", [16, 128, 2048])
    dbg = {}
    if DEBUG:
        for nm, n in (("d_mT", 4096), ("d_ynT", 8192), ("d_pzT", 2048), ("d_th", 8192), ("d_yb", 2048), ("d_xc", 12288)):
            dbg[nm] = dout(nm, [128, n])

    with ExitStack() as es:
        def sb(name, shape, dt=F32):
            return es.enter_context(nc.sbuf_tensor(name, list(shape), dt))

        sems = {e: es.enter_context(nc.semaphore("s_" + e)) for e in Sched.ENG}
        dsem = [es.enter_context(nc.semaphore("d%d" % i)) for i in range(24)]
        S = Sched(sems, dsem)

        NSLOT = 4
        wslot = [sb("wslot%d" % i, [128, 4096], BF16) for i in range(NSLOT)]
        B_wslot = bufs("wslot", NSLOT)
        cst = sb("cst", [128, NCST]); B_cst = Buf("cst")
        cbf = sb("cbf", [128, 8 * 128], BF16); B_cbf = Buf("cbf")
        vfm = sb("vfm", [128, NVF]); B_vfm = Buf("vfm")
        vrow = sb("vrow", [128, NVR]); B_vrow = Buf("vrow")
        misc = sb("misc", [128, 256]); B_misc = Buf("misc")
        cTs = sb("cTs", [128, 8, 17]); B_cTs = Buf("cTs")
        siluc = sb("siluc", [128, 8, 17], BF16); B_siluc = Buf("siluc")
        Gt = sb("Gt", [128, 8, 17]); B_Gt = Buf("Gt")
        St = sb("St", [128, 8, 17]); B_St = Buf("St")
        gate_p = sb("gate_p", [128, 1024]); B_gate_p = Buf("gate_p")
        gate_s = sb("gate_s", [128, 1024]); B_gate_s = Buf("gate_s")
        poolw = sb("poolw", [128, 4, 128], BF16); B_poolw = Buf("poolw")
        xb = [sb("xb%d" % i, [128, 1024]) for i in range(2)]; B_xb = bufs("xb", 2)
        tmpA = sb("tmpA", [128, 1024]); B_tmpA = Buf("tmpA")
        stat = sb("stat", [128, 16]); B_stat = Buf("stat"); B_statA = bufs("statA", 2); B_statF = bufs("statF", 2)
        cm05 = sb("cm05", [128, 1]); B_cm05 = Buf("cm05")
        hT = sb("hT", [128, 8, 512], BF16); B_hT = bufs("hT", 4, 8)
        ynT = sb("ynT", [128, 16, 512], BF16); B_ynT = bufs("ynT", 4, 2)
        uX = sb("uX", [128, 4, 527]); B_uX = bufs("uX", 4)
        SaSb = sb("SaSb", [128, 1056]); B_Sa = Buf("Sa"); B_Sb = Buf("Sb")
        Sa = SaSb[:, 0:528]
        Sb_ = SaSb[:, 528:1056]
        xw = SaSb[:].bitcast(BF16)[:, 0:2048]; B_xw = [B_Sa, B_Sb]
        siluz = sb("siluz", [128, 4, 512], BF16); B_siluz = bufs("siluz", 4)
        dTt = sb("dT", [128, 4, 512], BF16); B_dT = bufs("dT", 4)
        pzT = siluz; B_pzT = B_siluz
        raw = [sb("raw%d" % i, [128, 516], BF16) for i in range(2)]; B_raw = bufs("raw", 2)
        chist = sb("chist", [128, 24, 3], BF16); B_chist = bufs("chist", 24)
        dg = [sb("dg%d" % i, [128, 4, 128], BF16) for i in range(2)]; B_dg = bufs("dg", 2)
        xcT = sb("xcT", [128, 24, 512], BF16); B_xcT = bufs("xcT", 24)
        sz = sb("sz", [128, 4, 2048], BF16); B_sz = bufs("sz", 4)
        vdt = sb("vdt", [128, 4, 32]); B_vdt = bufs("vdt", 4)
        dtt = sb("dtt", [128, 32]); B_dtt = Buf("dtt")
        aa = sb("aa", [128, 32]); B_aa = Buf("aa")
        aab = sb("aab", [128, 32], BF16); B_aab = Buf("aab")
        ex = sb("ex", [128, 96]); B_ex = Buf("ex")
        xdt = sb("xdt", [128, 2048], BF16); B_xdt = Buf("xdt")
        xD = sb("xD", [128, 2048], BF16); B_xD = Buf("xD")
        Btm = sb("Btm", [128, 512], BF16); B_Btm = Buf("Btm")
        Rg = [sb("Rg%d" % i, [128, 1024], BF16) for i in range(2)]; B_Rg = bufs("Rg", 2)
        dec = sb("dec", [128, 2176], BF16); B_dec = bufs("dec", 2)
        att = sb("att", [128, 2048], BF16); B_att = bufs("att", 2)
        cbTm = sb("cbTm", [128, 4, 128], BF16); B_cbTm = Buf("cbTm")
        ynb = dTt[:].rearrange("p g t -> p (g t)"); B_ynb = B_dT
        yb = sb("yb", [128, 2048]); B_yb = bufs("yb", 4)
        hst = sb("hst", [128, 2048]); B_hst = bufs("hst", 4)
        hstb = sb("hstb", [128, 2048], BF16); B_hstb = bufs("hstb", 4)
        tmpg = [sb("tmpg%d" % i, [128, 512]) for i in range(2)]; B_tmpg = bufs("tmpg", 2)
        hsf = [hstb[:].bitcast(F32)[:, i * 512:(i + 1) * 512] for i in range(2)]; B_hsf = [[B_hstb[0], B_hstb[1]], [B_hstb[2], B_hstb[3]]]
        hsb = [sb("hsb%d" % i, [128, 512], BF16) for i in range(2)]; B_hsb = bufs("hsb", 2)
        am = sb("am", [128, 16, 32], BF16); B_am = Buf("am")
        exl = sb("exl", [128, 512]); B_exl = Buf("exl")
        ncf = hst[:, 0:1152].rearrange("p (b s r) -> p b s r", b=24, s=16); B_ncf = [B_hst[(b * 48) // 512] for b in range(24)]
        ncpt = sb("ncpt", [128, 24, 3]); B_ncpt = Buf("ncpt")
        outb = [yb[:, 0:1024], yb[:, 1024:2048]]; B_outb = [[B_yb[0], B_yb[1]], [B_yb[2], B_yb[3]]]

        ps = es.enter_context(nc.psum_tensor("ps", [128, 4096], F32))
        B_ps = [Buf("ps%d" % i, excl=True) for i in range(8)]
        psrr = [0]

        def ps_one():
            b = psrr[0]
            psrr[0] = (b + 1) % 8
            return b

        def ps_pair():
            if psrr[0] % 2:
                psrr[0] = (psrr[0] + 1) % 8
            b = psrr[0]
            psrr[0] = (b + 2) % 8
            return b

        def psf(b, n=512, off=0):
            return ps[:, b * 512 + off: b * 512 + off + n]

        def psb(b, n, off=0):
            nb = (off + n + 1023) // 1024
            v = ps[:, b * 512: (b + nb) * 512].bitcast(BF16)
            return v[:, off: off + n]

        ident = cst[:, C_ID:C_ID + 128]
        identb = cbf[:, 0:128]
        Ub, Lmb, oneb, Usb, Lmsb, onesb, zerob = (cbf[:, i * 128:(i + 1) * 128] for i in range(1, 8))
        seqm = cst[:, C_SEQ:C_SEQ + 16]
        A_b = misc[:, 0:32]
        D_b = misc[:, 32:64]
        hbg = tmpA

        wq = []
        wstate = {"issued": 0, "used": 0}

        def w_enqueue(src, kk, n):
            wq.append((src, kk, n))

        wfree = list(range(NSLOT))
        wslot_of = {}

        def w_prefetch():
            while wfree and wstate["issued"] < len(wq):
                i = wstate["issued"]
                src, kk, n = wq[i]
                sl = wfree.pop(0)
                wslot_of[i] = sl
                dst = wslot[sl][:, 0:kk * n].rearrange("p (k n) -> p k n", k=kk)
                S.dma("pool", C("dma_start", out=dst, in_=src), writes=[B_wslot[sl]])
                wstate["issued"] += 1

        w_cur = [None]
        w_prev = [None]

        def w_next(kk, n, hold=False, lag=True):
            if w_prev[0] is not None:
                w_release(w_prev[0])
            w_prev[0] = w_cur[0]
            w_cur[0] = None
            if not lag and w_prev[0] is not None:
                w_release(w_prev[0])
                w_prev[0] = None
            j = wstate["used"]
            wstate["used"] += 1
            w_prefetch()
            assert j in wslot_of, ("weight stream stalled: no free slot", j)
            sl = wslot_of[j]
            assert wq[j][1] == kk and wq[j][2] == n, (j, wq[j][1:], kk, n)
            if not hold:
                w_cur[0] = j
            return wslot[sl][:, 0:kk * n].rearrange("p (k n) -> p k n", k=kk), B_wslot[sl], j

        def w_release(j):
            wfree.append(wslot_of[j])
            w_prefetch()

        def src_rows(d_ap, c0, n):
            return d_ap[:, c0:c0 + n].rearrange("(k p) n -> p k n", p=128)

        for gi in range(6):
            w_enqueue(src_rows(wada_d, gi * 512, 512), 8, 512)
        ST_LIST = [(0, 4, False), (512, 4, False), (1024, 4, False), (1536, 4, False), (2048, 1, True)]
        for _ in ST_LIST:
            w_enqueue(src_rows(win_d, IN_U, 512), 8, 512)
            w_enqueue(src_rows(win_d, IN_Z, 512), 8, 512)
            for i in range(6):
                w_enqueue(src_rows(win_d, IN_XBC + i * 512, 512), 8, 512)
            for i in range(4):
                w_enqueue(src_rows(win_d, IN_ZS + i * 512, 512), 8, 512)
            w_enqueue(src_rows(win_d, IN_DT, 32), 8, 32)
            for i in range(4):
                w_enqueue(src_rows(win_d, IN_G + i * 512, 512), 8, 512)
            w_enqueue(src_rows(wpo_d, 0, 1024), 4, 1024)
            for i in range(4):
                w_enqueue(src_rows(wssm_d, i * 256, 256), 16, 256)
            for i in range(2):
                w_enqueue(src_rows(wo_d, i * 512, 512), 8, 512)

        S.dma("sp", C("dma_start", out=cst[:], in_=cst_d[:, :]), writes=[B_cst])
        S.dma("sp", C("dma_start", out=vfm[:], in_=vfm_d[:, :]), writes=[B_vfm])
        S.dma("sp", C("dma_start", out=vrow[:], in_=vrow_d[0:1, :].partition_broadcast(128)), writes=[B_vrow])
        S.dma("sp", C("dma_start", out=cTs[:], in_=cT_d[:, :].rearrange("(k p) b -> p k b", p=128)), writes=[B_cTs])
        S.dma("pool", C("dma_start", out=poolw[:], in_=poolw_d[:, :].rearrange("(g c) d -> c g d", c=128)), writes=[B_poolw])
        w_prefetch()
        S.dma("pool", C("dma_start", out=cbf[:], in_=msk_d[:, :]), writes=[B_cbf])
        S.op("pool", C("memset", cm05[:], -0.5), writes=[B_cm05])
        S.op("pool", C("memset", hst[:], 0.0), writes=B_hst)
        S.op("pool", C("memset", hstb[:], 0.0), writes=B_hstb)
        S.op("pool", C("memset", chist[:], 0.0), writes=B_chist)
        S.op("pool", C("memset", uX[:], 0.0), writes=B_uX)
        S.op("act", C("activation", out=A_b, in_=vrow[:, VR_ALOG:VR_ALOG + 32], func=AF.Exp), reads=[B_vrow], writes=[B_misc])
        S.op("dve", C("tensor_scalar", out=A_b, in0=A_b, scalar1=-1.0, scalar2=None, op0=ALU.mult), reads=[B_misc], writes=[B_misc])
        S.op("dve", C("tensor_copy", out=D_b, in_=vrow[:, VR_DSKIP:VR_DSKIP + 32]), reads=[B_vrow], writes=[B_misc])
        S.op("act", C("activation", out=siluc[:], in_=cTs[:], func=AF.Silu), reads=[B_cTs], writes=[B_siluc])
        cexp_p = xD[:, 0:1024].rearrange("p (k t) -> p k t", k=8)
        cexp_s = xD[:, 1024:2048].rearrange("p (k t) -> p k t", k=8)
        S.op("dve", C("tensor_copy", out=cexp_p, in_=bc(siluc[:, :, 0:1], [128, 8, 128])), reads=[B_siluc], writes=[B_xD])
        for k in range(8):
            S.op("dve", C("tensor_copy",
                out=cexp_s[:, k, :].rearrange("p (t s) -> p t s", s=16),
                in_=bc(siluc[:, k, 1:17].unsqueeze(1), [128, 8, 16])), reads=[B_siluc], writes=[B_xD])
        bmod = ps_one()
        for gi in range(4):
            wv, wb, wh = w_next(8, 512)
            for jb in range(4):
                j = gi * 4 + jb
                for k in range(8):
                    S.op("pe", C("matmul",
                        psf(bmod, 17, j * 17), lhsT=wv[:, k, jb * 128:(jb + 1) * 128], rhs=siluc[:, k, :],
                        start=(k == 0), stop=(k == 7)), reads=[wb, B_siluc], writes=[B_ps[bmod]])
        modv = psf(bmod, 272).rearrange("p (j b) -> p j b", j=16)
        S.op("dve", C("tensor_tensor", out=St[:], in0=modv[:, 0:8, :], in1=bc(vfm[:, VF_BSHIFT:VF_BSHIFT + 8].unsqueeze(2), [128, 8, 17]), op=ALU.add),
             reads=[B_ps[bmod], B_vfm], writes=[B_St])
        S.op("dve", C("tensor_tensor", out=Gt[:], in0=modv[:, 8:16, :], in1=bc(vfm[:, VF_BSCALE:VF_BSCALE + 8].unsqueeze(2), [128, 8, 17]), op=ALU.add),
             reads=[B_ps[bmod], B_vfm], writes=[B_Gt])
        S.op("dve", C("scalar_tensor_tensor", out=Gt[:], in0=Gt[:], scalar=1.0, in1=bc(vfm[:, VF_NORMG:VF_NORMG + 8].unsqueeze(2), [128, 8, 17]), op0=ALU.add, op1=ALU.mult),
             reads=[B_Gt, B_vfm], writes=[B_Gt])
        S.dma("sp", C("dma_start", out=hbg[:], in_=bgate_d[0:1, :].partition_broadcast(128)), writes=[B_tmpA])
        S.op("dve", C("tensor_scalar", out=hbg[:], in0=hbg[:], scalar1=0.5, scalar2=None, op0=ALU.mult), reads=[B_tmpA], writes=[B_tmpA])
        for hf in range(2):
            wv, wb, wh = w_next(8, 512)
            for (cexp, gt, Bg) in ((cexp_p, gate_p, B_gate_p), (cexp_s, gate_s, B_gate_s)):
                b = ps_one()
                for k in range(8):
                    S.op("pe", C("matmul", psf(b), lhsT=cexp[:, k, :], rhs=wv[:, k, :], start=(k == 0), stop=(k == 7)),
                         reads=[B_xD, wb], writes=[B_ps[b]])
                S.op("dve", C("scalar_tensor_tensor", out=gt[:, hf * 512:(hf + 1) * 512], in0=psf(b), scalar=0.5, in1=hbg[:, hf * 512:(hf + 1) * 512], op0=ALU.mult, op1=ALU.add),
                     reads=[B_ps[b], B_tmpA], writes=[Bg])

        def rstd_from_ss(col, n, Bs=None):
            Bs = Bs or B_stat
            S.op("dve", C("tensor_scalar", out=stat[:, col + 1:col + 2], in0=stat[:, col:col + 1], scalar1=1.0 / n, scalar2=EPS, op0=ALU.mult, op1=ALU.add),
                 reads=[Bs], writes=[Bs])
            S.op("pool", C("tensor_tensor", out=stat[:, col + 1:col + 2], in0=stat[:, col + 1:col + 2], in1=cm05[:], op=ALU.pow),
                 reads=[Bs, B_cm05], writes=[Bs])

        xbrr = [0]

        def x_load(r0):
            i = xbrr[0]
            xbrr[0] ^= 1
            S.dma("sp", C("dma_start", out=xb[i][:], in_=x_d[r0:r0 + 128, :]), writes=[B_xb[i]])
            return i

        def ckpt(tag):
            if K_STOP == tag:
                raise _Stop()

        def main_loop():
          ckpt('P')
          def run_il(*gens):
              gens = list(gens)
              while gens:
                  for gen in list(gens):
                      try:
                          next(gen)
                      except StopIteration:
                          gens.remove(gen)

          def phaseA(tok0, NCH, SAMPLE):
              junk = dec[:, 0:1024]
              st = {}

              def sa(c):
                  xi = x_load(tok0 + c * 128)
                  st[c] = xi
                  S.op("act", C("activation", out=junk, in_=xb[xi][:], func=AF.Square, accum_out=stat[:, 2 * (c % 2):2 * (c % 2) + 1]),
                       reads=[B_xb[xi]], writes=[B_dec[0], B_statA[c % 2]])
                  rstd_from_ss(2 * (c % 2), 1024, B_statA[c % 2])

              def sb_(c):
                  xi = st[c]
                  S.op("act", C("activation", out=tmpA[:], in_=xb[xi][:], func=AF.Copy, scale=stat[:, 2 * (c % 2) + 1:2 * (c % 2) + 2]),
                       reads=[B_xb[xi], B_statA[c % 2]], writes=[B_tmpA])
                  bp = ps_pair()
                  st[("bp", c)] = bp
                  for k in range(8):
                      S.op("pe", C("transpose", out=ps[:, bp * 512 + k * 128: bp * 512 + (k + 1) * 128], in_=tmpA[:, k * 128:(k + 1) * 128], identity=ident),
                           reads=[B_tmpA, B_cst], writes=[B_ps[bp + k // 4]])

              def sd(c):
                  bp = st[("bp", c)]
                  if not SAMPLE:
                      for k in range(8):
                          src = ps[:, bp * 512 + k * 128: bp * 512 + (k + 1) * 128]
                          dst = hT[:, k, c * 128:(c + 1) * 128]
                          if k < 4:
                              S.op("dve", C("tensor_scalar", out=dst, in0=src, scalar1=Gt[:, k, 0:1], scalar2=St[:, k, 0:1], op0=ALU.mult, op1=ALU.add),
                                   reads=[B_ps[bp + k // 4], B_Gt, B_St], writes=[B_hT[c][k]])
                          else:
                              S.op("act", C("activation", out=dst, in_=src, func=AF.Identity, scale=Gt[:, k, 0:1], bias=St[:, k, 0:1]),
                                   reads=[B_ps[bp + k // 4], B_Gt, B_St], writes=[B_hT[c][k]])
                  else:
                      for k in range(8):
                          src = ps[:, bp * 512 + k * 128: bp * 512 + (k + 1) * 128].rearrange("p (t s) -> p t s", s=16)
                          dst = hT[:, k, 0:128].rearrange("p (t s) -> p t s", s=16)
                          tv = tmpg[0][:, 0:128].rearrange("p (t s) -> p t s", s=16)
                          S.op("dve", C("tensor_tensor", out=tv, in0=src, in1=bc(Gt[:, k, 1:17].unsqueeze(1), [128, 8, 16]), op=ALU.mult),
                               reads=[B_ps[bp + k // 4], B_Gt], writes=[B_tmpg[0]])
                          S.op("dve", C("tensor_tensor", out=dst, in0=tv, in1=bc(St[:, k, 1:17].unsqueeze(1), [128, 8, 16]), op=ALU.add),
                               reads=[B_tmpg[0], B_St], writes=[B_hT[0][k]])

              sa(0)
              yield
              for c in range(NCH):
                  sb_(c)
                  yield
                  if c + 1 < NCH:
                      sa(c + 1)
                      yield
                  sd(c)
                  yield

          for sti, (tok0, NCH, SAMPLE) in enumerate(ST_LIST):
              T = NCH * 128
              LAST_PROMPT = (tok0 == 1536)
              NS = 16 if SAMPLE else 1
              L = 8 if SAMPLE else T
              gate_t, Bgate = (gate_s, B_gate_s) if SAMPLE else (gate_p, B_gate_p)
              mU, mLm, mOne = (Usb, Lmsb, onesb) if SAMPLE else (Ub, Lmb, oneb)

              if sti == 0:
                  run_il(phaseA(tok0, NCH, SAMPLE))

              def hT_reads(k):
                  return [B_hT[c][k] for c in range(NCH)]

              def inproj_block(wv, wb, jb, b):
                  for k in range(8):
                      S.op("pe", C("matmul", psf(b, T), lhsT=wv[:, k, jb * 128:(jb + 1) * 128], rhs=hT[:, k, 0:T], start=(k == 0), stop=(k == 7)),
                           reads=[wb] + hT_reads(k), writes=[B_ps[b]])

              ckpt('%d:A' % sti)
              TS = 16 if SAMPLE else 1
              if SAMPLE:
                  for g in range(4):
                      S.dma("sp", C("dma_start", out=uX[:, g, 0:240], in_=spT_d[:, g * 240:(g + 1) * 240]), writes=[B_uX[g]])
              wv, wb, wh = w_next(8, 512)
              for g in range(4):
                  b = ps_one()
                  inproj_block(wv, wb, g, b)
                  S.op("act", C("activation", out=uX[:, g, 15 * TS:15 * TS + T], in_=psf(b, T), func=AF.Copy),
                       reads=[B_ps[b]], writes=[B_uX[g]])
              wv, wb, wh = w_next(8, 512)
              for g in range(4):
                  b = ps_one()
                  inproj_block(wv, wb, g, b)
                  S.op("act", C("activation", out=siluz[:, g, 0:T], in_=psf(b, T), func=AF.Silu),
                       reads=[B_ps[b]], writes=[B_siluz[g]])
              for g in range(4):
                  w = 2 << g
                  M = 15 + L
                  los = []
                  lo = 15
                  for lev in range(g, -1, -1):
                      los.append(lo)
                      lo -= (1 << lev)
                  los = los[::-1]
                  cur, Bcur = uX[:, g, :], B_uX[g]
                  for lev in range(g + 1):
                      sh = 1 << lev
                      lo_l = los[lev]
                      dstv, Bd = (Sa, B_Sa) if lev % 2 == 0 else (Sb_, B_Sb)
                      S.op("pool", C("tensor_tensor", out=dstv[:, lo_l * TS:M * TS], in0=cur[:, lo_l * TS:M * TS], in1=cur[:, (lo_l - sh) * TS:(M - sh) * TS], op=ALU.add),
                           reads=[Bcur], writes=[Bd])
                      cur, Bcur = dstv, Bd
                  S.op("dve", C("scalar_tensor_tensor", out=dTt[:, g, 0:T], in0=cur[:, 15 * TS:15 * TS + T], scalar=1.0 / w, in1=uX[:, g, 15 * TS:15 * TS + T], op0=ALU.mult, op1=ALU.subtract),
                       reads=[Bcur, B_uX[g]], writes=[B_dT[g]])
                  if tok0 == 0 and not SAMPLE:
                      S.op("dve", C("tensor_tensor", out=tmpg[0][:, 0:16], in0=cur[:, 15:31], in1=cst[:, C_INVC + g * 16:C_INVC + (g + 1) * 16], op=ALU.mult),
                           reads=[Bcur, B_cst], writes=[B_tmpg[0]])
                      S.op("dve", C("tensor_tensor", out=dTt[:, g, 0:16], in0=tmpg[0][:, 0:16], in1=uX[:, g, 15:31], op=ALU.subtract),
                           reads=[B_tmpg[0], B_uX[g]], writes=[B_dT[g]])
                  if LAST_PROMPT:
                      S.dma("sp", C("dma_start", out=npp_d[:, g * 15:(g + 1) * 15], in_=uX[:, g, T:T + 15]), reads=[B_uX[g]], final=True)
                  elif SAMPLE:
                      S.dma("sp", C("dma_start", out=nps_d[:, g * 240:(g + 1) * 240], in_=uX[:, g, 8 * 16:23 * 16]), reads=[B_uX[g]], final=True)
                  if not SAMPLE and not LAST_PROMPT:
                      S.op("pool", C("tensor_copy", out=uX[:, g, 0:15], in_=uX[:, g, T:T + 15]), reads=[B_uX[g]], writes=[B_uX[g]])

              ckpt('%d:B1' % sti)
              def diag_build(blk):
                  di = blk % 2
                  S.op("pool", C("tensor_tensor", out=dg[di][:], in0=bc(identb.unsqueeze(1), [128, 4, 128]), in1=bc(vfm[:, VF_CONVW + blk * 4:VF_CONVW + blk * 4 + 4].unsqueeze(2), [128, 4, 128]), op=ALU.mult),
                       reads=[B_cbf, B_vfm], writes=[B_dg[di]])

              def conv_block(blk, ri, braw):
                  di = blk % 2
                  b2 = ps_one()
                  for k in range(4):
                      S.op("pe", C("matmul", psf(b2, T), lhsT=dg[di][:, k, :], rhs=raw[ri][:, k * TS:k * TS + T], start=(k == 0), stop=(k == 3)),
                           reads=[B_dg[di], B_raw[ri]], writes=[B_ps[b2]])
                  S.op("act", C("activation", out=xcT[:, blk, 0:T], in_=psf(b2, T), func=AF.Silu, bias=vfm[:, VF_CONVB + blk:VF_CONVB + blk + 1]),
                       reads=[B_ps[b2], B_vfm], writes=[B_xcT[blk]])

              ncf2 = hst[:, 0:1152].rearrange("p (b m) -> p b m", b=24)
              if SAMPLE:
                  S.dma("sp", C("dma_start", out=hst[:, 0:1152], in_=scT_d[:, :]), writes=B_hst)
              pend = None
              for gi in range(6):
                  wv, wb, wh = w_next(8, 512)
                  for jb in range(4):
                      blk = gi * 4 + jb
                      ri = blk % 2
                      diag_build(blk)
                      if SAMPLE:
                          S.op("dve", C("tensor_copy", out=raw[ri][:, 0:48], in_=ncf2[:, blk, :]), reads=[B_ncf[blk]], writes=[B_raw[ri]])
                      else:
                          S.op("dve", C("tensor_copy", out=raw[ri][:, 0:3], in_=chist[:, blk, :]), reads=[B_chist[blk]], writes=[B_raw[ri]])
                      b = ps_one()
                      inproj_block(wv, wb, jb, b)
                      S.op("dve", C("tensor_copy", out=raw[ri][:, 3 * TS:3 * TS + T], in_=psf(b, T)), reads=[B_ps[b]], writes=[B_raw[ri]])
                      if LAST_PROMPT:
                          S.op("dve", C("tensor_copy", out=ncpt[:, blk, :], in_=psf(b, 3, T - 3)), reads=[B_ps[b]], writes=[B_ncpt])
                      elif SAMPLE:
                          S.op("dve", C("tensor_copy", out=ncf2[:, blk, :], in_=psf(b, 48, 80)), reads=[B_ps[b]], writes=[B_ncf[blk]])
                      else:
                          S.op("dve", C("tensor_copy", out=chist[:, blk, :], in_=psf(b, 3, T - 3)), reads=[B_ps[b]], writes=[B_chist[blk]])
                      if pend is not None:
                          conv_block(*pend)
                      pend = (blk, ri, b)
              conv_block(*pend)
              if SAMPLE:
                  S.dma("sp", C("dma_start", out=ncs_d[:, :], in_=hst[:, 0:1152]), reads=B_hst, final=True)
              if LAST_PROMPT:
                  S.dma("sp", C("dma_start", out=ncp_d[:, :].rearrange("p (b r) -> p b r", b=24), in_=ncpt[:]), reads=[B_ncpt], final=True)

              for g in range(4):
                  b = ps_one()
                  S.op("pe", C("matmul", psf(b, T), lhsT=poolw[:, g, :], rhs=dTt[:, g, 0:T], start=True, stop=True),
                       reads=[B_poolw, B_dT[g]], writes=[B_ps[b]])
                  S.op("dve", C("scalar_tensor_tensor", out=pzT[:, g, 0:T], in0=psf(b, T), scalar=vfm[:, VF_PSCALE + g:VF_PSCALE + g + 1], in1=siluz[:, g, 0:T], op0=ALU.mult, op1=ALU.mult),
                       reads=[B_ps[b], B_vfm, B_siluz[g]], writes=[B_siluz[g]])
              ckpt('%d:C' % sti)
              for zg in range(4):
                  wv, wb, wh = w_next(8, 512)
                  for c in range(NCH):
                      b = ps_one()
                      for k in range(8):
                          S.op("pe", C("matmul", psf(b), lhsT=hT[:, k, c * 128:(c + 1) * 128], rhs=wv[:, k, :], start=(k == 0), stop=(k == 7)),
                               reads=[wb, B_hT[c][k]], writes=[B_ps[b]])
                      S.op("act", C("activation", out=sz[:, c, zg * 512:(zg + 1) * 512], in_=psf(b), func=AF.Silu),
                           reads=[B_ps[b]], writes=[B_sz[c]])
              wv, wb, wh = w_next(8, 32)
              for c in range(NCH):
                  b = ps_one()
                  for k in range(8):
                      S.op("pe", C("matmul", psf(b, 32), lhsT=hT[:, k, c * 128:(c + 1) * 128], rhs=wv[:, k, :], start=(k == 0), stop=(k == 7)),
                           reads=[wb, B_hT[c][k]], writes=[B_ps[b]])
                  S.op("dve", C("tensor_tensor", out=vdt[:, c, :], in0=psf(b, 32), in1=vrow[:, VR_DTB:VR_DTB + 32], op=ALU.add),
                       reads=[B_ps[b], B_vrow], writes=[B_vdt[c]])

              ckpt('%d:B3' % sti)
              ybv = [yb[:].rearrange("p (g q) -> p g q", g=4), uX[:, :, 15:527]]
              B_ybv = [B_yb, B_uX]
              expacs, exprem, explast = ex[:, 0:32], ex[:, 32:64], ex[:, 64:96]
              xdt3 = xdt[:].rearrange("p (h q) -> p h q", h=32)
              xw3 = xw.rearrange("p (h q) -> p h q", h=32)

              def ystate_evac(b, g, yv, Byv):
                  S.op("dve", C("tensor_tensor", out=yv[:, g, :].rearrange("p (h q) -> p h q", h=8), in0=psf(b).rearrange("p (h q) -> p h q", h=8),
                                 in1=bc(expacs[:, g * 8:(g + 1) * 8].unsqueeze(2), [128, 8, 64]), op=ALU.mult),
                       reads=[B_ps[b], B_ex], writes=[Byv[g]])

              def r_build(g):
                  ri = g % 2
                  S.op("pool", C("tensor_tensor", out=Rg[ri][:].rearrange("p (h i) -> p h i", h=8), in0=bc(aa[:, g * 8:(g + 1) * 8].unsqueeze(2), [128, 8, 128]),
                                                                    in1=bc(mU.unsqueeze(1), [128, 8, 128]), op=ALU.mult),
                       reads=[B_aa, B_cbf], writes=[B_Rg[ri]])

              def stage1(c):
                  cs = slice(c * 128, (c + 1) * 128)
                  yv, Byv = ybv[c % 2], B_ybv[c % 2]
                  S.op("act", C("activation", out=dtt[:], in_=vdt[:, c, :], func=AF.Exp), reads=[B_vdt[c]], writes=[B_dtt])
                  S.op("act", C("activation", out=dtt[:], in_=dtt[:], func=AF.Ln, bias=1.0), reads=[B_dtt], writes=[B_dtt])
                  S.op("dve", C("tensor_tensor", out=aa[:], in0=dtt[:], in1=A_b, op=ALU.mult), reads=[B_dtt, B_misc], writes=[B_aa])
                  S.op("dve", C("tensor_copy", out=aab[:], in_=aa[:]), reads=[B_aa], writes=[B_aab])
                  bsm = ps_one()
                  for i, m in enumerate((mU, mLm, mOne)):
                      S.op("pe", C("matmul", psf(bsm, 32, i * 32), lhsT=m, rhs=aab[:], start=True, stop=True),
                           reads=[B_cbf, B_aab], writes=[B_ps[bsm]])
                  S.op("act", C("activation", out=ex[:], in_=psf(bsm, 96), func=AF.Exp), reads=[B_ps[bsm]], writes=[B_ex])
                  for g in range(2):
                      r_build(g)
                  yield
                  bx = ps_pair()
                  for blk in range(16):
                      S.op("pe", C("transpose", out=psb(bx, 128, blk * 128), in_=xcT[:, blk, cs], identity=identb),
                           reads=[B_xcT[blk], B_cbf], writes=[B_ps[bx + blk // 8]])
                  bB = ps_one()
                  for g in range(4):
                      S.op("pe", C("transpose", out=psb(bB, 128, g * 128), in_=xcT[:, 16 + g, cs], identity=identb),
                           reads=[B_xcT[16 + g], B_cbf], writes=[B_ps[bB]])
                  yield
                  xT3 = psb(bx, 2048).rearrange("p (h q) -> p h q", h=32)
                  S.op("dve", C("tensor_tensor", out=xdt3, in0=xT3, in1=bc(dtt[:].unsqueeze(2), [128, 32, 64]), op=ALU.mult),
                       reads=[B_ps[bx], B_ps[bx + 1], B_dtt], writes=[B_xdt])
                  S.op("act", C("activation", out=Btm[:], in_=psb(bB, 512), func=AF.Copy), reads=[B_ps[bB]], writes=[B_Btm])
                  yield
                  S.op("pool", C("tensor_tensor", out=xw3, in0=xdt3, in1=bc(exprem.unsqueeze(2), [128, 32, 64]), op=ALU.mult),
                       reads=[B_xdt, B_ex], writes=B_xw)
                  S.op("dve", C("tensor_tensor", out=xD[:].rearrange("p (h q) -> p h q", h=32), in0=xT3, in1=bc(D_b.unsqueeze(2), [128, 32, 64]), op=ALU.mult),
                       reads=[B_ps[bx], B_ps[bx + 1], B_misc], writes=[B_xD])
                  yield
                  bcb = ps_one()
                  for g in range(4):
                      S.op("pe", C("matmul", psf(bcb, 128, g * 128), lhsT=xcT[:, 16 + g, cs], rhs=xcT[:, 20 + g, cs], start=True, stop=True),
                           reads=[B_xcT[16 + g], B_xcT[20 + g]], writes=[B_ps[bcb]])
                  S.op("dve", C("tensor_tensor", out=cbTm[:], in0=psf(bcb).rearrange("p (g i) -> p g i", g=4), in1=bc(mU.unsqueeze(1), [128, 4, 128]), op=ALU.mult),
                       reads=[B_ps[bcb], B_cbf], writes=[B_cbTm])
                  yield
                  if not SAMPLE:
                      for g in range(4):
                          b = ps_one()
                          S.op("pe", C("matmul", psf(b), lhsT=xcT[:, 20 + g, cs], rhs=hstb[:, g * 512:(g + 1) * 512], start=True, stop=True),
                               reads=[B_xcT[20 + g], B_hstb[g]], writes=[B_ps[b]])
                          ystate_evac(b, g, yv, Byv)
                      yield
                      S.op("pool", C("tensor_tensor", out=hst[:].rearrange("p (h q) -> p h q", h=32), in0=hst[:].rearrange("p (h q) -> p h q", h=32),
                                     in1=bc(explast.unsqueeze(2), [128, 32, 64]), op=ALU.mult), reads=B_hst + [B_ex], writes=B_hst)
                      yield
                      for g in range(4):
                          b = ps_one()
                          S.op("pe", C("matmul", psf(b), lhsT=Btm[:, g * 128:(g + 1) * 128], rhs=xw[:, g * 512:(g + 1) * 512], start=True, stop=True),
                               reads=[B_Btm] + B_xw, writes=[B_ps[b]])
                          S.op("dve", C("tensor_tensor", out=hst[:, g * 512:(g + 1) * 512], in0=hst[:, g * 512:(g + 1) * 512], in1=psf(b), op=ALU.add),
                               reads=[B_hst[g], B_ps[b]], writes=[B_hst[g]])
                          S.op("act", C("activation", out=hstb[:, g * 512:(g + 1) * 512], in_=hst[:, g * 512:(g + 1) * 512], func=AF.Copy),
                               reads=[B_hst[g]], writes=[B_hstb[g]])
                      if LAST_PROMPT and c == NCH - 1:
                          S.dma("sp", C("dma_start", out=nsp_d[:, :], in_=hst[:]), reads=B_hst, final=True)
                  else:
                      S.op("pool", C("tensor_tensor", out=am[:], in0=bc(aa[:].unsqueeze(1), [128, 16, 32]), in1=bc(seqm.unsqueeze(2), [128, 16, 32]), op=ALU.mult),
                           reads=[B_aa, B_cst], writes=[B_am])
                      bl = ps_one()
                      S.op("pe", C("matmul", psf(bl), lhsT=oneb, rhs=am[:].rearrange("p s h -> p (s h)"), start=True, stop=True), reads=[B_cbf, B_am], writes=[B_ps[bl]])
                      S.op("act", C("activation", out=exl[:], in_=psf(bl), func=AF.Exp), reads=[B_ps[bl]], writes=[B_exl])
                      exl3 = exl[:].rearrange("p (s h) -> p s h", s=16)
                      NFB = 6
                      fbuf = [hsf[0], hsf[1]] + [uX[:, j, 15:527] for j in range(4)]
                      B_fbuf = [B_hsf[0], B_hsf[1]] + [[B_uX[j]] for j in range(4)]
                      bbuf = [hsb[0][:], hsb[1][:]] + [dTt[:, j, :] for j in range(4)]
                      B_bbuf = [[B_hsb[0]], [B_hsb[1]]] + [[B_dT[j]] for j in range(4)]
                      items = [(g, s) for g in range(4) for s in range(16)]

                      def ld(it):
                          g, s = items[it]
                          S.dma("sp", C("dma_start", out=fbuf[it % NFB], in_=ssT_d[s, :, g * 512:(g + 1) * 512]), writes=B_fbuf[it % NFB])
                          S.dma("pool", C("dma_start", out=bbuf[it % NFB], in_=ssT_d[s, :, g * 512:(g + 1) * 512]), writes=B_bbuf[it % NFB])

                      for it in range(NFB - 1):
                          ld(it)
                      bS = None
                      for it, (g, s) in enumerate(items):
                          if s == 0:
                              S.op("pool", C("memset", dec[:], 0.0), writes=B_dec)
                              S.op("pool", C("tensor_copy", out=dec[:, 0:16 * 129].rearrange("p (s m) -> p s m", m=129)[:, :, 0:113:16], in_=xcT[:, 20 + g, 0:128].rearrange("p (t s) -> p s t", s=16)),
                                   reads=[B_xcT[20 + g]], writes=B_dec)
                              S.op("pool", C("tensor_tensor", out=att[:].rearrange("p (s n) -> p s n", s=16), in0=bc(Btm[:, g * 128:(g + 1) * 128].unsqueeze(1), [128, 16, 128]),
                                                                         in1=bc(seqm.unsqueeze(2), [128, 16, 128]), op=ALU.mult),
                                   reads=[B_Btm, B_cst], writes=B_att)
                              bS = ps_one()
                          if it + NFB - 1 < len(items):
                              ld(it + NFB - 1)
                          fb_, Bf = fbuf[it % NFB], B_fbuf[it % NFB]
                          bb_, Bb = bbuf[it % NFB], B_bbuf[it % NFB]
                          S.op("pe", C("matmul", psf(bS), lhsT=dec[:, s * 128:(s + 1) * 128], rhs=bb_, start=(s == 0), stop=(s == 15)),
                               reads=B_dec + Bb, writes=[B_ps[bS]])
                          bH = ps_one()
                          if bH == bS:
                              bH = ps_one()
                          S.op("pe", C("matmul", psf(bH), lhsT=att[:, s * 128:(s + 1) * 128], rhs=xw[:, g * 512:(g + 1) * 512], start=True, stop=True),
                               reads=B_att + B_xw, writes=[B_ps[bH]])
                          S.op("pool" if it % 2 == 0 else "dve", C("tensor_tensor", out=fb_.rearrange("p (h q) -> p h q", h=8), in0=fb_.rearrange("p (h q) -> p h q", h=8),
                                                                               in1=bc(exl3[:, s, g * 8:(g + 1) * 8].unsqueeze(2), [128, 8, 64]), op=ALU.mult),
                               reads=Bf + [B_exl], writes=Bf)
                          S.op("dve", C("tensor_tensor", out=fb_, in0=fb_, in1=psf(bH), op=ALU.add),
                               reads=Bf + [B_ps[bH]], writes=Bf)
                          S.dma("act", C("dma_start", out=nss_d[s, :, g * 512:(g + 1) * 512], in_=fb_), reads=Bf, final=True)
                          if s == 15:
                              ystate_evac(bS, g, yv, Byv)

              def stage2(c):
                  yv, Byv = ybv[c % 2], B_ybv[c % 2]
                  def pre(g):
                      ri = g % 2
                      if g >= 2:
                          r_build(g)
                      bs = ps_pair()
                      for q in range(2):
                          S.op("pe", C("matmul", psf(bs + q), lhsT=mLm, rhs=Rg[ri][:, q * 512:(q + 1) * 512], start=True, stop=True),
                               reads=[B_cbf, B_Rg[ri]], writes=[B_ps[bs + q]])
                      dv = dec[:, ri * 1024:(ri + 1) * 1024]
                      S.op("act", C("activation", out=dv, in_=ps[:, bs * 512:(bs + 2) * 512], func=AF.Exp),
                           reads=[B_ps[bs], B_ps[bs + 1]], writes=[B_dec[ri]])

                  def post(g):
                      ri = g % 2
                      dv = dec[:, ri * 1024:(ri + 1) * 1024]
                      av = att[:, ri * 1024:(ri + 1) * 1024].rearrange("p (h i) -> p h i", h=8)
                      S.op("dve", C("tensor_tensor", out=av, in0=dv.rearrange("p (h i) -> p h i", h=8), in1=bc(cbTm[:, g, :].unsqueeze(1), [128, 8, 128]), op=ALU.mult),
                           reads=[B_dec[ri], B_cbTm], writes=[B_att[ri]])
                      by = ps_one()
                      S.op("pe", C("matmul", psf(by), lhsT=identb, rhs=xD[:, g * 512:(g + 1) * 512], start=True, stop=False),
                           reads=[B_cbf, B_xD], writes=[B_ps[by]])
                      for hh in range(8):
                          S.op("pe", C("matmul", psf(by, 64, hh * 64), lhsT=av[:, hh, :], rhs=xdt[:, (g * 8 + hh) * 64:(g * 8 + hh + 1) * 64], start=False, stop=(hh == 7)),
                               reads=[B_att[ri], B_xdt], writes=[B_ps[by]])
                      S.op("dve", C("tensor_tensor", out=yv[:, g, :], in0=yv[:, g, :], in1=psf(by), op=ALU.add),
                           reads=[Byv[g], B_ps[by]], writes=[Byv[g]])

                  pre(0)
                  yield
                  pre(1)
                  yield
                  post(0)
                  yield
                  pre(2)
                  yield
                  post(1)
                  yield
                  pre(3)
                  yield
                  post(2)
                  yield
                  post(3)

              def stage3(c):
                  cs = slice(c * 128, (c + 1) * 128)
                  yv, Byv = ybv[c % 2], B_ybv[c % 2]
                  if DEBUG and sti == 0 and c == 0:
                      S.dma("sp", C("dma_start", out=dbg["d_yb"][:, :].rearrange("p (g q) -> p g q", g=4), in_=yv), reads=Byv, final=True)
                  S.op("dve", C("tensor_tensor", out=yv, in0=yv, in1=sz[:, c, :].rearrange("p (g q) -> p g q", g=4), op=ALU.mult), reads=Byv + [B_sz[c]], writes=Byv)
                  yield
                  S.op("act", C("activation", out=tmpA[:].rearrange("p (g q) -> p g q", g=2), in_=yv[:, 0:2, :], func=AF.Square, accum_out=stat[:, 4:5]), reads=Byv, writes=[B_tmpA, B_stat])
                  S.op("act", C("activation", out=tmpA[:].rearrange("p (g q) -> p g q", g=2), in_=yv[:, 2:4, :], func=AF.Square, accum_out=stat[:, 5:6]), reads=Byv, writes=[B_tmpA, B_stat])
                  yield
                  S.op("dve", C("tensor_tensor", out=stat[:, 6:7], in0=stat[:, 4:5], in1=stat[:, 5:6], op=ALU.add), reads=[B_stat], writes=[B_stat])
                  S.op("dve", C("tensor_scalar", out=stat[:, 7:8], in0=stat[:, 6:7], scalar1=1.0 / 2048, scalar2=EPS, op0=ALU.mult, op1=ALU.add), reads=[B_stat], writes=[B_stat])
                  S.op("act", C("activation", out=stat[:, 7:8], in_=stat[:, 7:8], func=AF.Ln), reads=[B_stat], writes=[B_stat])
                  S.op("act", C("activation", out=stat[:, 7:8], in_=stat[:, 7:8], func=AF.Exp, scale=-0.5), reads=[B_stat], writes=[B_stat])
                  yield
                  S.op("act", C("activation", out=ynb[:].rearrange("p (g q) -> p g q", g=4), in_=yv, func=AF.Copy, scale=stat[:, 7:8]), reads=Byv + [B_stat], writes=B_ynb)
                  yield
                  by2 = ps_pair()
                  for kk in range(16):
                      S.op("pe", C("transpose", out=psb(by2, 128, kk * 128), in_=ynb[:, kk * 128:(kk + 1) * 128], identity=identb),
                           reads=B_ynb + [B_cbf], writes=[B_ps[by2 + kk // 8]])
                  yield
                  for hb in range(2):
                      S.op("dve", C("tensor_tensor", out=ynT[:, hb * 8:(hb + 1) * 8, cs], in0=psb(by2 + hb, 1024).rearrange("p (k t) -> p k t", k=8),
                                     in1=bc(vfm[:, VF_SNG + hb * 8:VF_SNG + (hb + 1) * 8].unsqueeze(2), [128, 8, 128]), op=ALU.mult),
                           reads=[B_ps[by2 + hb], B_vfm], writes=[B_ynT[c][hb]])

              run_il(stage1(0))
              run_il(stage2(0))
              for c in range(1, NCH):
                  run_il(stage1(c), stage3(c - 1))
                  run_il(stage2(c))
              run_il(stage3(NCH - 1))

              if DEBUG and sti == 0:
                  S.dma("pool", C("dma_start", out=dbg["d_ynT"][:, :], in_=ynT[:].rearrange("p k t -> p (k t)")), reads=[b for r in B_ynT for b in r], final=True)
                  S.dma("pool", C("dma_start", out=dbg["d_pzT"][:, :], in_=siluz[:].rearrange("p k t -> p (k t)")), reads=B_siluz, final=True)
                  S.dma("pool", C("dma_start", out=dbg["d_xc"][:, :], in_=xcT[:].rearrange("p k t -> p (k t)")), reads=B_xcT, final=True)
              ckpt('%d:D' % sti)
              th12 = sz[:].rearrange("p c n -> p (c n)")[:, 0:16 * T].rearrange("p (j t) -> p j t", j=16)
              for gg in range(4):
                  wv, wb, wh = w_next(8, 512)
                  for jb in range(4):
                      j = gg * 4 + jb
                      b = ps_one()
                      inproj_block(wv, wb, jb, b)
                      S.op("act", C("activation", out=th12[:, j, :], in_=psf(b, T), func=AF.Tanh, scale=0.5), reads=[B_ps[b]], writes=B_sz)
              def phaseEF():
                  for c in range(min(2, NCH)):
                      r0 = tok0 + c * 128
                      S.dma("sp", C("dma_start", out=outb[c % 2], in_=x_d[r0:r0 + 128, :]), writes=B_outb[c % 2])
                  wpv, wpb, wph = w_next(4, 1024, hold=True, lag=False)
                  for fb in range(8):
                      if fb % 2 == 0:
                          wsv, wsb, wsh = w_next(16, 256, lag=False)
                      bP = ps_one()
                      for kc in range(4):
                          S.op("pe", C("matmul", psf(bP, T), lhsT=wpv[:, kc, fb * 128:(fb + 1) * 128], rhs=pzT[:, kc, 0:T], start=(kc == 0), stop=(kc == 3)),
                               reads=[wpb, B_pzT[kc]], writes=[B_ps[bP]])
                      bS2 = ps_one()
                      for kc in range(16):
                          S.op("pe", C("matmul", psf(bS2, T), lhsT=wsv[:, kc, (fb % 2) * 128:(fb % 2 + 1) * 128], rhs=ynT[:, kc, 0:T], start=(kc == 0), stop=(kc == 15)),
                               reads=[wsb] + [B_ynT[cc][pp] for cc in range(NCH) for pp in range(2)], writes=[B_ps[bS2]])
                      S.op("dve", C("scalar_tensor_tensor", out=tmpg[0][:, 0:T], in0=th12[:, fb, :], scalar=1.0, in1=psf(bP, T), op0=ALU.add, op1=ALU.mult),
                           reads=B_sz + [B_ps[bP]], writes=[B_tmpg[0]])
                      S.op("dve", C("scalar_tensor_tensor", out=tmpg[1][:, 0:T], in0=th12[:, 8 + fb, :], scalar=1.0, in1=psf(bS2, T), op0=ALU.add, op1=ALU.mult),
                           reads=B_sz + [B_ps[bS2]], writes=[B_tmpg[1]])
                      S.op("pool", C("tensor_tensor", out=xcT[:, fb, 0:T], in0=tmpg[0][:, 0:T], in1=tmpg[1][:, 0:T], op=ALU.add),
                           reads=[B_tmpg[0], B_tmpg[1]], writes=[B_xcT[fb]])
                      yield
                  if DEBUG and sti == 0:
                      S.dma("pool", C("dma_start", out=dbg["d_mT"][:, :], in_=xcT[:, 0:8, :].rearrange("p k t -> p (k t)")), reads=B_xcT[0:8], final=True)
                      S.dma("pool", C("dma_start", out=dbg["d_th"][:, :], in_=sz[:].rearrange("p c n -> p (c n)")), reads=B_sz, final=True)
                  w_release(wph)
                  wov = []
                  for hf in range(2):
                      wov.append(w_next(8, 512, hold=True, lag=False))
                  fst = {}

                  def F1(c):
                      oi = c % 2
                      r0 = tok0 + c * 128
                      if c >= 2:
                          S.dma("sp", C("dma_start", out=outb[oi], in_=x_d[r0:r0 + 128, :]), writes=B_outb[oi])
                      for hf in range(2):
                          b = ps_one()
                          for k in range(8):
                              S.op("pe", C("matmul", psf(b), lhsT=xcT[:, k, c * 128:(c + 1) * 128], rhs=wov[hf][0][:, k, :], start=(k == 0), stop=(k == 7)),
                                   reads=[wov[hf][1], B_xcT[k]], writes=[B_ps[b]])
                          S.op("dve", C("tensor_tensor", out=tmpg[hf][:], in0=psf(b), in1=gate_t[:, hf * 512:(hf + 1) * 512], op=ALU.mult),
                               reads=[B_ps[b], Bgate], writes=[B_tmpg[hf]])
                          S.op("pool", C("tensor_tensor", out=outb[oi][:, hf * 512:(hf + 1) * 512], in0=outb[oi][:, hf * 512:(hf + 1) * 512], in1=tmpg[hf][:], op=ALU.add),
                               reads=B_outb[oi] + [B_tmpg[hf]], writes=B_outb[oi])

                  def F2(c):
                      oi = c % 2
                      S.op("act", C("activation", out=dec[:, 0:1024], in_=outb[oi], func=AF.Square, accum_out=stat[:, 8 + 2 * oi:9 + 2 * oi]), reads=B_outb[oi], writes=[B_dec[0], B_statF[oi]])
                      rstd_from_ss(8 + 2 * oi, 1024, B_statF[oi])

                  def F3(c):
                      oi = c % 2
                      S.op("act", C("activation", out=outb[oi], in_=outb[oi], func=AF.Copy, scale=stat[:, 9 + 2 * oi:10 + 2 * oi]), reads=B_outb[oi] + [B_statF[oi]], writes=B_outb[oi])
                      S.op("dve", C("tensor_tensor", out=outb[oi], in0=outb[oi], in1=vrow[:, VR_FING:VR_FING + 1024], op=ALU.mult), reads=B_outb[oi] + [B_vrow], writes=B_outb[oi])
                      r0 = tok0 + c * 128
                      S.dma("sp", C("dma_start", out=y_d[r0:r0 + 128, :], in_=outb[oi]), reads=B_outb[oi], final=True)

                  for c0 in range(0, NCH, 2):
                      cc = [c for c in (c0, c0 + 1) if c < NCH]
                      for fn in (F1, F2, F3):
                          for c in cc:
                              fn(c)
                          yield
                  for (_, _, whh) in wov:
                      w_release(whh)

              ckpt('%d:E' % sti)
              if sti + 1 < len(ST_LIST):
                  run_il(phaseEF(), phaseA(*ST_LIST[sti + 1]))
              else:
                  run_il(phaseEF())

        try:
            main_loop()
            assert wstate["used"] == len(wq), (wstate, len(wq))
        except _Stop:
            pass

        with nc.Block() as block:
            @block.sync
            def _(e):
                S.emit("sp", e)

            @block.tensor
            def _(e):
                S.emit("pe", e)

            @block.scalar
            def _(e):
                S.emit("act", e)

            @block.vector
            def _(e):
                S.emit("dve", e)

            @block.gpsimd
            def _(e):
                S.emit("pool", e)
    return nc


def _consts():
    k = np.arange(128)
    c = np.zeros((128, NCST), np.float32)
    c[:, C_ID:C_ID + 128] = np.eye(128)
    c[:, C_SEQ:C_SEQ + 16] = (k[:, None] % 16 == np.arange(16)[None, :])
    pos = np.arange(16)
    for g in range(4):
        w = 2 << g
        c[:, C_INVC + g * 16:C_INVC + (g + 1) * 16] = (1.0 / np.minimum(pos + 1, w))[None, :]
    U = (k[:, None] <= k[None, :])
    Lm = (k[:, None] > k[None, :])
    same = (k[:, None] % 16 == k[None, :] % 16)
    m = np.concatenate([np.eye(128), U, Lm, np.ones((128, 128)), U & same, Lm & same, same, np.zeros((128, 128))], axis=1).astype(np.float32)
    return c, m


def make_in_maps(inp, cores):
    f = lambda a: np.ascontiguousarray(np.asarray(a, dtype=np.float32))
    w_ada, w_in = f(inp["w_ada"][0]), f(inp["w_in"][0])
    wpo, wssm, wo = f(inp["w_pool_out"][0]), f(inp["w_ssm_out"][0]), f(inp["w_o"][0])
    poolw = f(np.asarray(inp["pool_w"][0]).reshape(512, 128))
    b_ada = np.asarray(inp["b_ada"][0], np.float32)
    fm = lambda v: np.asarray(v, np.float32).reshape(-1, 128).T
    vfm = np.zeros((128, NVF), np.float32)
    vfm[:, VF_BSHIFT:VF_BSHIFT + 8] = fm(b_ada[0:1024])
    vfm[:, VF_BSCALE:VF_BSCALE + 8] = fm(b_ada[1024:2048])
    vfm[:, VF_NORMG:VF_NORMG + 8] = fm(inp["norm_g"][0])
    cw = np.asarray(inp["conv_w"][0], np.float32)
    vfm[:, VF_CONVW:VF_CONVW + 96] = cw.reshape(4, 24, 128).transpose(2, 1, 0).reshape(128, 96)
    vfm[:, VF_CONVB:VF_CONVB + 24] = fm(inp["conv_b"][0])
    vfm[:, VF_PSCALE:VF_PSCALE + 4] = fm(inp["pool_scale"][0])
    vfm[:, VF_SNG:VF_SNG + 16] = fm(inp["ssm_norm_g"][0])
    vrow = np.zeros((1, NVR), np.float32)
    vrow[0, VR_FING:VR_FING + 1024] = np.asarray(inp["final_g"], np.float32)
    vrow[0, VR_DTB:VR_DTB + 32] = np.asarray(inp["dt_bias"][0], np.float32)
    vrow[0, VR_ALOG:VR_ALOG + 32] = np.asarray(inp["a_log"][0], np.float32)
    vrow[0, VR_DSKIP:VR_DSKIP + 32] = np.asarray(inp["d_skip"][0], np.float32)
    cst, msk = _consts()
    xp, xs = np.asarray(inp["x_prompt"], np.float32), np.asarray(inp["x_sample"], np.float32)
    cp, csm = np.asarray(inp["c_prompt"], np.float32), np.asarray(inp["c_sample"], np.float32)
    sp, sc, ss = np.asarray(inp["state_pool"][0]), np.asarray(inp["state_conv"][0]), np.asarray(inp["state_ssm"][0])
    maps = []
    for c in cores:
        sl = slice(16 * c, 16 * c + 16)
        x = np.concatenate([xp[c], xs[sl].transpose(1, 0, 2).reshape(128, D)], axis=0)
        cT = np.concatenate([cp[c:c + 1], csm[sl]], axis=0).T
        spT = sp[sl].reshape(16, 15, 4, 128).transpose(3, 2, 1, 0).reshape(128, 960)
        scT = sc[sl].reshape(16, 3, 24, 128).transpose(3, 2, 1, 0).reshape(128, 1152)
        ssT = ss[sl].reshape(16, 2048, 128).transpose(0, 2, 1)
        maps.append({
            "x": f(x), "cT": f(cT), "spT": f(spT), "scT": f(scT), "ssT": f(ssT),
            "w_ada": w_ada, "w_in": w_in, "w_pool_out": wpo, "w_ssm_out": wssm, "w_o": wo,
            "pool_w": poolw, "vecs_fm": vfm, "vecs_row": vrow, "consts": cst, "masks": msk, "b_gate": f(b_ada[None, 2048:3072]),
        })
    return maps


def assemble(results, cores):
    n = len(cores)
    y_prompt = np.zeros((n, SEQ, D), np.float32)
    y_sample = np.zeros((16 * n, 8, D), np.float32)
    npp = np.zeros((1, n, 15, 512), np.float32)
    ncp = np.zeros((1, n, 3, 3072), np.float32)
    nsp = np.zeros((1, n, 32, 64, 128), np.float32)
    nps = np.zeros((1, 16 * n, 15, 512), np.float32)
    ncs = np.zeros((1, 16 * n, 3, 3072), np.float32)
    nss = np.zeros((1, 16 * n, 32, 64, 128), np.float32)
    for i, r in enumerate(results):
        sl = slice(16 * i, 16 * i + 16)
        y = np.asarray(r["y"])
        y_prompt[i] = y[0:SEQ]
        y_sample[sl] = y[SEQ:].reshape(8, 16, D).transpose(1, 0, 2)
        npp[0, i] = np.asarray(r["npp"]).reshape(128, 4, 15).transpose(2, 1, 0).reshape(15, 512)
        ncp[0, i] = np.asarray(r["ncp"]).reshape(128, 24, 3).transpose(2, 1, 0).reshape(3, 3072)
        nsp[0, i] = np.asarray(r["nsp"]).T.reshape(32, 64, 128)
        nps[0, sl] = np.asarray(r["nps"]).reshape(128, 4, 15, 16).transpose(3, 2, 1, 0).reshape(16, 15, 512)
        ncs[0, sl] = np.asarray(r["ncs"]).reshape(128, 24, 3, 16).transpose(3, 2, 1, 0).reshape(16, 3, 3072)
        nss[0, sl] = np.asarray(r["nss"]).transpose(0, 2, 1).reshape(16, 32, 64, 128)
    return (y_prompt, y_sample, npp, ncp, nsp, nps, ncs, nss)


def kernel(**inputs):
    cores = list(range(NCORES))
    nc = build_nc()
    in_maps = make_in_maps(inputs, cores)
    res = run_bass_kernel_spmd(nc, in_maps, core_ids=cores)
    return assemble(res.results, cores)
```
